# Optimizing a Trainium2 kernel written in Bass

```python
import math
import jax
import jax.numpy as jnp
from jax import lax
import numpy as np

D_MODEL = 1024
BATCH = 32
SEQ = 256
DEPTH = 4
DEC_BATCH = 2
DEC_SEQ = 1024
PAST_LEN = 256

GRID_W = 64
D_MIX = D_MODEL
HEAD_DIM = 64
ATT_W = D_MIX // 2
N_HEADS = ATT_W // HEAD_DIM
N_KV = 2
GQA_G = N_HEADS // N_KV
KV_W = N_KV * HEAD_DIM
Q_BLOCK = 128
ROPE_BASE = 10000.0
LRU_W = D_MIX // 4
LRU_BLOCKS = 4
LRU_BS = LRU_W // LRU_BLOCKS
LRU_C = 8.0
LRU_CONV = 4
HY_W = D_MIX // 4
HY_CONV = 3
HY_BANDS = 16
HY_FEAT = 2 * HY_BANDS + 1
HY_FILT_HID = 64
HY_DECAY_FAST = 0.3
HY_DECAY_SLOW = 1.5
HY_DECAY_TARGET = 1e-2
D_FF = -(-8 * D_MODEL // (3 * 256)) * 256
EPS = 1e-6
OFF_K = ATT_W
OFF_V = OFF_K + KV_W
OFF_LX = OFF_V + KV_W
OFF_LG = OFF_LX + LRU_W
OFF_HY = OFF_LG + LRU_W
D_IN = OFF_HY + 3 * HY_W

kernel_name = 'hybrid_dit_attn_rglru_hyena_step'


def rmsnorm(x, w):
    xf = x.astype(jnp.float32)
    y = xf * lax.rsqrt(jnp.mean(xf * xf, axis=-1, keepdims=True) + EPS)
    return (y * w.astype(jnp.float32)).astype(x.dtype)


def adaln_mod(cond, w_mod, b_mod):
    m = jax.nn.silu(cond) @ w_mod + b_mod
    return tuple(t[:, None, :] for t in jnp.split(m, 6, axis=-1))


def dwconv_centred(x, w, pad_left):
    K = w.shape[0]
    L = x.shape[1]
    xp = jnp.pad(x, ((0, 0), (pad_left, K - 1 - pad_left), (0, 0)))
    return sum(w[k] * xp[:, k:k + L] for k in range(K))


def axial_rope(x):
    L = x.shape[1]
    rows = L // GRID_W
    row = jnp.repeat(jnp.arange(rows, dtype=jnp.float32), GRID_W)
    col = jnp.tile(jnp.arange(GRID_W, dtype=jnp.float32), rows)
    nq = HEAD_DIM // 4
    freqs = ROPE_BASE ** (-jnp.arange(nq, dtype=jnp.float32) / nq)
    xf = x.astype(jnp.float32)

    def rot(xh, pos):
        ang = pos[:, None] * freqs[None, :]
        cos = jnp.cos(ang)[None, :, None, :]
        sin = jnp.sin(ang)[None, :, None, :]
        x1, x2 = xh[..., :nq], xh[..., nq:]
        return jnp.concatenate([x1 * cos - x2 * sin, x2 * cos + x1 * sin], axis=-1)

    half = HEAD_DIM // 2
    out = jnp.concatenate([rot(xf[..., :half], row), rot(xf[..., half:], col)], axis=-1)
    return out.astype(x.dtype)


def block_attention(q, k, v):
    B, Lq = q.shape[:2]
    nb = Lq // Q_BLOCK
    qb = q.reshape(B, nb, Q_BLOCK, N_KV, GQA_G, HEAD_DIM).transpose(1, 0, 2, 3, 4, 5)
    scale = HEAD_DIM ** -0.5

    def one_block(qblk):
        s = jnp.einsum('bqkgd,bskd->bkgqs', qblk, k).astype(jnp.float32) * scale
        p = jax.nn.softmax(s, axis=-1).astype(v.dtype)
        return jnp.einsum('bkgqs,bskd->bqkgd', p, v)

    o = lax.map(one_block, qb)
    return o.transpose(1, 0, 2, 3, 4, 5).reshape(B, Lq, ATT_W)


def rglru_coeffs(xc, gate_w, gate_b, lam):
    B, L = xc.shape[:2]
    xf = xc.astype(jnp.float32)
    xb = xf.reshape(B, L, LRU_BLOCKS, LRU_BS)
    g = jnp.einsum('blni,dgnij->dgblnj', xb, gate_w.astype(jnp.float32)).reshape(2, 2, B, L, LRU_W)
    g = g + gate_b.astype(jnp.float32)[:, :, None, None, :]
    r = jax.nn.sigmoid(g[:, 0])
    i = jax.nn.sigmoid(g[:, 1])
    log_a = -LRU_C * r * jax.nn.softplus(-lam.astype(jnp.float32))[:, None, None, :]
    a = jnp.exp(log_a)
    b = jnp.sqrt(-jnp.expm1(2.0 * log_a)) * (i * xf[None])
    return a, b


def _affine_combine(e1, e2):
    a1, b1 = e1
    a2, b2 = e2
    return a1 * a2, a2 * b1 + b2


def linear_recurrence(a, b, h0):
    A, Bc = lax.associative_scan(_affine_combine, (a, b), axis=1)
    return A * h0[:, None, :] + Bc


def bidir_scan(a, b, h0f, h0b):
    hf = linear_recurrence(a[0], b[0], h0f)
    hb = jnp.flip(linear_recurrence(jnp.flip(a[1], 1), jnp.flip(b[1], 1), h0b), 1)
    return hf, hb


def hyena_filter_spectra(L, w1, b1, w2, b2, w3, b3):
    f32 = jnp.float32
    t = jnp.arange(L, dtype=f32)
    tn = t / L
    bands = jnp.linspace(1e-4, HY_BANDS - 1, HY_BANDS, dtype=f32)
    ang = (2.0 * math.pi / L) * t[:, None] * bands[None, :]
    z = jnp.concatenate([tn[:, None], jnp.cos(ang), -jnp.sin(ang)], axis=-1)
    h = jnp.sin(z @ w1.astype(f32) + b1.astype(f32))
    h = jnp.sin(h @ w2.astype(f32) + b2.astype(f32))
    hf = (h @ w3.astype(f32) + b3.astype(f32)).reshape(L, 2, 2, HY_W)
    deltas = jnp.linspace(math.log(HY_DECAY_TARGET) / HY_DECAY_SLOW,
                          math.log(HY_DECAY_TARGET) / HY_DECAY_FAST, HY_W, dtype=f32)
    hf = hf * jnp.exp(-tn[:, None] * jnp.abs(deltas)[None, :])[:, None, None, :]
    fwd, bwd = hf[:, :, 0], hf[:, :, 1]
    taps = jnp.concatenate([fwd, jnp.zeros_like(fwd[:1]), jnp.flip(bwd[1:], axis=0)], axis=0)
    return jnp.fft.rfft(taps, axis=0)


def long_conv(u, kf, skip):
    L = u.shape[1]
    uf = u.astype(jnp.float32)
    y = jnp.fft.irfft(jnp.fft.rfft(uf, n=2 * L, axis=1) * kf[None], n=2 * L, axis=1)[:, :L]
    return (y + uf * skip.astype(jnp.float32)).astype(u.dtype)


def hyena_mixer(u, lp):
    L = u.shape[1]
    uc = dwconv_centred(u, lp['hy_conv_w'], HY_CONV // 2)
    v, x1, x2 = jnp.split(uc, 3, axis=-1)
    kf = hyena_filter_spectra(L, lp['hy_filt_w1'], lp['hy_filt_b1'], lp['hy_filt_w2'],
                              lp['hy_filt_b2'], lp['hy_filt_w3'], lp['hy_filt_b3'])
    z = x1 * long_conv(v, kf[:, 0], lp['hy_skip'][0])
    return x2 * long_conv(z, kf[:, 1], lp['hy_skip'][1])


def trunk_layer(x, mod, lp, ctx):
    sh1, sc1, g1, sh2, sc2, g2 = mod
    B, L, _ = x.shape
    h = rmsnorm(x, lp['norm1_w']) * (1 + sc1) + sh1
    p = h @ lp['w_in']
    q = rmsnorm(p[..., :OFF_K].reshape(B, L, N_HEADS, HEAD_DIM), lp['q_norm_w'])
    k = rmsnorm(p[..., OFF_K:OFF_V].reshape(B, L, N_KV, HEAD_DIM), lp['k_norm_w'])
    v = p[..., OFF_V:OFF_LX].reshape(B, L, N_KV, HEAD_DIM)
    if ctx is None:
        att = block_attention(q, k, v)
        h0f = jnp.zeros((B, LRU_W), jnp.float32)
        h0b = jnp.zeros((B, LRU_W), jnp.float32)
    else:
        k_ctx, v_ctx, s_ctx = ctx
        q = axial_rope(q)
        k = axial_rope(k)
        k_all = jnp.concatenate([k_ctx.astype(k.dtype), k], axis=1)
        v_all = jnp.concatenate([v_ctx.astype(v.dtype), v], axis=1)
        att = block_attention(q, k_all, v_all)
        h0f = s_ctx[:, 0].astype(jnp.float32)
        h0b = s_ctx[:, 1].astype(jnp.float32)
    xc = dwconv_centred(p[..., OFF_LX:OFF_LG], lp['lru_conv_w'], LRU_CONV // 2) + lp['lru_conv_b']
    a, b = rglru_coeffs(xc, lp['lru_gate_w'], lp['lru_gate_b'], lp['lru_lambda'])
    hf, hb = bidir_scan(a, b, h0f, h0b)
    lru = jax.nn.gelu(p[..., OFF_LG:OFF_HY]) * (hf + hb).astype(x.dtype)
    hy = hyena_mixer(p[..., OFF_HY:], lp)
    mix = jnp.concatenate([att, lru, hy], axis=-1)
    x = x + g1 * (mix @ lp['w_out'])
    h2 = rmsnorm(x, lp['norm2_w']) * (1 + sc2) + sh2
    x = x + g2 * ((jax.nn.silu(h2 @ lp['ffn_w1']) * (h2 @ lp['ffn_w3'])) @ lp['ffn_w2'])
    if ctx is None:
        state = jnp.stack([hf[:, -1], hb[:, 0]], axis=1).astype(x.dtype)
        return x, (k, v, state)
    return x, None


def setup_inputs(seed: int = 0) -> dict:
    key = jax.random.key(seed)
    ks = jax.random.split(key, 32)

    def nrm(k, shape, s):
        return s * jax.random.normal(k, shape, jnp.float32)

    a0 = jax.random.uniform(ks[18], (DEPTH, 2, LRU_W), jnp.float32, 0.9, 0.999) ** (1.0 / LRU_C)
    return {
        'x_prompt': nrm(ks[0], (BATCH, SEQ, D_MODEL), 1.0),
        'x_sample': nrm(ks[1], (DEC_BATCH, DEC_SEQ, D_MODEL), 1.0),
        'cache_k': nrm(ks[2], (DEC_BATCH, DEPTH, PAST_LEN, N_KV, HEAD_DIM), 1.0),
        'cache_v': nrm(ks[3], (DEC_BATCH, DEPTH, PAST_LEN, N_KV, HEAD_DIM), 1.0),
        'state_lru': nrm(ks[4], (DEC_BATCH, DEPTH, 2, LRU_W), 0.5),
        'c': nrm(ks[5], (DEC_BATCH, D_MODEL), 1.0),
        'c_ctx': nrm(ks[6], (D_MODEL,), 1.0),
        'w_mod': nrm(ks[7], (DEPTH, D_MODEL, 6 * D_MODEL), 0.5 * D_MODEL ** -0.5),
        'b_mod': nrm(ks[8], (DEPTH, 6 * D_MODEL), 0.01),
        'norm1_w': 1.0 + nrm(ks[9], (DEPTH, D_MODEL), 0.05),
        'norm2_w': 1.0 + nrm(ks[10], (DEPTH, D_MODEL), 0.05),
        'w_in': nrm(ks[11], (DEPTH, D_MODEL, D_IN), D_MODEL ** -0.5),
        'q_norm_w': 1.0 + nrm(ks[12], (DEPTH, HEAD_DIM), 0.05),
        'k_norm_w': 1.0 + nrm(ks[13], (DEPTH, HEAD_DIM), 0.05),
        'lru_conv_w': nrm(ks[14], (DEPTH, LRU_CONV, LRU_W), LRU_CONV ** -0.5),
        'lru_conv_b': nrm(ks[15], (DEPTH, LRU_W), 0.01),
        'lru_gate_w': nrm(ks[16], (DEPTH, 2, 2, LRU_BLOCKS, LRU_BS, LRU_BS), LRU_BS ** -0.5),
        'lru_gate_b': nrm(ks[17], (DEPTH, 2, 2, LRU_W), 0.1),
        'lru_lambda': jnp.log(a0) - jnp.log1p(-a0),
        'hy_conv_w': nrm(ks[19], (DEPTH, HY_CONV, 3 * HY_W), HY_CONV ** -0.5),
        'hy_filt_w1': nrm(ks[20], (DEPTH, HY_FEAT, HY_FILT_HID), HY_FEAT ** -0.5),
        'hy_filt_b1': nrm(ks[21], (DEPTH, HY_FILT_HID), 0.1),
        'hy_filt_w2': nrm(ks[22], (DEPTH, HY_FILT_HID, HY_FILT_HID), HY_FILT_HID ** -0.5),
        'hy_filt_b2': nrm(ks[23], (DEPTH, HY_FILT_HID), 0.1),
        'hy_filt_w3': nrm(ks[24], (DEPTH, HY_FILT_HID, 4 * HY_W), 0.05 * HY_FILT_HID ** -0.5),
        'hy_filt_b3': nrm(ks[25], (DEPTH, 4 * HY_W), 0.01),
        'hy_skip': nrm(ks[26], (DEPTH, 2, HY_W), 1.0),
        'w_out': nrm(ks[27], (DEPTH, D_MIX, D_MODEL), D_MIX ** -0.5),
        'ffn_w1': nrm(ks[28], (DEPTH, D_MODEL, D_FF), D_MODEL ** -0.5),
        'ffn_w3': nrm(ks[29], (DEPTH, D_MODEL, D_FF), D_MODEL ** -0.5),
        'ffn_w2': nrm(ks[30], (DEPTH, D_FF, D_MODEL), D_FF ** -0.5),
    }


def reference(x_prompt, x_sample, cache_k, cache_v, state_lru, c, c_ctx, w_mod, b_mod, norm1_w,
              norm2_w, w_in, q_norm_w, k_norm_w, lru_conv_w, lru_conv_b, lru_gate_w, lru_gate_b,
              lru_lambda, hy_conv_w, hy_filt_w1, hy_filt_b1, hy_filt_w2, hy_filt_b2, hy_filt_w3,
              hy_filt_b3, hy_skip, w_out, ffn_w1, ffn_w3, ffn_w2):
    y_p = x_prompt
    y_s = x_sample
    ks_new, vs_new, ss_new = [], [], []
    for l in range(DEPTH):
        lp = {
            'norm1_w': norm1_w[l], 'norm2_w': norm2_w[l], 'w_in': w_in[l],
            'q_norm_w': q_norm_w[l], 'k_norm_w': k_norm_w[l],
            'lru_conv_w': lru_conv_w[l], 'lru_conv_b': lru_conv_b[l],
            'lru_gate_w': lru_gate_w[l], 'lru_gate_b': lru_gate_b[l], 'lru_lambda': lru_lambda[l],
            'hy_conv_w': hy_conv_w[l], 'hy_filt_w1': hy_filt_w1[l], 'hy_filt_b1': hy_filt_b1[l],
            'hy_filt_w2': hy_filt_w2[l], 'hy_filt_b2': hy_filt_b2[l], 'hy_filt_w3': hy_filt_w3[l],
            'hy_filt_b3': hy_filt_b3[l], 'hy_skip': hy_skip[l], 'w_out': w_out[l],
            'ffn_w1': ffn_w1[l], 'ffn_w3': ffn_w3[l], 'ffn_w2': ffn_w2[l],
        }
        mod_ctx = adaln_mod(c_ctx[None, :], w_mod[l], b_mod[l])
        y_p, (k_l, v_l, s_l) = trunk_layer(y_p, mod_ctx, lp, None)
        ks_new.append(k_l)
        vs_new.append(v_l)
        ss_new.append(s_l)
        mod_lat = adaln_mod(c, w_mod[l], b_mod[l])
        y_s, _ = trunk_layer(y_s, mod_lat, lp, (cache_k[:, l], cache_v[:, l], state_lru[:, l]))
    new_k = jnp.stack(ks_new, axis=1)
    new_v = jnp.stack(vs_new, axis=1)
    new_state_lru = jnp.stack(ss_new, axis=1)
    return (y_p, y_s, new_k, new_v, new_state_lru)
```

```python
import math
from contextlib import ExitStack
import numpy as np
import concourse.bass as bass
import concourse.mybir as mybir
from concourse.bass_utils import run_bass_kernel_spmd

F32 = mybir.dt.float32
BF16 = mybir.dt.bfloat16
AF = mybir.ActivationFunctionType
ALU = mybir.AluOpType

D = 1024; NL = 4; T = 1024; HD = 64
OFF_K = 512; OFF_V = 640; OFF_LX = 768; OFF_LG = 1024; OFF_HY = 1280
DFF = 2816; NF = 22
EPS = 1e-6
PI = math.pi

CFG = {"layers": NL, "prompt": True, "sample": True, "dbg": []}

VCOLS = {}
def _mk_vcols():
    o = 0
    def add(n, c):
        nonlocal o
        VCOLS[n] = o; o += c
    for l in range(NL):
        add(("n1", l), 8); add(("n2", l), 8); add(("qn", l), 1); add(("kn", l), 1)
        add(("lcw", l), 8)
        add(("lcb", l), 2)
        add(("lgb", l), 8)
        add(("lam", l), 4)
        add(("hcw", l), 18)
        add(("hsk", l), 4)
        add(("fb1", l), 1); add(("fb2", l), 1)
    for l in range(NL):
        add(("bmod", l), 48)
    add("cw256", 2); add("cw1024", 8); add("altc", 1)
    return o
NV = _mk_vcols()
PV_COND = 0; PV_H0 = 16; NPV = 32


def _fm(v, nchunk):
    return np.ascontiguousarray(np.asarray(v, np.float32).reshape(nchunk, 128).T)


def _host_consts():
    f32 = np.float32
    c = {}
    c["ident"] = np.eye(128, dtype=f32)
    R = np.zeros((128, 128), f32)
    for m in range(128):
        if m % 32 < 16: R[m + 16, m] = -1.0
        else: R[m - 16, m] = 1.0
    c["rrot"] = R
    t = np.arange(T)
    row = (t // 64).astype(f32); col = (t % 64).astype(f32)
    freqs = (f32(10000.0) ** (-(np.arange(16, dtype=f32)) / f32(16))).astype(f32)
    rc = np.zeros((128, T), f32); rs = np.zeros((128, T), f32)
    for p in range(128):
        d = p % 64
        pos = row if d < 32 else col
        ang = (pos * freqs[d % 16]).astype(f32)
        rc[p] = np.cos(ang); rs[p] = np.sin(ang)
    c["ropec"] = rc; c["ropes"] = rs
    for L in (256, 1024):
        tt = np.arange(L, dtype=f32); tn = (tt / f32(L)).astype(f32)
        bands = np.linspace(1e-4, 15, 16, dtype=f32)
        ang = (f32(2.0 * math.pi / L) * tt[:, None] * bands[None, :]).astype(f32)
        z = np.concatenate([tn[:, None], np.cos(ang), -np.sin(ang)], axis=-1).astype(f32)
        c["zT%d" % L] = np.ascontiguousarray(z.T)
        deltas = np.linspace(math.log(1e-2) / 1.5, math.log(1e-2) / 0.3, 256, dtype=f32)
        dec = np.exp(-tn[:, None] * np.abs(deltas)[None, :]).astype(f32)
        decb = dec.copy(); decb[0] = 0.0
        dd = np.stack([dec, decb], axis=1)
        c["dec%d" % L] = np.ascontiguousarray(dd.reshape(L // 128, 128, 2, 256).transpose(1, 0, 2, 3))
        k = np.arange(L, dtype=np.float64)
        th = math.pi / L * np.outer(k, k)
        C = np.cos(th).astype(f32); S = np.sin(th).astype(f32)
        c["dftc%d" % L] = np.ascontiguousarray(C.reshape(L // 128, 128, L).transpose(1, 0, 2))
        c["dfts%d" % L] = np.ascontiguousarray(S.reshape(L // 128, 128, L).transpose(1, 0, 2))
    c["altr"] = ((-1.0) ** np.arange(T)).astype(f32)[None, :]
    c["altcin"] = ((-1.0) ** np.arange(128)).astype(f32)[:, None]
    return c


def _host_shared(inp):
    f32 = np.float32
    g = lambda k: np.asarray(inp[k], f32)
    sh = _host_consts()
    vec = np.zeros((128, NV), f32)
    def put(key, arr):
        vec[:, VCOLS[key]:VCOLS[key] + arr.shape[1]] = arr
    for l in range(NL):
        put(("n1", l), _fm(g("norm1_w")[l], 8)); put(("n2", l), _fm(g("norm2_w")[l], 8))
        put(("qn", l), np.tile(g("q_norm_w")[l], 2)[:, None]); put(("kn", l), np.tile(g("k_norm_w")[l], 2)[:, None])
        put(("lcw", l), np.concatenate([_fm(g("lru_conv_w")[l, k], 2) for k in range(4)], axis=1))
        put(("lcb", l), _fm(g("lru_conv_b")[l], 2))
        put(("lgb", l), np.concatenate([_fm(g("lru_gate_b")[l, d, gt], 2) for d in range(2) for gt in range(2)], axis=1))
        put(("lam", l), np.concatenate([_fm(g("lru_lambda")[l, d], 2) for d in range(2)], axis=1))
        put(("hcw", l), np.concatenate([_fm(g("hy_conv_w")[l, k], 6) for k in range(3)], axis=1))
        put(("hsk", l), np.concatenate([_fm(g("hy_skip")[l, o], 2) for o in range(2)], axis=1))
        b1 = np.zeros((128, 1), f32); b1[:64, 0] = g("hy_filt_b1")[l]; put(("fb1", l), b1)
        b2 = np.zeros((128, 1), f32); b2[:64, 0] = g("hy_filt_b2")[l]; put(("fb2", l), b2)
    for L, key in ((256, "cw256"), (1024, "cw1024")):
        cw = np.full(L, 1.0 / L, f32); cw[0] = 0.5 / L
        put(key, _fm(cw, L // 128))
    put("altc", (((-1.0) ** np.arange(128)).astype(f32))[:, None])
    sh["vec"] = vec
    for l in range(NL):
        put(("bmod", l), _fm(g("b_mod")[l], 48))
    cols = []
    cols += list(range(OFF_K, OFF_K + 64)) * 2
    cols += list(range(OFF_K + 64, OFF_K + 128)) * 2
    cols += list(range(OFF_V, OFF_V + 128))
    cols += list(range(0, 512))
    cols += list(range(OFF_LX, OFF_LX + 128)) + list(range(OFF_LG, OFF_LG + 128))
    cols += list(range(OFF_LX + 128, OFF_LX + 256)) + list(range(OFF_LG + 128, OFF_LG + 256))
    cols += list(range(OFF_HY, OFF_HY + 768))
    cols = np.array(cols)
    def pieces(W, nk, nch):
        return np.ascontiguousarray(W.reshape(NL, nk, 128, nch, 128).transpose(0, 3, 2, 1, 4))
    sh["w_in_p"] = pieces(g("w_in")[:, :, cols], 8, 17)
    sh["w_out_p"] = pieces(g("w_out"), 8, 8)
    sh["w1_p"] = pieces(g("ffn_w1"), 8, NF)
    sh["w3_p"] = pieces(g("ffn_w3"), 8, NF)
    sh["w2_p"] = pieces(g("ffn_w2"), NF, 8)
    sh["w_mod_p"] = pieces(g("w_mod"), 8, 48)
    gw = np.zeros((NL, 128, 8, 128), f32)
    G = g("lru_gate_w")
    for l in range(NL):
        for d in range(2):
            for gt in range(2):
                for c in range(2):
                    idx = (d * 2 + gt) * 2 + c
                    for h in range(2):
                        gw[l, h * 64:(h + 1) * 64, idx, h * 64:(h + 1) * 64] = G[l, d, gt, 2 * c + h]
    sh["gw"] = gw
    sh["hf_w1"] = np.ascontiguousarray(g("hy_filt_w1"))
    sh["hf_w2"] = np.ascontiguousarray(g("hy_filt_w2"))
    sh["hf_w3"] = np.ascontiguousarray(np.concatenate([g("hy_filt_w3"), g("hy_filt_b3")[:, None, :]], axis=1))
    return sh


def _host_core(inp, c):
    f32 = np.float32
    m = {}
    xp = np.asarray(inp["x_prompt"], f32)[4 * c:4 * c + 4].reshape(T, 8, 128)
    m["xp"] = np.ascontiguousarray(xp.transpose(2, 1, 0))
    b = c % 2
    xs = np.asarray(inp["x_sample"], f32)[b].reshape(T, 8, 128)
    m["xs"] = np.ascontiguousarray(xs.transpose(2, 1, 0))
    ck = np.asarray(inp["cache_k"], f32)[b]
    ckT = np.zeros((NL, 2, 128, 256), f32)
    for g_ in range(2):
        kt = ck[:, :, g_, :].transpose(0, 2, 1)
        ckT[:, g_, :64] = kt; ckT[:, g_, 64:] = kt
    m["ckT"] = ckT
    cv = np.asarray(inp["cache_v"], f32)[b].reshape(NL, 2, 128, 128)
    m["cv"] = np.ascontiguousarray(cv.transpose(0, 2, 1, 3))
    pv = np.zeros((128, NPV), f32)
    cc = _fm(np.asarray(inp["c_ctx"], f32), 8); cb = _fm(np.asarray(inp["c"], f32)[b], 8)
    for k in range(8):
        pv[:, PV_COND + 2 * k] = cc[:, k]; pv[:, PV_COND + 2 * k + 1] = cb[:, k]
    st = np.asarray(inp["state_lru"], f32)[b]
    for l in range(NL):
        for d in range(2):
            pv[:, PV_H0 + (l * 2 + d) * 2:PV_H0 + (l * 2 + d) * 2 + 2] = _fm(st[l, d], 2)
    m["pvec"] = pv
    return m


class Res:
    __slots__ = ("w", "r", "name")
    def __init__(self, name=""):
        self.w = None; self.r = {}; self.name = name


class KB:
    def __init__(self, nc, es):
        self.nc = nc; self.es = es
        self.E = {"pe": nc.tensor, "dve": nc.vector, "act": nc.scalar, "pool": nc.gpsimd, "sp": nc.sync}
        self.semobj = {}
        self.cnt = {}
        for k in ("pe", "dve", "act", "pool"):
            self.semobj[k] = nc.alloc_semaphore(name="sem_" + k); self.cnt[k] = 0
        self.waited = {k: {} for k in self.E}
        self.banks = []; self.bres = []
        for i in range(8):
            self.banks.append(es.enter_context(nc.psum_tensor("bank%d" % i, [128, 512], F32)))
            self.bres.append(Res("bank%d" % i))
        self.bptr = {"A": 0, "B": 0}
        self.nsb = 0
        self.outsems = []

    def sb(self, name, shape, dt=F32):
        t = self.es.enter_context(self.nc.sbuf_tensor("s_" + name, list(shape), dt))
        return t

    def bank(self, pool):
        i = self.bptr[pool]; self.bptr[pool] = (i + 1) % 4
        j = i if pool == "A" else 4 + i
        return self.banks[j], self.bres[j]

    def deps_of(self, reads, writes):
        d = {}
        for r in reads:
            if r.w is not None:
                sk, v = r.w; d[sk] = max(d.get(sk, 0), v)
        for w in writes:
            if w.w is not None:
                sk, v = w.w; d[sk] = max(d.get(sk, 0), v)
            for sk, v in w.r.items(): d[sk] = max(d.get(sk, 0), v)
        return list(d.items())

    def _wait(self, e, deps):
        for sk, v in deps:
            if self.waited[e].get(sk, 0) >= v: continue
            self.E[e].wait_ge(self.semobj[sk], v)
            self.waited[e][sk] = v

    def op(self, e, fn, reads=(), writes=()):
        self._wait(e, self.deps_of(reads, writes))
        ins = fn(self.E[e])
        self.cnt[e] += 1
        ins.then_inc(self.semobj[e], 1)
        for r in reads: r.r[e] = self.cnt[e]
        for w in writes: w.w = (e, self.cnt[e]); w.r = {}
        return ins

    def mm(self, out, lhsT, rhs, start, stop, reads, bank, transpose=False):
        deps = [(s, v) for s, v in self.deps_of(reads, [bank]) if s != "pe"]
        self._wait("pe", deps)
        if transpose:
            ins = self.nc.tensor.transpose(out, lhsT, rhs)
        else:
            ins = self.nc.tensor.matmul(out, lhsT, rhs, start=start, stop=stop)
        self.cnt["pe"] += 1
        ins.then_inc(self.semobj["pe"], 1)
        for r in reads: r.r["pe"] = self.cnt["pe"]
        if stop:
            bank.w = ("pe", self.cnt["pe"]); bank.r = {}
        return ins

    def dma(self, q, out, in_, reads, writes, semkey):
        if semkey not in self.semobj:
            self.semobj[semkey] = self.nc.alloc_semaphore(name="dsem_" + semkey); self.cnt[semkey] = 0
        self._wait(q, self.deps_of(reads, writes))
        ins = self.E[q].dma_start(out=out, in_=in_)
        self.cnt[semkey] += 16
        ins.then_inc(self.semobj[semkey], 16)
        v = self.cnt[semkey]
        for r in reads: r.r[semkey] = v
        for w in writes: w.w = (semkey, v); w.r = {}
        return ins


class WStream:
    def __init__(self, kb, nslots=6, width=11 * 128):
        self.kb = kb; self.n = nslots; self.i = 0
        self.slots = [kb.sb("wslot%d" % i, [128, width], BF16) for i in range(nslots)]
        self.res = [Res("wslot%d" % i) for i in range(nslots)]

    def load(self, dram_piece, nk):
        i = self.i; self.i = (i + 1) % self.n
        s = self.slots[i]
        v = s[:, 0:nk * 128].rearrange("p (k n) -> p k n", k=nk)
        self.kb.dma("pool", v, dram_piece, [], [self.res[i]], "w%d" % i)
        return v, self.res[i]


def build_program(cfg):
    nc = bass.Bass("TRN2", target_bir_lowering=False)
    es = ExitStack()
    kb = KB(nc, es)
    NLAY = cfg["layers"]
    dbg_names = cfg.get("dbg", [])

    def din(name, shape, dt=F32):
        return nc.dram_tensor(name, list(shape), dt, kind="ExternalInput").ap()

    def dout(name, shape, dt=F32):
        return nc.dram_tensor(name, list(shape), dt, kind="ExternalOutput").ap()

    I = {}
    I["xp"] = din("xp", [128, 8, T]); I["xs"] = din("xs", [128, 8, T])
    I["ckT"] = din("ckT", [NL, 2, 128, 256]); I["cv"] = din("cv", [NL, 128, 2, 128])
    I["pvec"] = din("pvec", [128, NPV]); I["vec"] = din("vec", [128, NV])
    NLD = cfg.get("nld", NL)
    I["w_in_p"] = din("w_in_p", [NLD, 17, 128, 8, 128]); I["w_out_p"] = din("w_out_p", [NLD, 8, 128, 8, 128])
    I["w1_p"] = din("w1_p", [NLD, NF, 128, 8, 128]); I["w3_p"] = din("w3_p", [NLD, NF, 128, 8, 128])
    I["w2_p"] = din("w2_p", [NLD, 8, 128, NF, 128]); I["w_mod_p"] = din("w_mod_p", [NLD, cfg.get("nf", 48), 128, 8, 128])
    I["gw"] = din("gw", [NL, 128, 8, 128])
    I["hf_w1"] = din("hf_w1", [NL, 33, 64]); I["hf_w2"] = din("hf_w2", [NL, 64, 64]); I["hf_w3"] = din("hf_w3", [NL, 65, 1024])
    I["ident"] = din("ident", [128, 128]); I["rrot"] = din("rrot", [128, 128])
    I["ropec"] = din("ropec", [128, T]); I["ropes"] = din("ropes", [128, T])
    for L in (256, 1024):
        I["zT%d" % L] = din("zT%d" % L, [33, L]); I["dec%d" % L] = din("dec%d" % L, [128, L // 128, 2, 256])
        I["dftc%d" % L] = din("dftc%d" % L, [128, L // 128, L]); I["dfts%d" % L] = din("dfts%d" % L, [128, L // 128, L])
    I["altr"] = din("altr", [1, T]); I["altcin"] = din("altcin", [128, 1])
    O = {}
    O["yp"] = dout("yp", [128, 8, T]); O["ys"] = dout("ys", [128, 8, T])
    O["nk"] = dout("nk", [NL, 128, T]); O["nv"] = dout("nv", [NL, 128, T]); O["nst"] = dout("nst", [128, 64])
    DBG = {}

    def dbg(name, ap, res, shape):
        if name not in dbg_names: return
        o = dout("dbg_" + name, shape, ap.dtype)
        kb.dma("sp", o, ap, [res], [], "dbg_" + name)
        kb.outsems.append("dbg_" + name)

    sb = kb.sb
    x = sb("x", [128, 8, T]); x_r = [[Res("x%d_%d" % (c, h)) for h in range(2)] for c in range(8)]
    hT = sb("hT", [128, 8, T], BF16); hT_r = [[Res() for h in range(2)] for c in range(8)]
    mix_r = [[Res() for h in range(2)] for c in range(8)]
    ovl = sb("ovl", [128, 60 * 1024], mybir.dt.uint8)
    mixT = ovl[:, 0:16 * 1024].bitcast(BF16).rearrange("p (c t) -> p c t", c=8)
    vec = sb("vec", [128, NV]); pvec = sb("pvec", [128, NPV]); cres = Res("consts")
    ident = sb("ident", [128, 128]); rrot = sb("rrot", [128, 128])
    identb = sb("identb", [128, 128], BF16)
    ropec = sb("ropec", [128, T]); ropes = sb("ropes", [128, T])
    onesb = sb("onesb", [128, 128], BF16); onesbd = sb("onesbd", [128, 128], BF16)
    altc = sb("altcb", [128, 1], BF16); altr = sb("altrb", [1, T], BF16)
    dft = {256: (sb("dc256", [128, 2, 256], BF16), sb("ds256", [128, 2, 256], BF16)),
           1024: (sb("dc1024", [128, 8, 1024], BF16), sb("ds1024", [128, 8, 1024], BF16))}
    modv = sb("modv", [128, NL, 48, 2])
    modA = sb("modA", [128, NL, 2, 2, 8])
    sT = sb("sT", [128, 8, 2], BF16)
    stt = sb("stt", [128, 64])
    stt_r = Res("stt")
    gwb = sb("gwb", [128, 8, 128], BF16); gwb_r = Res("gwb")
    fw1 = sb("fw1", [33, 64]); fw2 = sb("fw2", [64, 64]); fw_r = Res("fw")
    rstd_t = [sb("rstd%d" % i, [128, 512]) for i in range(2)]; rstd_r = [Res("rstd%d" % i) for i in range(2)]; rstd_p = [0]
    dect = [sb("dect%d" % i, [128, 2, 256]) for i in range(2)]; dect_r = [Res("dect%d" % i) for i in range(2)]
    KN = sb("KN", [1, 512]); KN_r = Res("KN")
    cneg = sb("cneg", [128, 4]); cneg_r = Res("cneg")
    scr = [sb("scr%d" % i, [128, 512]) for i in range(6)]; scr_r = [Res("scr%d" % i) for i in range(6)]
    scb = [sb("scb%d" % i, [128, 512], BF16) for i in range(4)]; scb_r = [Res("scb%d" % i) for i in range(4)]
    scrp = [0]; scbp = [0]

    def S():
        i = scrp[0]; scrp[0] = (i + 1) % len(scr); return scr[i], scr_r[i]

    def SBF():
        i = scbp[0]; scbp[0] = (i + 1) % len(scb); return scb[i], scb_r[i]

    ws = WStream(kb)

    def carve(off, shape, dt):
        esz = 4 if dt == F32 else 2
        n = int(np.prod(shape[1:])) * esz
        v = ovl[:, off:off + n].bitcast(dt)
        if len(shape) == 3:
            v = v.rearrange("p (a b) -> p a b", a=shape[1])
        elif len(shape) == 4:
            v = v.rearrange("p (a b c) -> p a b c", a=shape[1], b=shape[2])
        return v, off + n

    V = lambda key, n=1: vec[:, VCOLS[key]:VCOLS[key] + n]

    def cload(q, dst, src):
        kb.dma(q, dst, src, [], [cres], "c0")
    cload("sp", vec[:], I["vec"]); cload("sp", pvec[:], I["pvec"])
    cload("sp", ident[:], I["ident"]); cload("sp", rrot[:], I["rrot"])
    cload("sp", ropec[:], I["ropec"]); cload("sp", ropes[:], I["ropes"])
    kb.dma("pool", identb[:], I["ident"], [], [cres], "c1")
    kb.dma("pool", altr[:], I["altr"], [], [cres], "c1")
    kb.dma("pool", dft[256][0][:], I["dftc256"], [], [cres], "c1")
    kb.dma("pool", dft[256][1][:], I["dfts256"], [], [cres], "c1")
    kb.dma("pool", altc[:], I["altcin"], [], [cres], "c1")
    cres2 = Res("c0all"); cres2.w = ("c0", kb.cnt["c0"])
    cres3 = Res("c1all"); cres3.w = ("c1", kb.cnt["c1"])
    CR = [cres2, cres3]
    kb.op("dve", lambda e: e.memset(onesb[:], 1.0), [], [cres])
    kb.op("dve", lambda e: e.memset(onesbd[:], 0.0), [], [cres])
    kb.op("dve", lambda e: e.memset(onesbd[0:64, 0:64], 1.0), [], [cres])
    kb.op("dve", lambda e: e.memset(onesbd[64:128, 64:128], 1.0), [], [cres])
    kb.op("dve", lambda e: e.memset(stt[:], 0.0), [], [stt_r])
    CR.append(cres)

    nmod = NLAY if cfg.get("stop") not in ("const",) else 0
    kb.op("act", lambda e: e.activation(out=sT[:].rearrange("p k j -> p (k j)"), in_=pvec[:, PV_COND:PV_COND + 16], func=AF.Silu), CR, [cres])
    mod_r = [Res("mod%d" % l) for l in range(NL)]

    def mod_piece(l, f):
        wv, wr = ws.load(I["w_mod_p"][l, f], 8)
        bk, br = kb.bank("A")
        for k in range(8):
            kb.mm(bk[:, 0:2], wv[:, k, :], sT[:, k, :], k == 0, k == 7, [wr, cres], br)
        kb.op("dve", lambda e: e.tensor_scalar(out=modv[:, l, f, :], in0=bk[:, 0:2], scalar1=V(("bmod", l), 48)[:, f:f + 1], scalar2=None, op0=ALU.add), [br] + CR, [mod_r[l]])

    def mod_finish(l, whs=(0, 1)):
        for j in range(2):
            for wh in whs:
                sc = modv[:, l, (8 if wh == 0 else 32):(16 if wh == 0 else 40), j]
                nw = V(("n1" if wh == 0 else "n2", l), 8)
                kb.op("dve", lambda e: e.scalar_tensor_tensor(out=modA[:, l, j, wh, :], in0=sc, scalar=1.0, in1=nw, op0=ALU.add, op1=ALU.mult), CR + [mod_r[l]], [mod_r[l]])

    pending_mod = []
    if nmod > 0:
        nf0 = cfg.get("nf", 48)
        for f in range(min(16, nf0)):
            mod_piece(0, f)
        mod_finish(0, (0,))
        pending_mod.extend(("p", 0, f) for f in range(16, nf0))
        pending_mod.append(("f", 0, (1,)))

    def run_pending_mod(n):
        for _ in range(n):
            if not pending_mod: return
            it = pending_mod.pop(0)
            if it[0] == "p": mod_piece(it[1], it[2])
            else: mod_finish(it[1], it[2])
    dbg("modv", modv[:].rearrange("p l f j -> p (l f j)"), mod_r[0], [128, NL * 96])

    def rms_bc(src_fn, src_res, nchunk, ones_t, inv_n, half_w=512):
        bk, br = kb.bank("B")
        for c in range(nchunk):
            sq, sqr = SBF()
            a, ar = src_fn(c)
            if nchunk > 1 and c % 2 == 0 and cfg.get("poolsq", True):
                kb.op("pool", lambda e: e.tensor_tensor(sq[:, 0:half_w], a, a, ALU.mult), [ar], [sqr])
            else:
                kb.op("act", lambda e: e.activation(out=sq[:, 0:half_w], in_=a, func=AF.Square), [ar], [sqr])
            kb.mm(bk[:, 0:half_w], ones_t[:], sq[:, 0:half_w], c == 0, c == nchunk - 1, [sqr, cres], br)
        i_ = rstd_p[0]; rstd_p[0] = 1 - i_
        s1, s1r = rstd_t[i_], rstd_r[i_]
        kb.op("act", lambda e: e.activation(out=s1[:, 0:half_w], in_=bk[:, 0:half_w], func=AF.Sqrt, bias=EPS, scale=inv_n), [br], [s1r])
        kb.op("dve", lambda e: e.reciprocal(s1[:, 0:half_w], s1[:, 0:half_w]), [s1r], [s1r])
        return s1, s1r

    def norm_mod(l, j, wh):
        shoff = 0 if wh == 0 else 24
        bks = [kb.bank("B"), kb.bank("B")]
        for c in range(8):
            for h in range(2):
                hs = slice(h * 512, (h + 1) * 512)
                sq, sqr = SBF()
                if (c + h) % 2 == 0:
                    kb.op("pool", lambda e: e.tensor_tensor(sq[:], x[:, c, hs], x[:, c, hs], ALU.mult), [x_r[c][h]], [sqr])
                else:
                    kb.op("act", lambda e: e.activation(out=sq[:], in_=x[:, c, hs], func=AF.Square), [x_r[c][h]], [sqr])
                kb.mm(bks[h][0][:, :], onesb[:], sq[:], c == 0, c == 7, [sqr, cres], bks[h][1])
        for h in range(2):
            kb.op("act", lambda e: e.activation(out=rstd_t[h][:], in_=bks[h][0][:, :], func=AF.Ln, bias=EPS, scale=1.0 / D), [bks[h][1]], [rstd_r[h]])
        for h in range(2):
            kb.op("act", lambda e: e.activation(out=rstd_t[h][:], in_=rstd_t[h][:], func=AF.Exp, scale=-0.5), [rstd_r[h]], [rstd_r[h]])
        for c in range(8):
            for h in range(2):
                hs = slice(h * 512, (h + 1) * 512)
                t1, t1r = S()
                kb.op("dve", lambda e: e.scalar_tensor_tensor(out=t1[:], in0=x[:, c, hs], scalar=modA[:, l, j, wh, c:c + 1], in1=rstd_t[h][:], op0=ALU.mult, op1=ALU.mult), [x_r[c][h], rstd_r[h], cres, mod_r[l]], [t1r])
                kb.op("act", lambda e: e.activation(out=hT[:, c, hs], in_=t1[:], func=AF.Identity, bias=modv[:, l, shoff + c, j:j + 1], scale=1.0), [t1r, cres, mod_r[l]], [hT_r[c][h]])

    def project(piece, nk, rhs_fn, pool="A"):
        wv, wr = ws.load(piece, nk)
        outs = []
        for h in range(2):
            bk, br = kb.bank(pool)
            for k in range(nk):
                a, ar = rhs_fn(k, h)
                kb.mm(bk[:, :], wv[:, k, :], a, k == 0, k == nk - 1, [wr, ar], br)
            outs.append((bk, br))
        run_pending_mod(2)
        return outs

    hT_rhs = lambda k, h: (hT[:, k, h * 512:(h + 1) * 512], hT_r[k][h])
    mix_rhs = lambda k, h: (mixT[:, k, h * 512:(h + 1) * 512], mix_r[k][h])

    def headnorm(l, bk, br, wkey):
        rstd, rr = rms_bc(lambda c: (bk[:, :], br), None, 1, onesbd, 1.0 / HD)
        o, o_r = S()
        kb.op("dve", lambda e: e.scalar_tensor_tensor(out=o[:], in0=bk[:, :], scalar=V((wkey, l)), in1=rstd[:], op0=ALU.mult, op1=ALU.mult), [br, rr, cres], [o_r])
        return o, o_r

    def rope(src, src_r, h, dst, dst_r):
        hs = slice(h * 512, (h + 1) * 512)
        bk, br = kb.bank("B")
        kb.mm(bk[:, :], rrot[:], src[:], True, True, [src_r, cres], br)
        t1, t1r = S()
        kb.op("dve", lambda e: e.tensor_tensor(t1[:], bk[:, :], ropes[:, hs], ALU.mult), [br, cres], [t1r])
        t2, t2r = S()
        kb.op("dve", lambda e: e.tensor_tensor(t2[:], src[:], ropec[:, hs], ALU.mult), [src_r, cres], [t2r])
        kb.op("dve", lambda e: e.tensor_tensor(dst, t1[:], t2[:], ALU.add), [t1r, t2r], [dst_r])

    def qk_prep(l, outs, wkey, dsts, dst_r, do_rope, extra=None):
        sq = []
        for h in range(2):
            t_, r_ = SBF(); sq.append((t_, r_))
            kb.op("act", lambda e: e.activation(out=t_[:], in_=outs[h][0][:, :], func=AF.Square), [outs[h][1]], [r_])
        rb = []
        for h in range(2):
            b_, br_ = kb.bank("B"); rb.append((b_, br_))
            kb.mm(b_[:, :], onesbd[:], sq[h][0][:], True, True, [sq[h][1], cres], br_)
        for h in range(2):
            kb.op("act", lambda e: e.activation(out=rstd_t[h][:], in_=rb[h][0][:, :], func=AF.Ln, bias=EPS, scale=1.0 / HD), [rb[h][1]], [rstd_r[h]])
        for h in range(2):
            kb.op("act", lambda e: e.activation(out=rstd_t[h][:], in_=rstd_t[h][:], func=AF.Exp, scale=-0.5), [rstd_r[h]], [rstd_r[h]])
        o = []
        for h in range(2):
            t_, r_ = S(); o.append((t_, r_))
            kb.op("dve", lambda e: e.scalar_tensor_tensor(out=t_[:], in0=outs[h][0][:, :], scalar=V((wkey, l)), in1=rstd_t[h][:], op0=ALU.mult, op1=ALU.mult), [outs[h][1], rstd_r[h], cres], [r_])
        if extra is not None:
            for h in range(2): extra(h, o[h][0], o[h][1])
        if not do_rope:
            for h in range(2):
                kb.op("act", lambda e: e.activation(out=dsts[h], in_=o[h][0][:], func=AF.Copy), [o[h][1]], [dst_r])
            return
        pb = []
        for h in range(2):
            b_, br_ = kb.bank("B"); pb.append((b_, br_))
            kb.mm(b_[:, :], rrot[:], o[h][0][:], True, True, [o[h][1], cres], br_)
        t1 = []; t2 = []
        for h in range(2):
            t_, r_ = S(); t1.append((t_, r_))
            kb.op("dve", lambda e: e.tensor_tensor(t_[:], pb[h][0][:, :], ropes[:, h * 512:(h + 1) * 512], ALU.mult), [pb[h][1], cres], [r_])
        for h in range(2):
            t_, r_ = S(); t2.append((t_, r_))
            kb.op("dve", lambda e: e.tensor_tensor(t_[:], o[h][0][:], ropec[:, h * 512:(h + 1) * 512], ALU.mult), [o[h][1], cres], [r_])
        for h in range(2):
            kb.op("dve", lambda e: e.tensor_tensor(dsts[h], t1[h][0][:], t2[h][0][:], ALU.add), [t1[h][1], t2[h][1]], [dst_r])

    def dwconv(out3, in3, res_out, res_in, center, taps, bias):
        L = in3.shape[-1]
        kb.op("act", lambda e: e.activation(out=out3, in_=in3, func=AF.Identity, bias=bias, scale=center), [res_in, cres], [res_out])
        for sh, wcol in taps:
            if sh < 0:
                o_ = out3[:, :, -sh:L]; i_ = in3[:, :, 0:L + sh]
            else:
                o_ = out3[:, :, 0:L - sh]; i_ = in3[:, :, sh:L]
            kb.op("dve", lambda e: e.scalar_tensor_tensor(out=o_, in0=i_, scalar=wcol, in1=o_, op0=ALU.mult, op1=ALU.add), [res_in, res_out, cres], [res_out])

    st = dict(nc=nc, kb=kb, es=es, I=I, O=O, x=x, x_r=x_r, hT=hT, hT_r=hT_r, mixT=mixT, mix_r=mix_r, ws=ws,
              V=V, CR=CR, cres=cres, S=S, SBF=SBF, carve=carve, dbg=dbg, norm_mod=norm_mod, project=project,
              hT_rhs=hT_rhs, mix_rhs=mix_rhs, mod_piece=mod_piece, mod_finish=mod_finish, mod_r=mod_r, run_pending_mod=run_pending_mod, headnorm=headnorm, rope=rope, qk_prep=qk_prep, dwconv=dwconv, modv=modv, pvec=pvec,
              ident=ident, identb=identb, onesb=onesb, altc=altc, altr=altr, dft=dft, stt=stt, stt_r=stt_r,
              gwb=gwb, gwb_r=gwb_r, fw1=fw1, fw2=fw2, fw_r=fw_r, cfg=cfg, NLAY=NLAY, ovl=ovl, dect=dect, dect_r=dect_r,
              KN=KN, KN_r=KN_r, cneg=cneg, cneg_r=cneg_r, ropec=ropec, ropes=ropes, sb=sb)
    return st


class _Stop(Exception):
    pass


def emit_passes(st):
    nc = st["nc"]; kb = st["kb"]; I = st["I"]; O = st["O"]; x = st["x"]; x_r = st["x_r"]
    hT = st["hT"]; hT_r = st["hT_r"]; mixT = st["mixT"]; mix_r = st["mix_r"]; ws = st["ws"]
    V = st["V"]; CR = st["CR"]; cres = st["cres"]; S = st["S"]; SBF = st["SBF"]; carve = st["carve"]; dbg = st["dbg"]
    project = st["project"]; hT_rhs = st["hT_rhs"]; mix_rhs = st["mix_rhs"]; headnorm = st["headnorm"]; rope = st["rope"]
    dwconv = st["dwconv"]; modv = st["modv"]; pvec = st["pvec"]; ident = st["ident"]; identb = st["identb"]
    onesb = st["onesb"]; altc = st["altc"]; altr = st["altr"]; dft = st["dft"]; stt = st["stt"]; stt_r = st["stt_r"]
    gwb = st["gwb"]; gwb_r = st["gwb_r"]; fw1 = st["fw1"]; fw2 = st["fw2"]; fw_r = st["fw_r"]
    dect = st["dect"]; dect_r = st["dect_r"]; KN = st["KN"]; KN_r = st["KN_r"]; cneg = st["cneg"]; cneg_r = st["cneg_r"]
    cfg = st["cfg"]; NLAY = st["NLAY"]
    WOFF = 16 * 1024; FOFF = 44 * 1024

    live = {"M": [], "W": [], "F": [], "G": []}

    def pe_warm(n):
        for i in range(n):
            bk, br = kb.bank("A")
            kb.mm(bk[:, :], onesb[:], hT[:, 0, 0:512], True, True, CR, br)

    STAGES = ["const", "mod", "filt", "norm", "attn", "lru", "hy", "wout"]

    def chk(name):
        sp = cfg.get("stop")
        if sp is not None and STAGES.index(name) >= STAGES.index(sp): raise _Stop()

    def merge_into(new_list, old_lists):
        ev = {}
        for ol in old_lists:
            for r in ol:
                if r.w is not None: ev[r.w[0]] = max(ev.get(r.w[0], 0), r.w[1])
                for sk, v in r.r.items(): ev[sk] = max(ev.get(sk, 0), v)
        for r in new_list:
            r.w = None; r.r = dict(ev)

    def w_phase(new_list):
        merge_into(new_list, [live["W"]]); live["W"] = list(new_list)

    allmix = [r for c in mix_r for r in c]
    live["M"] = allmix

    def run_pass(grp):
        P = grp == "P"; j = 0 if P else 1
        nseq, L = (4, 256) if P else (1, 1024); nP = L // 128
        KT = T if P else T + 256
        Cq, Sq = dft[L]
        xin = I["xp"] if P else I["xs"]
        for c in range(8):
            kb.dma("sp", x[:, c, :], xin[:, c, :], [], [x_r[c][0], x_r[c][1]], "xin%d" % c)
        if cfg["sample"] and "c2" not in kb.cnt:
            c2w = Res("c2w")
            kb.dma("pool", dft[1024][0][:], I["dftc1024"], [], [c2w], "c2")
            kb.dma("pool", dft[1024][1][:], I["dfts1024"], [], [c2w], "c2")
            st["c2all"] = Res("c2all"); st["c2all"].w = ("c2", kb.cnt["c2"])
        if not P:
            CR.append(st["c2all"])

        for l in range(NLAY):
            oldG = live["G"]; live["G"] = []
            merge_into(allmix, [oldG])
            At, _ = carve(FOFF, [128, nP, 512], BF16); Bt, _ = carve(FOFF + nP * 1024, [128, nP, 512], BF16)
            At_r = Res("At"); Bt_r = Res("Bt")
            merge_into([At_r, Bt_r], [live["F"]]); live["F"] = [At_r, Bt_r]
            o_ = WOFF
            h2a = st["ovl"][0:65, o_:o_ + L * 2].bitcast(BF16); o_ += L * 4
            fw3 = st["ovl"][0:65, o_:o_ + 2048].bitcast(BF16); o_ += 4096
            Pq, o_ = carve(o_, [128, nP, 512], BF16); Qq, o_ = carve(o_, [128, nP, 512], BF16)
            h2a_r = Res("h2a"); fw3_r = Res("fw3"); Pq_r = Res("Pq"); Qq_r = Res("Qq")
            merge_into([h2a_r, fw3_r, Pq_r, Qq_r], [live["W"], oldG]); live["W"] = [h2a_r, fw3_r, Pq_r, Qq_r]
            kb.dma("sp", fw1[:], I["hf_w1"][l], [], [fw_r], "fw")
            kb.dma("sp", fw2[:], I["hf_w2"][l], [], [fw_r], "fw")
            kb.dma("pool", fw3, I["hf_w3"][l], [], [fw3_r], "fw3")
            fwa = Res("fwall"); fwa.w = ("fw", kb.cnt["fw"])
            kb.dma("pool", gwb[:], I["gw"][l], [], [gwb_r], "gw")
            kb.op("dve", lambda e: e.memset(h2a[64:65, :], 1.0), [], [h2a_r])

            def sin_evac(ps, br, bcol, dst, dst_r, w):
                xx, xr = S()
                kb.op("dve", lambda e: e.tensor_scalar(out=xx[0:64, 0:w], in0=ps, scalar1=bcol, scalar2=None, op0=ALU.add), [br, cres], [xr])
                m1, m1r = S()
                kb.op("dve", lambda e: e.tensor_scalar(out=m1[0:64, 0:w], in0=xx[0:64, 0:w], scalar1=PI, scalar2=-2 * PI, op0=ALU.is_gt, op1=ALU.mult), [xr], [m1r])
                m2, m2r = S()
                kb.op("dve", lambda e: e.tensor_scalar(out=m2[0:64, 0:w], in0=xx[0:64, 0:w], scalar1=-PI, scalar2=2 * PI, op0=ALU.is_lt, op1=ALU.mult), [xr], [m2r])
                kb.op("dve", lambda e: e.tensor_tensor(xx[0:64, 0:w], xx[0:64, 0:w], m1[0:64, 0:w], ALU.add), [xr, m1r], [xr])
                kb.op("dve", lambda e: e.tensor_tensor(xx[0:64, 0:w], xx[0:64, 0:w], m2[0:64, 0:w], ALU.add), [xr, m2r], [xr])
                kb.op("dve", lambda e: e.tensor_scalar(out=xx[0:64, 0:w], in0=xx[0:64, 0:w], scalar1=3.1415925, scalar2=-3.1415925, op0=ALU.min, op1=ALU.max), [xr], [xr])
                kb.op("act", lambda e: e.activation(out=dst, in_=xx[0:64, 0:w], func=AF.Sin), [xr], [dst_r])

            for cb in range((L + 511) // 512):
                w = min(512, L - cb * 512)
                zt, ztr = S()
                kb.dma("sp", zt[0:33, 0:w], I["zT%d" % L][:, cb * 512:cb * 512 + w], [], [ztr], "zt")
                bk, br = kb.bank("A")
                kb.mm(bk[0:64, 0:w], fw1[0:33, 0:64], zt[0:33, 0:w], True, True, [fwa, ztr], br)
                h1, h1r = S()
                sin_evac(bk[0:64, 0:w], br, V(("fb1", l))[0:64, :], h1[0:64, 0:w], h1r, w)
                bk2, br2 = kb.bank("A")
                kb.mm(bk2[0:64, 0:w], fw2[0:64, 0:64], h1[0:64, 0:w], True, True, [fwa, h1r], br2)
                sin_evac(bk2[0:64, 0:w], br2, V(("fb2", l))[0:64, :], h2a[0:64, cb * 512:cb * 512 + w], h2a_r, w)
            for pc in range(nP):
                di = pc % 2
                kb.dma("sp", dect[di][:], I["dec%d" % L][:, pc], [], [dect_r[di]], "dec%d" % di)
                for ob in range(2):
                    bk, br = kb.bank("A")
                    kb.mm(bk[:, :], h2a[0:65, pc * 128:(pc + 1) * 128], fw3[0:65, ob * 512:(ob + 1) * 512], True, True, [h2a_r, fw3_r], br)
                    f_, fr = S()
                    kb.op("dve", lambda e: e.tensor_tensor(f_[:], bk[:, :], dect[di][:].rearrange("p a b -> p (a b)"), ALU.mult), [br, dect_r[di]], [fr])
                    kb.op("dve", lambda e: e.tensor_tensor(Pq[:, pc, ob * 256:(ob + 1) * 256], f_[:, 0:256], f_[:, 256:512], ALU.add), [fr], [Pq_r])
                    kb.op("dve", lambda e: e.tensor_tensor(Qq[:, pc, ob * 256:(ob + 1) * 256], f_[:, 256:512], f_[:, 0:256], ALU.subtract), [fr], [Qq_r])
            cwk = "cw%d" % L
            for m in range(nP):
                for (tab, tq, src, src_r, dst, dst_r) in ((Cq, 0, Pq, Pq_r, At, At_r), (Sq, 1, Qq, Qq_r, Bt, Bt_r)):
                    bk, br = kb.bank("A")
                    for pc in range(nP):
                        kb.mm(bk[:, :], tab[:, pc, m * 128:(m + 1) * 128], src[:, pc, :], pc == 0, pc == nP - 1, [src_r] + CR, br)
                    kb.op("act", lambda e: e.activation(out=dst[:, m, :], in_=bk[:, :], func=AF.Identity, scale=V(cwk, nP)[:, m:m + 1]), [br] + CR, [dst_r])
            bk, br = kb.bank("A")
            for pc in range(nP):
                kb.mm(bk[0:1, :], altc[:, 0:1], Pq[:, pc, :], pc == 0, pc == nP - 1, [Pq_r] + CR, br)
            kb.op("act", lambda e: e.activation(out=KN[0:1, :], in_=bk[0:1, :], func=AF.Identity, scale=0.5 / L), [br], [KN_r])
            if l == 0:
                dbg("At" + grp, At[:].rearrange("p a b -> p (a b)"), At_r, [128, nP * 512])
                dbg("KN" + grp, KN[:], KN_r, [1, 512])

            chk("filt")
            kb.op("act", lambda e: e.activation(out=cneg[:], in_=V(("lam", l), 4), func=AF.Exp, scale=-1.0), CR, [cneg_r])
            kb.op("act", lambda e: e.activation(out=cneg[:], in_=cneg[:], func=AF.Ln, bias=1.0, scale=1.0), [cneg_r], [cneg_r])
            kb.op("dve", lambda e: e.tensor_scalar(out=cneg[:], in0=cneg[:], scalar1=-8.0, scalar2=None, op0=ALU.mult), [cneg_r], [cneg_r])

            st["norm_mod"](l, j, 0)
            if l == 0: dbg("hT" + grp, hT[:, 0, :], hT_r[0][1], [128, T])

            chk("norm")
            o_ = WOFF
            qT, o_ = carve(o_, [128, 4, T], BF16); kdT, o_ = carve(o_, [128, 2, KT], BF16)
            Vtok, o_ = carve(o_, [128, KT // 128, 256], BF16); vT, o_ = carve(o_, [128, T], F32); kst, o_ = carve(o_, [128, T], F32)
            q_r = [Res("q%d" % c) for c in range(4)]; kd_r = [Res("kd0"), Res("kd1")]; vt_r = Res("Vtok"); vT_r = Res("vT"); kst_r = Res("kst")
            w_phase(q_r + kd_r + [vt_r, vT_r, kst_r])
            koff = 0 if P else 256
            Vt4 = Vtok.rearrange("p k (g e) -> p k g e", g=2)
            kb.op("dve", lambda e: e.memset(Vtok[:], 1.0), [], [vt_r])
            if not P:
                for g in range(2):
                    kb.dma("pool", kdT[:, g, 0:256], I["ckT"][l, g], [], [kd_r[g]], "ck%d" % g)
                kb.dma("pool", Vt4[:, 0:2, :, 0:64], I["cv"][l].rearrange("p c (g d) -> p c g d", g=2), [vt_r], [vt_r], "cv")
            for g in range(2):
                outs = project(I["w_in_p"][l, g], 8, hT_rhs)
                if P:
                    def kextra(h, o_, o_r_, g=g):
                        kb.op("act", lambda e: e.activation(out=kst[g * 64:(g + 1) * 64, h * 512:(h + 1) * 512], in_=o_[0:64, :], func=AF.Copy), [o_r_], [kst_r])
                    st["qk_prep"](l, outs, "kn", [kdT[:, g, 0:512], kdT[:, g, 512:1024]], kd_r[g], False, kextra)
                else:
                    st["qk_prep"](l, outs, "kn", [kdT[:, g, 256:768], kdT[:, g, 768:1280]], kd_r[g], True)
            if P:
                kb.dma("sp", O["nk"][l], kst[:], [kst_r], [], "onk");
                if "onk" not in kb.outsems: kb.outsems.append("onk")
            outs = project(I["w_in_p"][l, 2], 8, hT_rhs)
            for h, (bk, br) in enumerate(outs):
                kb.op("act", lambda e: e.activation(out=vT[:, h * 512:(h + 1) * 512], in_=bk[:, :], func=AF.Copy), [br], [vT_r])
            if P:
                kb.dma("sp", O["nv"][l], vT[:], [vT_r], [], "onv")
                if "onv" not in kb.outsems: kb.outsems.append("onv")
            for tb4 in range(2):
                bk, br = kb.bank("B")
                for i4 in range(4):
                    tb = tb4 * 4 + i4
                    kb.mm(bk[:, i4 * 128:(i4 + 1) * 128], vT[:, tb * 128:(tb + 1) * 128], ident[:], True, True, [vT_r] + CR, br, transpose=True)
                for g_ in range(2):
                    kb.op("dve", lambda e: e.tensor_copy(Vt4[:, koff // 128 + tb4 * 4:koff // 128 + tb4 * 4 + 4, g_, 0:64], bk[:, :].rearrange("p (a g d) -> p a g d", a=4, g=2)[:, :, g_, :]), [br, vt_r], [vt_r])
            if l == 0:
                dbg("kdT" + grp, kdT[:, 0, :], kd_r[0], [128, KT])

            for c in range(4):
                outs = project(I["w_in_p"][l, 3 + c], 8, hT_rhs)
                st["qk_prep"](l, outs, "qn", [qT[:, c, 0:512], qT[:, c, 512:1024]], q_r[c], not P)
            for c in range(4):
                g = c // 2
                NI = 2 if P else KT // 128
                items = [(qh, i) for qh in range(2) for i in range(NI)]
                Sb = {}
                pair = {}

                def emit_S(n):
                    qh, i = items[n]
                    bks = [kb.bank("A"), kb.bank("A")]
                    if P:
                        s_ = qh * 2 + i
                        for kc in range(2):
                            for ph in range(2):
                                ps_ = slice(ph * 64, (ph + 1) * 64)
                                kb.mm(bks[ph][0][:, kc * 256:(kc + 1) * 256], kdT[ps_, g, s_ * 256 + kc * 128:s_ * 256 + (kc + 1) * 128], qT[ps_, c, s_ * 256:(s_ + 1) * 256], True, True, [kd_r[g], q_r[c]], bks[ph][1])
                    else:
                        for ph in range(2):
                            ps_ = slice(ph * 64, (ph + 1) * 64)
                            kb.mm(bks[ph][0][:, :], kdT[ps_, g, i * 128:(i + 1) * 128], qT[ps_, c, qh * 512:(qh + 1) * 512], True, True, [kd_r[g], q_r[c]], bks[ph][1])
                    Sb[n] = bks

                def emit_PV(n):
                    qh, i = items[n]
                    if i == 0:
                        pair[qh] = [kb.bank("B"), kb.bank("B")]
                    bks = Sb.pop(n)
                    pts = []
                    for ph in range(2):
                        pt, ptr = SBF(); pts.append((pt, ptr))
                        kb.op("act", lambda e: e.activation(out=pt[:], in_=bks[ph][0][:, :], func=AF.Exp, scale=0.125), [bks[ph][1]], [ptr])
                    for ph in range(2):
                        nb, nbr = pair[qh][ph]
                        pt, ptr = pts[ph]
                        if P:
                            s_ = qh * 2 + i
                            for kc in range(2):
                                kb.mm(nb[:, i * 256:(i + 1) * 256], Vt4[:, 2 * s_ + kc, g, :], pt[:, kc * 256:(kc + 1) * 256], kc == 0, kc == 1, [vt_r, ptr], nbr)
                        else:
                            kb.mm(nb[:, :], Vt4[:, i, g, :], pt[:], i == 0, i == NI - 1, [vt_r, ptr], nbr)
                    if i == NI - 1:
                        for ph in range(2):
                            nb, nbr = pair[qh][ph]
                            rc, rcr = S()
                            if P or cfg.get("actrc", False):
                                kb.op("act", lambda e: e.activation(out=rc[0:64, :], in_=nb[64:128, :], func=AF.Ln), [nbr], [rcr])
                                kb.op("act", lambda e: e.activation(out=rc[0:64, :], in_=rc[0:64, :], func=AF.Exp, scale=-1.0), [rcr], [rcr])
                            else:
                                kb.op("dve", lambda e: e.reciprocal(rc[0:64, :], nb[64:128, :]), [nbr], [rcr])
                            kb.op("dve", lambda e: e.tensor_tensor(mixT[ph * 64:(ph + 1) * 64, c, qh * 512:(qh + 1) * 512], nb[0:64, :], rc[0:64, :], ALU.mult), [nbr, rcr], [mix_r[c][qh]])

                LA = 1
                for n in range(len(items) + LA):
                    if n < len(items): emit_S(n)
                    if n - LA >= 0: emit_PV(n - LA)

            if l == 0:
                for cdbg in range(4):
                    dbg("att%d" % cdbg + grp, mixT[:, cdbg, :], mix_r[cdbg][1], [128, T])
            chk("attn")
            for c in range(2):
                o_ = WOFF
                lx, o_ = carve(o_, [128, T], F32); xc, o_ = carve(o_, [128, T], F32); xcb, o_ = carve(o_, [128, T], BF16)
                gl, o_ = carve(o_, [128, T], BF16)
                rgs = []; igs = []
                for d in range(2):
                    t_, o_ = carve(o_, [128, T], F32); rgs.append(t_)
                    t_, o_ = carve(o_, [128, T], F32); igs.append(t_)
                lx_r = Res("lx"); xc_r = Res("xc"); xcb_r = Res("xcb"); gl_r = Res("gl")
                rg_rs = [Res("rg0"), Res("rg1")]; ig_rs = [Res("ig0"), Res("ig1")]
                w_phase([lx_r, xc_r, xcb_r, gl_r] + rg_rs + ig_rs)
                hd = [lx, xc]; hd_r = [lx_r, xc_r]
                hd0, hd1 = hd
                outs = project(I["w_in_p"][l, 7 + 2 * c], 8, hT_rhs)
                for h, (bk, br) in enumerate(outs):
                    kb.op("act", lambda e: e.activation(out=lx[:, h * 512:(h + 1) * 512], in_=bk[:, :], func=AF.Copy), [br], [lx_r])
                outs = project(I["w_in_p"][l, 8 + 2 * c], 8, hT_rhs)
                for h, (bk, br) in enumerate(outs):
                    hs = slice(h * 512, (h + 1) * 512)
                    a1, a1r = S()
                    kb.op("act", lambda e: e.activation(out=a1[:], in_=bk[:, :], func=AF.Square), [br], [a1r])
                    kb.op("dve", lambda e: e.tensor_scalar(out=a1[:], in0=a1[:], scalar1=0.044715, scalar2=1.0, op0=ALU.mult, op1=ALU.add), [a1r], [a1r])
                    kb.op("dve", lambda e: e.tensor_tensor(a1[:], a1[:], bk[:, :], ALU.mult), [a1r, br], [a1r])
                    kb.op("act", lambda e: e.activation(out=a1[:], in_=a1[:], func=AF.Sigmoid, scale=1.5957691216), [a1r], [a1r])
                    kb.op("dve", lambda e: e.tensor_tensor(gl[:, hs], a1[:], bk[:, :], ALU.mult), [a1r, br], [gl_r])
                v3 = lambda a: a.rearrange("p (s t) -> p s t", s=nseq)
                lcw = V(("lcw", l), 8)
                dwconv(v3(xc), v3(lx), xc_r, lx_r, lcw[:, 4 + c:5 + c],
                       [(-2, lcw[:, 0 + c:1 + c]), (-1, lcw[:, 2 + c:3 + c]), (1, lcw[:, 6 + c:7 + c])], V(("lcb", l), 2)[:, c:c + 1])
                kb.op("act", lambda e: e.activation(out=xcb, in_=xc, func=AF.Copy), [xc_r], [xcb_r])
                if l == 0 and c == 0: dbg("xc" + grp, xc, xc_r, [128, T])
                for d in range(2):
                    for gt, dst, dst_r in ((0, rgs[d], rg_rs[d]), (1, igs[d], ig_rs[d])):
                        idx = (d * 2 + gt) * 2 + c
                        for h in range(2):
                            hs = slice(h * 512, (h + 1) * 512)
                            bk, br = kb.bank("A")
                            kb.mm(bk[:, :], gwb[:, idx, :], xcb[:, hs], True, True, [gwb_r, xcb_r], br)
                            kb.op("act", lambda e: e.activation(out=dst[:, hs], in_=bk[:, :], func=AF.Sigmoid, bias=V(("lgb", l), 8)[:, idx:idx + 1], scale=1.0), [br] + CR, [dst_r])
                for d in range(2):
                    kb.op("act", lambda e: e.activation(out=rgs[d], in_=rgs[d], func=AF.Exp, scale=cneg[:, d * 2 + c:d * 2 + c + 1]), [rg_rs[d], cneg_r], [rg_rs[d]])
                for d in range(2):
                    kb.op("dve", lambda e: e.tensor_tensor(igs[d], igs[d], xc, ALU.mult), [ig_rs[d], xc_r], [ig_rs[d]])
                a2s = []
                for d in range(2):
                    for h in range(2):
                        hs = slice(h * 512, (h + 1) * 512)
                        a2, a2r = S(); a2s.append((a2, a2r))
                        kb.op("dve", lambda e: e.tensor_tensor(a2[:], rgs[d][:, hs], rgs[d][:, hs], ALU.mult), [rg_rs[d]], [a2r])
                for a2, a2r in a2s:
                    kb.op("act", lambda e: e.activation(out=a2[:], in_=a2[:], func=AF.Sqrt, bias=1.0, scale=-1.0), [a2r], [a2r])
                for d in range(2):
                    for h in range(2):
                        hs = slice(h * 512, (h + 1) * 512)
                        a2, a2r = a2s[d * 2 + h]
                        kb.op("dve", lambda e: e.tensor_tensor(igs[d][:, hs], igs[d][:, hs], a2[:], ALU.mult), [ig_rs[d], a2r], [ig_rs[d]])
                for d in range(2):
                    rg = rgs[d]; ig = igs[d]
                    for s in range(nseq):
                        lo, hi = s * L, (s + 1) * L
                        init = 0.0 if P else pvec[:, PV_H0 + (l * 2 + d) * 2 + c:PV_H0 + (l * 2 + d) * 2 + c + 1]
                        if d == 0:
                            kb.op("dve", lambda e: e.tensor_tensor_scan(hd[d][:, lo:hi], rg[:, lo:hi], ig[:, lo:hi], init, ALU.mult, ALU.add), [rg_rs[d], ig_rs[d]] + CR, [hd_r[d]])
                        else:
                            rv = (lambda a: a[:, hi - 1::-1]) if lo == 0 else (lambda a: a[:, hi - 1:lo - 1:-1])
                            kb.op("dve", lambda e: e.tensor_tensor_scan(rv(hd[d]), rv(rg), rv(ig), init, ALU.mult, ALU.add), [rg_rs[d], ig_rs[d]] + CR, [hd_r[d]])
                    if P:
                        col = ((l * 2 + d) * 2 + c) * 4
                        src = v3(hd[d])[:, :, L - 1] if d == 0 else v3(hd[d])[:, :, 0]
                        kb.op("act", lambda e: e.activation(out=stt[:, col:col + 4], in_=src, func=AF.Copy), [hd_r[d]], [stt_r])
                if l == 0 and c == 0: dbg("hf" + grp, hd0, hd_r[0], [128, T]); dbg("hb" + grp, hd1, hd_r[1], [128, T])
                kb.op("dve", lambda e: e.tensor_tensor(hd0, hd0, hd1, ALU.add), hd_r, [hd_r[0]])
                for h in range(2):
                    hs = slice(h * 512, (h + 1) * 512)
                    kb.op("dve", lambda e: e.tensor_tensor(mixT[:, 4 + c, hs], hd0[:, hs], gl[:, hs], ALU.mult), [hd_r[0], gl_r], [mix_r[4 + c][h]])

            chk("lru")
            o_ = WOFF
            hv, o_ = carve(o_, [128, 2, T], BF16); hx1, o_ = carve(o_, [128, 2, T], BF16); hx2, o_ = carve(o_, [128, 2, T], BF16)
            o_s2 = o_
            raws = []; cvts = []
            for i2 in range(2):
                r_, o_ = carve(o_, [128, T], F32); c_, o_ = carve(o_, [128, T], F32)
                raws.append((r_, Res("raw%d" % i2))); cvts.append((c_, Res("cvt%d" % i2)))
            hv_r = Res("hv"); hx1_r = Res("hx1"); hx2_r = Res("hx2")
            w_phase([hv_r, hx1_r, hx2_r] + [r for _, r in raws] + [r for _, r in cvts])
            v3 = lambda a: a.rearrange("p (s t) -> p s t", s=nseq)
            hcw = V(("hcw", l), 18)
            for i in range(6):
                raw, raw_r = raws[i % 2]; cvt, cvt_r = cvts[i % 2]
                dst, dst_r = ((hv, hv_r), (hx1, hx1_r), (hx2, hx2_r))[i // 2]
                outs = project(I["w_in_p"][l, 11 + i], 8, hT_rhs)
                for h, (bk, br) in enumerate(outs):
                    hs = slice(h * 512, (h + 1) * 512)
                    kb.op("act", lambda e: e.activation(out=raw[:, hs], in_=bk[:, :], func=AF.Copy), [br], [raw_r])
                    kb.op("dve", lambda e: e.tensor_scalar(out=cvt[:, hs], in0=raw[:, hs], scalar1=hcw[:, 6 + i:7 + i], scalar2=None, op0=ALU.mult), [raw_r, cres], [cvt_r])
                r3 = v3(raw); c3 = v3(cvt); d3 = v3(dst[:, i % 2, :])
                kb.op("dve", lambda e: e.scalar_tensor_tensor(out=c3[:, :, 1:L], in0=r3[:, :, 0:L - 1], scalar=hcw[:, i:i + 1], in1=c3[:, :, 1:L], op0=ALU.mult, op1=ALU.add), [raw_r, cvt_r, cres], [cvt_r])
                kb.op("dve", lambda e: e.scalar_tensor_tensor(out=c3[:, :, 0:L - 1], in0=r3[:, :, 1:L], scalar=hcw[:, 12 + i:13 + i], in1=c3[:, :, 0:L - 1], op0=ALU.mult, op1=ALU.add), [raw_r, cvt_r, cres], [cvt_r])
                kb.op("act", lambda e: e.activation(out=dst[:, i % 2, :], in_=cvt, func=AF.Copy), [cvt_r], [dst_r])
            raw_r = raws[0][1]; cvt_r = raws[1][1]; raw_r2 = cvts[0][1]; cvt_r2 = cvts[1][1]
            if l == 0: dbg("hv" + grp, hv[:, 0, :], hv_r, [128, T])
            tok, o_ = carve(o_s2, [128, nP, nseq * 256], BF16); Zr, o_ = carve(o_, [128, nP, nseq * 256], BF16); Zs, o_ = carve(o_, [128, nP, nseq * 256], BF16)
            tok_r = Res("tok"); Zr_r = Res("Zr"); Zs_r = Res("Zs")
            merge_into([tok_r, Zr_r, Zs_r], [[raw_r, cvt_r, raw_r2, cvt_r2]]); live["W"] = [hv_r, hx1_r, hx2_r, tok_r, Zr_r, Zs_r]
            ZN = st.setdefault("ZN", None)
            if ZN is None:
                ZN = st["sb"]("ZN", [1, 1024], BF16); st["ZN"] = ZN; st["ZN_r"] = Res("ZN")
            ZN_r = st["ZN_r"]
            NCOL = nseq * 256; GW = min(512, NCOL); NG = NCOL // GW
            TW = min(L, 512); NTH = L // TW

            PWE = cfg.get("pwe", "pool")

            def longconv(o, u, u_r, epilogue):
                for s in range(nseq):
                    for tc in range(nP):
                        bk, br = kb.bank("B")
                        bkb = bk[:, :].bitcast(BF16)
                        for cc in range(2):
                            kb.mm(bkb[:, cc * 128:(cc + 1) * 128], u[:, cc, s * L + tc * 128:s * L + (tc + 1) * 128], identb[:], True, True, [u_r] + CR, br, transpose=True)
                        kb.op("act", lambda e: e.activation(out=tok[:, tc, s * 256:(s + 1) * 256], in_=bkb[:, 0:256], func=AF.Copy), [br], [tok_r])
                for gi in range(NG):
                    gs = slice(gi * GW, (gi + 1) * GW)
                    un, unr = kb.bank("B")
                    for tc in range(nP):
                        kb.mm(un[0:1, 0:GW], altc[:, 0:1], tok[:, tc, gs], tc == 0, tc == nP - 1, [tok_r] + CR, unr)
                    for si in range(GW // 256):
                        zc = gi * GW + si * 256
                        kb.op("dve", lambda e: e.tensor_tensor(ZN[0:1, zc:zc + 256], un[0:1, si * 256:(si + 1) * 256], KN[0:1, o * 256:(o + 1) * 256], ALU.mult), [unr, KN_r], [ZN_r])
                for m in range(nP):
                    for gi in range(NG):
                        gs = slice(gi * GW, (gi + 1) * GW)
                        ur, urr = kb.bank("A"); us, usr = kb.bank("B")
                        for tc in range(nP):
                            kb.mm(ur[:, 0:GW], Cq[:, tc, m * 128:(m + 1) * 128], tok[:, tc, gs], tc == 0, tc == nP - 1, [tok_r] + CR, urr)
                        for tc in range(nP):
                            kb.mm(us[:, 0:GW], Sq[:, tc, m * 128:(m + 1) * 128], tok[:, tc, gs], tc == 0, tc == nP - 1, [tok_r] + CR, usr)
                        for si in range(GW // 256):
                            cs = slice(si * 256, (si + 1) * 256); zc = gi * GW + si * 256
                            A_ = At[:, m, o * 256:(o + 1) * 256]; B_ = Bt[:, m, o * 256:(o + 1) * 256]
                            t1, t1r = S(); t2, t2r = S(); t3, t3r = S(); t4, t4r = S()
                            kb.op("dve", lambda e: e.tensor_tensor(t1[:, 0:256], ur[:, cs], A_, ALU.mult), [urr, At_r], [t1r])
                            kb.op("dve", lambda e: e.tensor_tensor(t2[:, 0:256], us[:, cs], B_, ALU.mult), [usr, Bt_r], [t2r])
                            kb.op(PWE, lambda e: e.tensor_tensor(Zr[:, m, zc:zc + 256], t1[:, 0:256], t2[:, 0:256], ALU.add), [t1r, t2r], [Zr_r])
                            kb.op("dve", lambda e: e.tensor_tensor(t3[:, 0:256], us[:, cs], A_, ALU.mult), [usr, At_r], [t3r])
                            kb.op("dve", lambda e: e.tensor_tensor(t4[:, 0:256], ur[:, cs], B_, ALU.mult), [urr, Bt_r], [t4r])
                            kb.op(PWE, lambda e: e.tensor_tensor(Zs[:, m, zc:zc + 256], t3[:, 0:256], t4[:, 0:256], ALU.subtract), [t3r, t4r], [Zs_r])
                for s in range(nseq):
                    for cc in range(2):
                        zs_ = slice(s * 256 + cc * 128, s * 256 + (cc + 1) * 128)
                        for th in range(NTH):
                            bk, br = kb.bank("A")
                            for m in range(nP):
                                kb.mm(bk[:, 0:TW], Zr[:, m, zs_], Cq[:, m, th * TW:(th + 1) * TW], m == 0, False, [Zr_r] + CR, br)
                            for m in range(nP):
                                kb.mm(bk[:, 0:TW], Zs[:, m, zs_], Sq[:, m, th * TW:(th + 1) * TW], False, False, [Zs_r] + CR, br)
                            kb.mm(bk[:, 0:TW], ZN[0:1, zs_], altr[0:1, th * TW:(th + 1) * TW], False, True, [ZN_r] + CR, br)
                            epilogue(cc, slice(s * L + th * TW, s * L + (th + 1) * TW), bk[:, 0:TW], br)

            hsk = V(("hsk", l), 4)

            def ep1(cc, ts, ps, br):
                t1, t1r = S()
                kb.op("dve", lambda e: e.scalar_tensor_tensor(out=t1[:, 0:TW], in0=hv[:, cc, ts], scalar=hsk[:, cc:cc + 1], in1=ps, op0=ALU.mult, op1=ALU.add), [hv_r, br] + CR, [t1r])
                kb.op("dve", lambda e: e.tensor_tensor(hx1[:, cc, ts], t1[:, 0:TW], hx1[:, cc, ts], ALU.mult), [t1r, hx1_r], [hx1_r])

            def ep2(cc, ts, ps, br):
                t1, t1r = S()
                kb.op("dve", lambda e: e.scalar_tensor_tensor(out=t1[:, 0:TW], in0=hx1[:, cc, ts], scalar=hsk[:, 2 + cc:3 + cc], in1=ps, op0=ALU.mult, op1=ALU.add), [hx1_r, br] + CR, [t1r])
                h_ = ts.start // 512
                kb.op("dve", lambda e: e.tensor_tensor(mixT[:, 6 + cc, ts], t1[:, 0:TW], hx2[:, cc, ts], ALU.mult), [t1r, hx2_r], [mix_r[6 + cc][h_]])

            longconv(0, hv, hv_r, ep1)
            if l == 0: dbg("z" + grp, hx1[:, 0, :], hx1_r, [128, T])
            longconv(1, hx1, hx1_r, ep2)
            if l == 0:
                for cdbg in range(8):
                    dbg("mix%d" % cdbg + grp, mixT[:, cdbg, :], mix_r[cdbg][1], [128, T])

            chk("hy")
            st["run_pending_mod"](100)
            for jo in range(8):
                outs = project(I["w_out_p"][l, jo], 8, mix_rhs)
                for h, (bk, br) in enumerate(outs):
                    hs = slice(h * 512, (h + 1) * 512)
                    kb.op("dve", lambda e: e.scalar_tensor_tensor(out=x[:, jo, hs], in0=bk[:, :], scalar=modv[:, l, 16 + jo, j:j + 1], in1=x[:, jo, hs], op0=ALU.mult, op1=ALU.add), [br, x_r[jo][h], cres, st["mod_r"][l]], [x_r[jo][h]])
            if l == 0: dbg("x1" + grp, x[:, 0, :], x_r[0][1], [128, T])

            chk("wout")
            st["norm_mod"](l, j, 1)
            G = st["ovl"][:, 0:44 * 1024].bitcast(BF16).rearrange("p (f t) -> p f t", f=NF)
            G_r = [[Res("G") for h in range(2)] for f in range(NF)]
            allG = [r for f in G_r for r in f]
            merge_into(allG, [live["W"], live["M"], live["F"]]); live["G"] = allG; live["W"] = []; live["F"] = []
            modq = list(range(48)) if (P and l + 1 < NLAY) or (not cfg["prompt"] and not P and l + 1 < NLAY) else []
            for f in range(NF):
                for _ in range(2):
                    if modq: st["mod_piece"](l + 1, modq.pop(0))
                o1 = project(I["w1_p"][l, f], 8, hT_rhs, pool="A")
                o3 = project(I["w3_p"][l, f], 8, hT_rhs, pool="B")
                for h in range(2):
                    sg, sgr = S()
                    kb.op("act", lambda e: e.activation(out=sg[:], in_=o1[h][0][:, :], func=AF.Silu), [o1[h][1]], [sgr])
                    kb.op("dve", lambda e: e.tensor_tensor(G[:, f, h * 512:(h + 1) * 512], sg[:], o3[h][0][:, :], ALU.mult), [sgr, o3[h][1]], [G_r[f][h]])
            for jo in range(8):
                if modq:
                    st["mod_piece"](l + 1, modq.pop(0))
                    if not modq: st["mod_finish"](l + 1)
                wa, war = ws.load(I["w2_p"][l, jo][:, 0:11, :], 11)
                wb, wbr = ws.load(I["w2_p"][l, jo][:, 11:22, :], 11)
                for h in range(2):
                    hs = slice(h * 512, (h + 1) * 512)
                    bk, br = kb.bank("A")
                    for k in range(NF):
                        wv, wr = (wa, war) if k < 11 else (wb, wbr)
                        kb.mm(bk[:, :], wv[:, k % 11, :], G[:, k, hs], k == 0, k == NF - 1, [wr, G_r[k][h]], br)
                    kb.op("dve", lambda e: e.scalar_tensor_tensor(out=x[:, jo, hs], in0=bk[:, :], scalar=modv[:, l, 40 + jo, j:j + 1], in1=x[:, jo, hs], op0=ALU.mult, op1=ALU.add), [br, x_r[jo][h], cres, st["mod_r"][l]], [x_r[jo][h]])
        yo = O["yp"] if P else O["ys"]
        for c in range(8):
            kb.dma("sp", yo[:, c, :], x[:, c, :], [x_r[c][0], x_r[c][1]], [], "oy%d" % c)
            if "oy%d" % c not in kb.outsems: kb.outsems.append("oy%d" % c)

    try:
        chk("mod")
        if cfg["prompt"]:
            run_pass("P")
            kb.dma("sp", O["nst"], stt[:], [stt_r], [], "onst"); kb.outsems.append("onst")
        if cfg["sample"]:
            run_pass("S")
    except _Stop:
        pass
    for sk, v in kb.cnt.items():
        if v > 0:
            nc.sync.wait_ge(kb.semobj[sk], v)


_CACHE = {}


def _get_program(cfg):
    key = (cfg["layers"], cfg["prompt"], cfg["sample"], tuple(cfg.get("dbg", [])), cfg.get("stop"))
    if key not in _CACHE:
        st = build_program(cfg)
        emit_passes(st)
        _CACHE[key] = st
    return _CACHE[key]


def kernel(**inputs):
    cfg = CFG
    st = _get_program(cfg)
    nc = st["nc"]
    shared = _host_shared(inputs)
    in_maps = []
    for c in range(8):
        m = dict(shared); m.update(_host_core(inputs, c)); in_maps.append(m)
    res = run_bass_kernel_spmd(nc, in_maps, core_ids=list(range(8)))
    R = res.results
    kernel.last = R
    f32 = np.float32
    y_p = np.zeros((32, 256, D), f32); y_s = np.zeros((2, T, D), f32)
    nk = np.zeros((32, NL, 256, 2, 64), f32); nv = np.zeros((32, NL, 256, 2, 64), f32); nst = np.zeros((32, NL, 2, 256), f32)
    for c in range(8):
        r = R[c]
        y_p[4 * c:4 * c + 4] = r["yp"].transpose(2, 1, 0).reshape(4, 256, D)
        if c < 2:
            y_s[c] = r["ys"].transpose(2, 1, 0).reshape(T, D)
        k = r["nk"].reshape(NL, 2, 64, 4, 256)
        nk[4 * c:4 * c + 4] = k.transpose(3, 0, 4, 1, 2)
        v = r["nv"].reshape(NL, 2, 64, 4, 256)
        nv[4 * c:4 * c + 4] = v.transpose(3, 0, 4, 1, 2)
        s_ = r["nst"].reshape(128, NL, 2, 2, 4)
        nst[4 * c:4 * c + 4] = s_.transpose(4, 1, 2, 3, 0).reshape(4, NL, 2, 256)
    return (y_p, y_s, nk, nv, nst)
```

```python
import math
from contextlib import ExitStack
import numpy as np
import concourse.bass as bass
import concourse.mybir as mybir
from concourse.bass_utils import run_bass_kernel_spmd

F32 = mybir.dt.float32
BF16 = mybir.dt.bfloat16
AF = mybir.ActivationFunctionType
ALU = mybir.AluOpType

D = 1024; NL = 4; T = 1024; HD = 64
OFF_K = 512; OFF_V = 640; OFF_LX = 768; OFF_LG = 1024; OFF_HY = 1280
DFF = 2816; NF = 22
EPS = 1e-6
PI = math.pi

CFG = {"layers": NL, "prompt": True, "sample": True, "dbg": []}

VCOLS = {}
def _mk_vcols():
    o = 0
    def add(n, c):
        nonlocal o
        VCOLS[n] = o; o += c
    for l in range(NL):
        add(("n1", l), 8); add(("n2", l), 8); add(("qn", l), 1); add(("kn", l), 1)
        add(("lcw", l), 8)
        add(("lcb", l), 2)
        add(("lgb", l), 8)
        add(("lam", l), 4)
        add(("hcw", l), 18)
        add(("hsk", l), 4)
        add(("fb1", l), 1); add(("fb2", l), 1)
    for l in range(NL):
        add(("bmod", l), 48)
    add("cw256", 2); add("cw1024", 8); add("altc", 1)
    return o
NV = _mk_vcols()
PV_COND = 0; PV_H0 = 16; NPV = 32


def _fm(v, nchunk):
    return np.ascontiguousarray(np.asarray(v, np.float32).reshape(nchunk, 128).T)


def _host_consts():
    f32 = np.float32
    c = {}
    c["ident"] = np.eye(128, dtype=f32)
    R = np.zeros((128, 128), f32)
    for m in range(128):
        if m % 32 < 16: R[m + 16, m] = -1.0
        else: R[m - 16, m] = 1.0
    c["rrot"] = R
    t = np.arange(T)
    row = (t // 64).astype(f32); col = (t % 64).astype(f32)
    freqs = (f32(10000.0) ** (-(np.arange(16, dtype=f32)) / f32(16))).astype(f32)
    rc = np.zeros((128, T), f32); rs = np.zeros((128, T), f32)
    for p in range(128):
        d = p % 64
        pos = row if d < 32 else col
        ang = (pos * freqs[d % 16]).astype(f32)
        rc[p] = np.cos(ang); rs[p] = np.sin(ang)
    c["ropec"] = rc; c["ropes"] = rs
    for L in (256, 1024):
        tt = np.arange(L, dtype=f32); tn = (tt / f32(L)).astype(f32)
        bands = np.linspace(1e-4, 15, 16, dtype=f32)
        ang = (f32(2.0 * math.pi / L) * tt[:, None] * bands[None, :]).astype(f32)
        z = np.concatenate([tn[:, None], np.cos(ang), -np.sin(ang)], axis=-1).astype(f32)
        c["zT%d" % L] = np.ascontiguousarray(z.T)
        deltas = np.linspace(math.log(1e-2) / 1.5, math.log(1e-2) / 0.3, 256, dtype=f32)
        dec = np.exp(-tn[:, None] * np.abs(deltas)[None, :]).astype(f32)
        decb = dec.copy(); decb[0] = 0.0
        dd = np.stack([dec, decb], axis=1)
        c["dec%d" % L] = np.ascontiguousarray(dd.reshape(L // 128, 128, 2, 256).transpose(1, 0, 2, 3))
        k = np.arange(L, dtype=np.float64)
        th = math.pi / L * np.outer(k, k)
        C = np.cos(th).astype(f32); S = np.sin(th).astype(f32)
        c["dftc%d" % L] = np.ascontiguousarray(C.reshape(L // 128, 128, L).transpose(1, 0, 2))
        c["dfts%d" % L] = np.ascontiguousarray(S.reshape(L // 128, 128, L).transpose(1, 0, 2))
    c["altr"] = ((-1.0) ** np.arange(T)).astype(f32)[None, :]
    c["altcin"] = ((-1.0) ** np.arange(128)).astype(f32)[:, None]
    return c


def _host_shared(inp):
    f32 = np.float32
    g = lambda k: np.asarray(inp[k], f32)
    sh = _host_consts()
    vec = np.zeros((128, NV), f32)
    def put(key, arr):
        vec[:, VCOLS[key]:VCOLS[key] + arr.shape[1]] = arr
    for l in range(NL):
        put(("n1", l), _fm(g("norm1_w")[l], 8)); put(("n2", l), _fm(g("norm2_w")[l], 8))
        put(("qn", l), np.tile(g("q_norm_w")[l], 2)[:, None]); put(("kn", l), np.tile(g("k_norm_w")[l], 2)[:, None])
        put(("lcw", l), np.concatenate([_fm(g("lru_conv_w")[l, k], 2) for k in range(4)], axis=1))
        put(("lcb", l), _fm(g("lru_conv_b")[l], 2))
        put(("lgb", l), np.concatenate([_fm(g("lru_gate_b")[l, d, gt], 2) for d in range(2) for gt in range(2)], axis=1))
        put(("lam", l), np.concatenate([_fm(g("lru_lambda")[l, d], 2) for d in range(2)], axis=1))
        put(("hcw", l), np.concatenate([_fm(g("hy_conv_w")[l, k], 6) for k in range(3)], axis=1))
        put(("hsk", l), np.concatenate([_fm(g("hy_skip")[l, o], 2) for o in range(2)], axis=1))
        b1 = np.zeros((128, 1), f32); b1[:64, 0] = g("hy_filt_b1")[l]; put(("fb1", l), b1)
        b2 = np.zeros((128, 1), f32); b2[:64, 0] = g("hy_filt_b2")[l]; put(("fb2", l), b2)
    for L, key in ((256, "cw256"), (1024, "cw1024")):
        cw = np.full(L, 1.0 / L, f32); cw[0] = 0.5 / L
        put(key, _fm(cw, L // 128))
    put("altc", (((-1.0) ** np.arange(128)).astype(f32))[:, None])
    sh["vec"] = vec
    for l in range(NL):
        put(("bmod", l), _fm(g("b_mod")[l], 48))
    cols = []
    cols += list(range(OFF_K, OFF_K + 64)) * 2
    cols += list(range(OFF_K + 64, OFF_K + 128)) * 2
    cols += list(range(OFF_V, OFF_V + 128))
    cols += list(range(0, 512))
    cols += list(range(OFF_LX, OFF_LX + 128)) + list(range(OFF_LG, OFF_LG + 128))
    cols += list(range(OFF_LX + 128, OFF_LX + 256)) + list(range(OFF_LG + 128, OFF_LG + 256))
    cols += list(range(OFF_HY, OFF_HY + 768))
    cols = np.array(cols)
    def pieces(W, nk, nch):
        return np.ascontiguousarray(W.reshape(NL, nk, 128, nch, 128).transpose(0, 3, 2, 1, 4))
    sh["w_in_p"] = pieces(g("w_in")[:, :, cols], 8, 17)
    sh["w_out_p"] = pieces(g("w_out"), 8, 8)
    sh["w1_p"] = pieces(g("ffn_w1"), 8, NF)
    sh["w3_p"] = pieces(g("ffn_w3"), 8, NF)
    sh["w2_p"] = pieces(g("ffn_w2"), NF, 8)
    sh["w_mod_p"] = pieces(g("w_mod"), 8, 48)
    gw = np.zeros((NL, 128, 8, 128), f32)
    G = g("lru_gate_w")
    for l in range(NL):
        for d in range(2):
            for gt in range(2):
                for c in range(2):
                    idx = (d * 2 + gt) * 2 + c
                    for h in range(2):
                        gw[l, h * 64:(h + 1) * 64, idx, h * 64:(h + 1) * 64] = G[l, d, gt, 2 * c + h]
    sh["gw"] = gw
    sh["hf_w1"] = np.ascontiguousarray(g("hy_filt_w1"))
    sh["hf_w2"] = np.ascontiguousarray(g("hy_filt_w2"))
    sh["hf_w3"] = np.ascontiguousarray(np.concatenate([g("hy_filt_w3"), g("hy_filt_b3")[:, None, :]], axis=1))
    return sh


def _host_core(inp, c):
    f32 = np.float32
    m = {}
    xp = np.asarray(inp["x_prompt"], f32)[4 * c:4 * c + 4].reshape(T, 8, 128)
    m["xp"] = np.ascontiguousarray(xp.transpose(2, 1, 0))
    b = c % 2
    xs = np.asarray(inp["x_sample"], f32)[b].reshape(T, 8, 128)
    m["xs"] = np.ascontiguousarray(xs.transpose(2, 1, 0))
    ck = np.asarray(inp["cache_k"], f32)[b]
    ckT = np.zeros((NL, 2, 128, 256), f32)
    for g_ in range(2):
        kt = ck[:, :, g_, :].transpose(0, 2, 1)
        ckT[:, g_, :64] = kt; ckT[:, g_, 64:] = kt
    m["ckT"] = ckT
    cv = np.asarray(inp["cache_v"], f32)[b].reshape(NL, 2, 128, 128)
    m["cv"] = np.ascontiguousarray(cv.transpose(0, 2, 1, 3))
    pv = np.zeros((128, NPV), f32)
    cc = _fm(np.asarray(inp["c_ctx"], f32), 8); cb = _fm(np.asarray(inp["c"], f32)[b], 8)
    for k in range(8):
        pv[:, PV_COND + 2 * k] = cc[:, k]; pv[:, PV_COND + 2 * k + 1] = cb[:, k]
    st = np.asarray(inp["state_lru"], f32)[b]
    for l in range(NL):
        for d in range(2):
            pv[:, PV_H0 + (l * 2 + d) * 2:PV_H0 + (l * 2 + d) * 2 + 2] = _fm(st[l, d], 2)
    m["pvec"] = pv
    return m


class Res:
    __slots__ = ("w", "r", "name")
    def __init__(self, name=""):
        self.w = None; self.r = {}; self.name = name


class KB:
    def __init__(self, nc, es):
        self.nc = nc; self.es = es
        self.E = {"pe": nc.tensor, "dve": nc.vector, "act": nc.scalar, "pool": nc.gpsimd, "sp": nc.sync}
        self.semobj = {}
        self.cnt = {}
        for k in ("pe", "dve", "act", "pool"):
            self.semobj[k] = nc.alloc_semaphore(name="sem_" + k); self.cnt[k] = 0
        self.waited = {k: {} for k in self.E}
        self.banks = []; self.bres = []
        for i in range(8):
            self.banks.append(es.enter_context(nc.psum_tensor("bank%d" % i, [128, 512], F32)))
            self.bres.append(Res("bank%d" % i))
        self.bptr = {"A": 0, "B": 0}
        self.nsb = 0
        self.outsems = []

    def sb(self, name, shape, dt=F32):
        t = self.es.enter_context(self.nc.sbuf_tensor("s_" + name, list(shape), dt))
        return t

    def bank(self, pool):
        i = self.bptr[pool]; self.bptr[pool] = (i + 1) % 4
        j = i if pool == "A" else 4 + i
        return self.banks[j], self.bres[j]

    def deps_of(self, reads, writes):
        d = {}
        for r in reads:
            if r.w is not None:
                sk, v = r.w; d[sk] = max(d.get(sk, 0), v)
        for w in writes:
            if w.w is not None:
                sk, v = w.w; d[sk] = max(d.get(sk, 0), v)
            for sk, v in w.r.items(): d[sk] = max(d.get(sk, 0), v)
        return list(d.items())

    def _wait(self, e, deps):
        for sk, v in deps:
            if self.waited[e].get(sk, 0) >= v: continue
            self.E[e].wait_ge(self.semobj[sk], v)
            self.waited[e][sk] = v

    def op(self, e, fn, reads=(), writes=()):
        self._wait(e, self.deps_of(reads, writes))
        ins = fn(self.E[e])
        self.cnt[e] += 1
        ins.then_inc(self.semobj[e], 1)
        for r in reads: r.r[e] = self.cnt[e]
        for w in writes: w.w = (e, self.cnt[e]); w.r = {}
        return ins

    def mm(self, out, lhsT, rhs, start, stop, reads, bank, transpose=False):
        deps = [(s, v) for s, v in self.deps_of(reads, [bank]) if s != "pe"]
        self._wait("pe", deps)
        if transpose:
            ins = self.nc.tensor.transpose(out, lhsT, rhs)
        else:
            ins = self.nc.tensor.matmul(out, lhsT, rhs, start=start, stop=stop)
        self.cnt["pe"] += 1
        ins.then_inc(self.semobj["pe"], 1)
        for r in reads: r.r["pe"] = self.cnt["pe"]
        if stop:
            bank.w = ("pe", self.cnt["pe"]); bank.r = {}
        return ins

    def dma(self, q, out, in_, reads, writes, semkey):
        if semkey not in self.semobj:
            self.semobj[semkey] = self.nc.alloc_semaphore(name="dsem_" + semkey); self.cnt[semkey] = 0
        self._wait(q, self.deps_of(reads, writes))
        ins = self.E[q].dma_start(out=out, in_=in_)
        self.cnt[semkey] += 16
        ins.then_inc(self.semobj[semkey], 16)
        v = self.cnt[semkey]
        for r in reads: r.r[semkey] = v
        for w in writes: w.w = (semkey, v); w.r = {}
        return ins


class WStream:
    def __init__(self, kb, nslots=6, width=11 * 128):
        self.kb = kb; self.n = nslots; self.i = 0
        self.slots = [kb.sb("wslot%d" % i, [128, width], BF16) for i in range(nslots)]
        self.res = [Res("wslot%d" % i) for i in range(nslots)]

    def load(self, dram_piece, nk):
        i = self.i; self.i = (i + 1) % self.n
        s = self.slots[i]
        v = s[:, 0:nk * 128].rearrange("p (k n) -> p k n", k=nk)
        self.kb.dma("pool", v, dram_piece, [], [self.res[i]], "w%d" % i)
        return v, self.res[i]


def build_program(cfg):
    nc = bass.Bass("TRN2", target_bir_lowering=False)
    es = ExitStack()
    kb = KB(nc, es)
    NLAY = cfg["layers"]
    dbg_names = cfg.get("dbg", [])

    def din(name, shape, dt=F32):
        return nc.dram_tensor(name, list(shape), dt, kind="ExternalInput").ap()

    def dout(name, shape, dt=F32):
        return nc.dram_tensor(name, list(shape), dt, kind="ExternalOutput").ap()

    I = {}
    I["xp"] = din("xp", [128, 8, T]); I["xs"] = din("xs", [128, 8, T])
    I["ckT"] = din("ckT", [NL, 2, 128, 256]); I["cv"] = din("cv", [NL, 128, 2, 128])
    I["pvec"] = din("pvec", [128, NPV]); I["vec"] = din("vec", [128, NV])
    NLD = cfg.get("nld", NL)
    I["w_in_p"] = din("w_in_p", [NLD, 17, 128, 8, 128]); I["w_out_p"] = din("w_out_p", [NLD, 8, 128, 8, 128])
    I["w1_p"] = din("w1_p", [NLD, NF, 128, 8, 128]); I["w3_p"] = din("w3_p", [NLD, NF, 128, 8, 128])
    I["w2_p"] = din("w2_p", [NLD, 8, 128, NF, 128]); I["w_mod_p"] = din("w_mod_p", [NLD, cfg.get("nf", 48), 128, 8, 128])
    I["gw"] = din("gw", [NL, 128, 8, 128])
    I["hf_w1"] = din("hf_w1", [NL, 33, 64]); I["hf_w2"] = din("hf_w2", [NL, 64, 64]); I["hf_w3"] = din("hf_w3", [NL, 65, 1024])
    I["ident"] = din("ident", [128, 128]); I["rrot"] = din("rrot", [128, 128])
    I["ropec"] = din("ropec", [128, T]); I["ropes"] = din("ropes", [128, T])
    for L in (256, 1024):
        I["zT%d" % L] = din("zT%d" % L, [33, L]); I["dec%d" % L] = din("dec%d" % L, [128, L // 128, 2, 256])
        I["dftc%d" % L] = din("dftc%d" % L, [128, L // 128, L]); I["dfts%d" % L] = din("dfts%d" % L, [128, L // 128, L])
    I["altr"] = din("altr", [1, T]); I["altcin"] = din("altcin", [128, 1])
    O = {}
    O["yp"] = dout("yp", [128, 8, T]); O["ys"] = dout("ys", [128, 8, T])
    O["nk"] = dout("nk", [NL, 128, T]); O["nv"] = dout("nv", [NL, 128, T]); O["nst"] = dout("nst", [128, 64])
    DBG = {}

    def dbg(name, ap, res, shape):
        if name not in dbg_names: return
        o = dout("dbg_" + name, shape, ap.dtype)
        kb.dma("sp", o, ap, [res], [], "dbg_" + name)
        kb.outsems.append("dbg_" + name)

    sb = kb.sb
    x = sb("x", [128, 8, T]); x_r = [[Res("x%d_%d" % (c, h)) for h in range(2)] for c in range(8)]
    hT = sb("hT", [128, 8, T], BF16); hT_r = [[Res() for h in range(2)] for c in range(8)]
    mix_r = [[Res() for h in range(2)] for c in range(8)]
    ovl = sb("ovl", [128, 60 * 1024], mybir.dt.uint8)
    mixT = ovl[:, 0:16 * 1024].bitcast(BF16).rearrange("p (c t) -> p c t", c=8)
    vec = sb("vec", [128, NV]); pvec = sb("pvec", [128, NPV]); cres = Res("consts")
    ident = sb("ident", [128, 128]); rrot = sb("rrot", [128, 128])
    identb = sb("identb", [128, 128], BF16)
    ropec = sb("ropec", [128, T]); ropes = sb("ropes", [128, T])
    onesb = sb("onesb", [128, 128], BF16); onesbd = sb("onesbd", [128, 128], BF16)
    altc = sb("altcb", [128, 1], BF16); altr = sb("altrb", [1, T], BF16)
    dft = {256: (sb("dc256", [128, 2, 256], BF16), sb("ds256", [128, 2, 256], BF16)),
           1024: (sb("dc1024", [128, 8, 1024], BF16), sb("ds1024", [128, 8, 1024], BF16))}
    modv = sb("modv", [128, NL, 48, 2])
    modA = sb("modA", [128, NL, 2, 2, 8])
    sT = sb("sT", [128, 8, 2], BF16)
    stt = sb("stt", [128, 64])
    stt_r = Res("stt")
    gwb = sb("gwb", [128, 8, 128], BF16); gwb_r = Res("gwb")
    fw1 = sb("fw1", [33, 64]); fw2 = sb("fw2", [64, 64]); fw_r = Res("fw")
    rstd_t = [sb("rstd%d" % i, [128, 512]) for i in range(2)]; rstd_r = [Res("rstd%d" % i) for i in range(2)]; rstd_p = [0]
    dect = [sb("dect%d" % i, [128, 2, 256]) for i in range(2)]; dect_r = [Res("dect%d" % i) for i in range(2)]
    KN = sb("KN", [1, 512]); KN_r = Res("KN")
    cneg = sb("cneg", [128, 4]); cneg_r = Res("cneg")
    scr = [sb("scr%d" % i, [128, 512]) for i in range(6)]; scr_r = [Res("scr%d" % i) for i in range(6)]
    scb = [sb("scb%d" % i, [128, 512], BF16) for i in range(4)]; scb_r = [Res("scb%d" % i) for i in range(4)]
    scrp = [0]; scbp = [0]

    def S():
        i = scrp[0]; scrp[0] = (i + 1) % len(scr); return scr[i], scr_r[i]

    def SBF():
        i = scbp[0]; scbp[0] = (i + 1) % len(scb); return scb[i], scb_r[i]

    ws = WStream(kb)

    def carve(off, shape, dt):
        esz = 4 if dt == F32 else 2
        n = int(np.prod(shape[1:])) * esz
        v = ovl[:, off:off + n].bitcast(dt)
        if len(shape) == 3:
            v = v.rearrange("p (a b) -> p a b", a=shape[1])
        elif len(shape) == 4:
            v = v.rearrange("p (a b c) -> p a b c", a=shape[1], b=shape[2])
        return v, off + n

    V = lambda key, n=1: vec[:, VCOLS[key]:VCOLS[key] + n]

    def cload(q, dst, src):
        kb.dma(q, dst, src, [], [cres], "c0")
    cload("sp", vec[:], I["vec"]); cload("sp", pvec[:], I["pvec"])
    cload("sp", ident[:], I["ident"]); cload("sp", rrot[:], I["rrot"])
    cload("sp", ropec[:], I["ropec"]); cload("sp", ropes[:], I["ropes"])
    kb.dma("pool", identb[:], I["ident"], [], [cres], "c1")
    kb.dma("pool", altr[:], I["altr"], [], [cres], "c1")
    kb.dma("pool", dft[256][0][:], I["dftc256"], [], [cres], "c1")
    kb.dma("pool", dft[256][1][:], I["dfts256"], [], [cres], "c1")
    kb.dma("pool", altc[:], I["altcin"], [], [cres], "c1")
    cres2 = Res("c0all"); cres2.w = ("c0", kb.cnt["c0"])
    cres3 = Res("c1all"); cres3.w = ("c1", kb.cnt["c1"])
    CR = [cres2, cres3]
    kb.op("dve", lambda e: e.memset(onesb[:], 1.0), [], [cres])
    kb.op("dve", lambda e: e.memset(onesbd[:], 0.0), [], [cres])
    kb.op("dve", lambda e: e.memset(onesbd[0:64, 0:64], 1.0), [], [cres])
    kb.op("dve", lambda e: e.memset(onesbd[64:128, 64:128], 1.0), [], [cres])
    kb.op("dve", lambda e: e.memset(stt[:], 0.0), [], [stt_r])
    CR.append(cres)

    nmod = NLAY if cfg.get("stop") not in ("const",) else 0
    kb.op("act", lambda e: e.activation(out=sT[:].rearrange("p k j -> p (k j)"), in_=pvec[:, PV_COND:PV_COND + 16], func=AF.Silu), CR, [cres])
    mod_r = [Res("mod%d" % l) for l in range(NL)]

    def mod_piece(l, f):
        wv, wr = ws.load(I["w_mod_p"][l, f], 8)
        bk, br = kb.bank("A")
        for k in range(8):
            kb.mm(bk[:, 0:2], wv[:, k, :], sT[:, k, :], k == 0, k == 7, [wr, cres], br)
        kb.op("dve", lambda e: e.tensor_scalar(out=modv[:, l, f, :], in0=bk[:, 0:2], scalar1=V(("bmod", l), 48)[:, f:f + 1], scalar2=None, op0=ALU.add), [br] + CR, [mod_r[l]])

    def mod_finish(l, whs=(0, 1)):
        for j in range(2):
            for wh in whs:
                sc = modv[:, l, (8 if wh == 0 else 32):(16 if wh == 0 else 40), j]
                nw = V(("n1" if wh == 0 else "n2", l), 8)
                kb.op("dve", lambda e: e.scalar_tensor_tensor(out=modA[:, l, j, wh, :], in0=sc, scalar=1.0, in1=nw, op0=ALU.add, op1=ALU.mult), CR + [mod_r[l]], [mod_r[l]])

    pending_mod = []
    if nmod > 0:
        nf0 = cfg.get("nf", 48)
        for f in range(min(16, nf0)):
            mod_piece(0, f)
        mod_finish(0, (0,))
        pending_mod.extend(("p", 0, f) for f in range(16, nf0))
        pending_mod.append(("f", 0, (1,)))

    def run_pending_mod(n):
        for _ in range(n):
            if not pending_mod: return
            it = pending_mod.pop(0)
            if it[0] == "p": mod_piece(it[1], it[2])
            else: mod_finish(it[1], it[2])
    dbg("modv", modv[:].rearrange("p l f j -> p (l f j)"), mod_r[0], [128, NL * 96])

    def rms_bc(src_fn, src_res, nchunk, ones_t, inv_n, half_w=512):
        bk, br = kb.bank("B")
        for c in range(nchunk):
            sq, sqr = SBF()
            a, ar = src_fn(c)
            if nchunk > 1 and c % 2 == 0 and cfg.get("poolsq", True):
                kb.op("pool", lambda e: e.tensor_tensor(sq[:, 0:half_w], a, a, ALU.mult), [ar], [sqr])
            else:
                kb.op("act", lambda e: e.activation(out=sq[:, 0:half_w], in_=a, func=AF.Square), [ar], [sqr])
            kb.mm(bk[:, 0:half_w], ones_t[:], sq[:, 0:half_w], c == 0, c == nchunk - 1, [sqr, cres], br)
        i_ = rstd_p[0]; rstd_p[0] = 1 - i_
        s1, s1r = rstd_t[i_], rstd_r[i_]
        kb.op("act", lambda e: e.activation(out=s1[:, 0:half_w], in_=bk[:, 0:half_w], func=AF.Sqrt, bias=EPS, scale=inv_n), [br], [s1r])
        kb.op("dve", lambda e: e.reciprocal(s1[:, 0:half_w], s1[:, 0:half_w]), [s1r], [s1r])
        return s1, s1r

    def norm_mod(l, j, wh):
        shoff = 0 if wh == 0 else 24
        bks = [kb.bank("B"), kb.bank("B")]
        for c in range(8):
            for h in range(2):
                hs = slice(h * 512, (h + 1) * 512)
                sq, sqr = SBF()
                if (c + h) % 2 == 0:
                    kb.op("pool", lambda e: e.tensor_tensor(sq[:], x[:, c, hs], x[:, c, hs], ALU.mult), [x_r[c][h]], [sqr])
                else:
                    kb.op("act", lambda e: e.activation(out=sq[:], in_=x[:, c, hs], func=AF.Square), [x_r[c][h]], [sqr])
                kb.mm(bks[h][0][:, :], onesb[:], sq[:], c == 0, c == 7, [sqr, cres], bks[h][1])
        for h in range(2):
            kb.op("act", lambda e: e.activation(out=rstd_t[h][:], in_=bks[h][0][:, :], func=AF.Ln, bias=EPS, scale=1.0 / D), [bks[h][1]], [rstd_r[h]])
        for h in range(2):
            kb.op("act", lambda e: e.activation(out=rstd_t[h][:], in_=rstd_t[h][:], func=AF.Exp, scale=-0.5), [rstd_r[h]], [rstd_r[h]])
        for c in range(8):
            for h in range(2):
                hs = slice(h * 512, (h + 1) * 512)
                t1, t1r = S()
                kb.op("dve", lambda e: e.scalar_tensor_tensor(out=t1[:], in0=x[:, c, hs], scalar=modA[:, l, j, wh, c:c + 1], in1=rstd_t[h][:], op0=ALU.mult, op1=ALU.mult), [x_r[c][h], rstd_r[h], cres, mod_r[l]], [t1r])
                kb.op("act", lambda e: e.activation(out=hT[:, c, hs], in_=t1[:], func=AF.Identity, bias=modv[:, l, shoff + c, j:j + 1], scale=1.0), [t1r, cres, mod_r[l]], [hT_r[c][h]])

    def project(piece, nk, rhs_fn, pool="A"):
        wv, wr = ws.load(piece, nk)
        outs = []
        for h in range(2):
            bk, br = kb.bank(pool)
            for k in range(nk):
                a, ar = rhs_fn(k, h)
                kb.mm(bk[:, :], wv[:, k, :], a, k == 0, k == nk - 1, [wr, ar], br)
            outs.append((bk, br))
        run_pending_mod(2)
        return outs

    hT_rhs = lambda k, h: (hT[:, k, h * 512:(h + 1) * 512], hT_r[k][h])
    mix_rhs = lambda k, h: (mixT[:, k, h * 512:(h + 1) * 512], mix_r[k][h])

    def headnorm(l, bk, br, wkey):
        rstd, rr = rms_bc(lambda c: (bk[:, :], br), None, 1, onesbd, 1.0 / HD)
        o, o_r = S()
        kb.op("dve", lambda e: e.scalar_tensor_tensor(out=o[:], in0=bk[:, :], scalar=V((wkey, l)), in1=rstd[:], op0=ALU.mult, op1=ALU.mult), [br, rr, cres], [o_r])
        return o, o_r

    def rope(src, src_r, h, dst, dst_r):
        hs = slice(h * 512, (h + 1) * 512)
        bk, br = kb.bank("B")
        kb.mm(bk[:, :], rrot[:], src[:], True, True, [src_r, cres], br)
        t1, t1r = S()
        kb.op("dve", lambda e: e.tensor_tensor(t1[:], bk[:, :], ropes[:, hs], ALU.mult), [br, cres], [t1r])
        t2, t2r = S()
        kb.op("dve", lambda e: e.tensor_tensor(t2[:], src[:], ropec[:, hs], ALU.mult), [src_r, cres], [t2r])
        kb.op("dve", lambda e: e.tensor_tensor(dst, t1[:], t2[:], ALU.add), [t1r, t2r], [dst_r])

    def qk_prep(l, outs, wkey, dsts, dst_r, do_rope, extra=None):
        sq = []
        for h in range(2):
            t_, r_ = SBF(); sq.append((t_, r_))
            kb.op("act", lambda e: e.activation(out=t_[:], in_=outs[h][0][:, :], func=AF.Square), [outs[h][1]], [r_])
        rb = []
        for h in range(2):
            b_, br_ = kb.bank("B"); rb.append((b_, br_))
            kb.mm(b_[:, :], onesbd[:], sq[h][0][:], True, True, [sq[h][1], cres], br_)
        for h in range(2):
            kb.op("act", lambda e: e.activation(out=rstd_t[h][:], in_=rb[h][0][:, :], func=AF.Ln, bias=EPS, scale=1.0 / HD), [rb[h][1]], [rstd_r[h]])
        for h in range(2):
            kb.op("act", lambda e: e.activation(out=rstd_t[h][:], in_=rstd_t[h][:], func=AF.Exp, scale=-0.5), [rstd_r[h]], [rstd_r[h]])
        if not do_rope and extra is None:
            for h in range(2):
                kb.op("dve", lambda e: e.scalar_tensor_tensor(out=dsts[h], in0=outs[h][0][:, :], scalar=V((wkey, l)), in1=rstd_t[h][:], op0=ALU.mult, op1=ALU.mult), [outs[h][1], rstd_r[h], cres], [dst_r])
            return
        o = []
        for h in range(2):
            t_, r_ = S(); o.append((t_, r_))
            kb.op("dve", lambda e: e.scalar_tensor_tensor(out=t_[:], in0=outs[h][0][:, :], scalar=V((wkey, l)), in1=rstd_t[h][:], op0=ALU.mult, op1=ALU.mult), [outs[h][1], rstd_r[h], cres], [r_])
        if extra is not None:
            for h in range(2): extra(h, o[h][0], o[h][1])
        if not do_rope:
            for h in range(2):
                kb.op("act", lambda e: e.activation(out=dsts[h], in_=o[h][0][:], func=AF.Copy), [o[h][1]], [dst_r])
            return
        pb = []
        for h in range(2):
            b_, br_ = kb.bank("B"); pb.append((b_, br_))
            kb.mm(b_[:, :], rrot[:], o[h][0][:], True, True, [o[h][1], cres], br_)
        t1 = []; t2 = []
        for h in range(2):
            t_, r_ = S(); t1.append((t_, r_))
            kb.op("dve", lambda e: e.tensor_tensor(t_[:], pb[h][0][:, :], ropes[:, h * 512:(h + 1) * 512], ALU.mult), [pb[h][1], cres], [r_])
        for h in range(2):
            t_, r_ = S(); t2.append((t_, r_))
            kb.op("dve", lambda e: e.tensor_tensor(t_[:], o[h][0][:], ropec[:, h * 512:(h + 1) * 512], ALU.mult), [o[h][1], cres], [r_])
        for h in range(2):
            kb.op("dve", lambda e: e.tensor_tensor(dsts[h], t1[h][0][:], t2[h][0][:], ALU.add), [t1[h][1], t2[h][1]], [dst_r])

    def dwconv(out3, in3, res_out, res_in, center, taps, bias):
        L = in3.shape[-1]
        kb.op("act", lambda e: e.activation(out=out3, in_=in3, func=AF.Identity, bias=bias, scale=center), [res_in, cres], [res_out])
        for sh, wcol in taps:
            if sh < 0:
                o_ = out3[:, :, -sh:L]; i_ = in3[:, :, 0:L + sh]
            else:
                o_ = out3[:, :, 0:L - sh]; i_ = in3[:, :, sh:L]
            kb.op("dve", lambda e: e.scalar_tensor_tensor(out=o_, in0=i_, scalar=wcol, in1=o_, op0=ALU.mult, op1=ALU.add), [res_in, res_out, cres], [res_out])

    st = dict(nc=nc, kb=kb, es=es, I=I, O=O, x=x, x_r=x_r, hT=hT, hT_r=hT_r, mixT=mixT, mix_r=mix_r, ws=ws,
              V=V, CR=CR, cres=cres, S=S, SBF=SBF, carve=carve, dbg=dbg, norm_mod=norm_mod, project=project,
              hT_rhs=hT_rhs, mix_rhs=mix_rhs, mod_piece=mod_piece, mod_finish=mod_finish, mod_r=mod_r, run_pending_mod=run_pending_mod, headnorm=headnorm, rope=rope, qk_prep=qk_prep, dwconv=dwconv, modv=modv, pvec=pvec,
              ident=ident, identb=identb, onesb=onesb, altc=altc, altr=altr, dft=dft, stt=stt, stt_r=stt_r,
              gwb=gwb, gwb_r=gwb_r, fw1=fw1, fw2=fw2, fw_r=fw_r, cfg=cfg, NLAY=NLAY, ovl=ovl, dect=dect, dect_r=dect_r,
              KN=KN, KN_r=KN_r, cneg=cneg, cneg_r=cneg_r, ropec=ropec, ropes=ropes, sb=sb)
    return st


class _Stop(Exception):
    pass


def emit_passes(st):
    nc = st["nc"]; kb = st["kb"]; I = st["I"]; O = st["O"]; x = st["x"]; x_r = st["x_r"]
    hT = st["hT"]; hT_r = st["hT_r"]; mixT = st["mixT"]; mix_r = st["mix_r"]; ws = st["ws"]
    V = st["V"]; CR = st["CR"]; cres = st["cres"]; S = st["S"]; SBF = st["SBF"]; carve = st["carve"]; dbg = st["dbg"]
    project = st["project"]; hT_rhs = st["hT_rhs"]; mix_rhs = st["mix_rhs"]; headnorm = st["headnorm"]; rope = st["rope"]
    dwconv = st["dwconv"]; modv = st["modv"]; pvec = st["pvec"]; ident = st["ident"]; identb = st["identb"]
    onesb = st["onesb"]; altc = st["altc"]; altr = st["altr"]; dft = st["dft"]; stt = st["stt"]; stt_r = st["stt_r"]
    gwb = st["gwb"]; gwb_r = st["gwb_r"]; fw1 = st["fw1"]; fw2 = st["fw2"]; fw_r = st["fw_r"]
    dect = st["dect"]; dect_r = st["dect_r"]; KN = st["KN"]; KN_r = st["KN_r"]; cneg = st["cneg"]; cneg_r = st["cneg_r"]
    cfg = st["cfg"]; NLAY = st["NLAY"]
    WOFF = 16 * 1024; FOFF = 44 * 1024

    live = {"M": [], "W": [], "F": [], "G": []}

    def pe_warm(n):
        for i in range(n):
            bk, br = kb.bank("A")
            kb.mm(bk[:, :], onesb[:], hT[:, 0, 0:512], True, True, CR, br)

    STAGES = ["const", "mod", "filt", "norm", "attn", "lru", "hy", "wout"]

    def chk(name):
        sp = cfg.get("stop")
        if sp is not None and STAGES.index(name) >= STAGES.index(sp): raise _Stop()

    def merge_into(new_list, old_lists):
        ev = {}
        for ol in old_lists:
            for r in ol:
                if r.w is not None: ev[r.w[0]] = max(ev.get(r.w[0], 0), r.w[1])
                for sk, v in r.r.items(): ev[sk] = max(ev.get(sk, 0), v)
        for r in new_list:
            r.w = None; r.r = dict(ev)

    def w_phase(new_list):
        merge_into(new_list, [live["W"]]); live["W"] = list(new_list)

    allmix = [r for c in mix_r for r in c]
    live["M"] = allmix

    def run_pass(grp):
        P = grp == "P"; j = 0 if P else 1
        nseq, L = (4, 256) if P else (1, 1024); nP = L // 128
        KT = T if P else T + 256
        Cq, Sq = dft[L]
        xin = I["xp"] if P else I["xs"]
        for c in range(8):
            kb.dma("sp", x[:, c, :], xin[:, c, :], [], [x_r[c][0], x_r[c][1]], "xin%d" % c)
        if cfg["sample"] and "c2" not in kb.cnt:
            c2w = Res("c2w")
            kb.dma("pool", dft[1024][0][:], I["dftc1024"], [], [c2w], "c2")
            kb.dma("pool", dft[1024][1][:], I["dfts1024"], [], [c2w], "c2")
            st["c2all"] = Res("c2all"); st["c2all"].w = ("c2", kb.cnt["c2"])
        if not P:
            CR.append(st["c2all"])

        for l in range(NLAY):
            oldG = live["G"]; live["G"] = []
            merge_into(allmix, [oldG])
            At, _ = carve(FOFF, [128, nP, 512], BF16); Bt, _ = carve(FOFF + nP * 1024, [128, nP, 512], BF16)
            At_r = Res("At"); Bt_r = Res("Bt")
            merge_into([At_r, Bt_r], [live["F"]]); live["F"] = [At_r, Bt_r]
            o_ = WOFF
            h2a = st["ovl"][0:65, o_:o_ + L * 2].bitcast(BF16); o_ += L * 4
            fw3 = st["ovl"][0:65, o_:o_ + 2048].bitcast(BF16); o_ += 4096
            Pq, o_ = carve(o_, [128, nP, 512], BF16); Qq, o_ = carve(o_, [128, nP, 512], BF16)
            h2a_r = Res("h2a"); fw3_r = Res("fw3"); Pq_r = Res("Pq"); Qq_r = Res("Qq")
            merge_into([h2a_r, fw3_r, Pq_r, Qq_r], [live["W"], oldG]); live["W"] = [h2a_r, fw3_r, Pq_r, Qq_r]
            kb.dma("sp", fw1[:], I["hf_w1"][l], [], [fw_r], "fw")
            kb.dma("sp", fw2[:], I["hf_w2"][l], [], [fw_r], "fw")
            kb.dma("pool", fw3, I["hf_w3"][l], [], [fw3_r], "fw3")
            fwa = Res("fwall"); fwa.w = ("fw", kb.cnt["fw"])
            kb.dma("pool", gwb[:], I["gw"][l], [], [gwb_r], "gw")
            kb.op("dve", lambda e: e.memset(h2a[64:65, :], 1.0), [], [h2a_r])

            def sin_evac(ps, br, bcol, dst, dst_r, w):
                xx, xr = S()
                kb.op("dve", lambda e: e.tensor_scalar(out=xx[0:64, 0:w], in0=ps, scalar1=bcol, scalar2=None, op0=ALU.add), [br, cres], [xr])
                m1, m1r = S()
                kb.op("dve", lambda e: e.tensor_scalar(out=m1[0:64, 0:w], in0=xx[0:64, 0:w], scalar1=PI, scalar2=-2 * PI, op0=ALU.is_gt, op1=ALU.mult), [xr], [m1r])
                m2, m2r = S()
                kb.op("dve", lambda e: e.tensor_scalar(out=m2[0:64, 0:w], in0=xx[0:64, 0:w], scalar1=-PI, scalar2=2 * PI, op0=ALU.is_lt, op1=ALU.mult), [xr], [m2r])
                kb.op("dve", lambda e: e.tensor_tensor(xx[0:64, 0:w], xx[0:64, 0:w], m1[0:64, 0:w], ALU.add), [xr, m1r], [xr])
                kb.op("dve", lambda e: e.tensor_tensor(xx[0:64, 0:w], xx[0:64, 0:w], m2[0:64, 0:w], ALU.add), [xr, m2r], [xr])
                kb.op("dve", lambda e: e.tensor_scalar(out=xx[0:64, 0:w], in0=xx[0:64, 0:w], scalar1=3.1415925, scalar2=-3.1415925, op0=ALU.min, op1=ALU.max), [xr], [xr])
                kb.op("act", lambda e: e.activation(out=dst, in_=xx[0:64, 0:w], func=AF.Sin), [xr], [dst_r])

            for cb in range((L + 511) // 512):
                w = min(512, L - cb * 512)
                zt, ztr = S()
                kb.dma("sp", zt[0:33, 0:w], I["zT%d" % L][:, cb * 512:cb * 512 + w], [], [ztr], "zt")
                bk, br = kb.bank("A")
                kb.mm(bk[0:64, 0:w], fw1[0:33, 0:64], zt[0:33, 0:w], True, True, [fwa, ztr], br)
                h1, h1r = S()
                sin_evac(bk[0:64, 0:w], br, V(("fb1", l))[0:64, :], h1[0:64, 0:w], h1r, w)
                bk2, br2 = kb.bank("A")
                kb.mm(bk2[0:64, 0:w], fw2[0:64, 0:64], h1[0:64, 0:w], True, True, [fwa, h1r], br2)
                sin_evac(bk2[0:64, 0:w], br2, V(("fb2", l))[0:64, :], h2a[0:64, cb * 512:cb * 512 + w], h2a_r, w)
            for pc in range(nP):
                di = pc % 2
                kb.dma("sp", dect[di][:], I["dec%d" % L][:, pc], [], [dect_r[di]], "dec%d" % di)
                for ob in range(2):
                    bk, br = kb.bank("A")
                    kb.mm(bk[:, :], h2a[0:65, pc * 128:(pc + 1) * 128], fw3[0:65, ob * 512:(ob + 1) * 512], True, True, [h2a_r, fw3_r], br)
                    f_, fr = S()
                    kb.op("dve", lambda e: e.tensor_tensor(f_[:], bk[:, :], dect[di][:].rearrange("p a b -> p (a b)"), ALU.mult), [br, dect_r[di]], [fr])
                    kb.op("dve", lambda e: e.tensor_tensor(Pq[:, pc, ob * 256:(ob + 1) * 256], f_[:, 0:256], f_[:, 256:512], ALU.add), [fr], [Pq_r])
                    kb.op("dve", lambda e: e.tensor_tensor(Qq[:, pc, ob * 256:(ob + 1) * 256], f_[:, 256:512], f_[:, 0:256], ALU.subtract), [fr], [Qq_r])
            cwk = "cw%d" % L
            for m in range(nP):
                for (tab, tq, src, src_r, dst, dst_r) in ((Cq, 0, Pq, Pq_r, At, At_r), (Sq, 1, Qq, Qq_r, Bt, Bt_r)):
                    bk, br = kb.bank("A")
                    for pc in range(nP):
                        kb.mm(bk[:, :], tab[:, pc, m * 128:(m + 1) * 128], src[:, pc, :], pc == 0, pc == nP - 1, [src_r] + CR, br)
                    kb.op("act", lambda e: e.activation(out=dst[:, m, :], in_=bk[:, :], func=AF.Identity, scale=V(cwk, nP)[:, m:m + 1]), [br] + CR, [dst_r])
            bk, br = kb.bank("A")
            for pc in range(nP):
                kb.mm(bk[0:1, :], altc[:, 0:1], Pq[:, pc, :], pc == 0, pc == nP - 1, [Pq_r] + CR, br)
            kb.op("act", lambda e: e.activation(out=KN[0:1, :], in_=bk[0:1, :], func=AF.Identity, scale=0.5 / L), [br], [KN_r])
            if l == 0:
                dbg("At" + grp, At[:].rearrange("p a b -> p (a b)"), At_r, [128, nP * 512])
                dbg("KN" + grp, KN[:], KN_r, [1, 512])

            chk("filt")
            kb.op("act", lambda e: e.activation(out=cneg[:], in_=V(("lam", l), 4), func=AF.Exp, scale=-1.0), CR, [cneg_r])
            kb.op("act", lambda e: e.activation(out=cneg[:], in_=cneg[:], func=AF.Ln, bias=1.0, scale=1.0), [cneg_r], [cneg_r])
            kb.op("dve", lambda e: e.tensor_scalar(out=cneg[:], in0=cneg[:], scalar1=-8.0, scalar2=None, op0=ALU.mult), [cneg_r], [cneg_r])

            st["norm_mod"](l, j, 0)
            if l == 0: dbg("hT" + grp, hT[:, 0, :], hT_r[0][1], [128, T])

            chk("norm")
            o_ = WOFF
            qT, o_ = carve(o_, [128, 4, T], BF16); kdT, o_ = carve(o_, [128, 2, KT], BF16)
            Vtok, o_ = carve(o_, [128, KT // 128, 256], BF16); vT, o_ = carve(o_, [128, T], F32); kst, o_ = carve(o_, [128, T], F32)
            q_r = [Res("q%d" % c) for c in range(4)]; kd_r = [Res("kd0"), Res("kd1")]; vt_r = Res("Vtok"); vT_r = Res("vT"); kst_r = Res("kst")
            w_phase(q_r + kd_r + [vt_r, vT_r, kst_r])
            koff = 0 if P else 256
            Vt4 = Vtok.rearrange("p k (g e) -> p k g e", g=2)
            kb.op("dve", lambda e: e.memset(Vtok[:], 1.0), [], [vt_r])
            if not P:
                for g in range(2):
                    kb.dma("pool", kdT[:, g, 0:256], I["ckT"][l, g], [], [kd_r[g]], "ck%d" % g)
                kb.dma("pool", Vt4[:, 0:2, :, 0:64], I["cv"][l].rearrange("p c (g d) -> p c g d", g=2), [vt_r], [vt_r], "cv")
            for g in range(2):
                outs = project(I["w_in_p"][l, g], 8, hT_rhs)
                if P:
                    def kextra(h, o_, o_r_, g=g):
                        kb.op("act", lambda e: e.activation(out=kst[g * 64:(g + 1) * 64, h * 512:(h + 1) * 512], in_=o_[0:64, :], func=AF.Copy), [o_r_], [kst_r])
                    st["qk_prep"](l, outs, "kn", [kdT[:, g, 0:512], kdT[:, g, 512:1024]], kd_r[g], False, kextra)
                else:
                    st["qk_prep"](l, outs, "kn", [kdT[:, g, 256:768], kdT[:, g, 768:1280]], kd_r[g], True)
            if P:
                kb.dma("sp", O["nk"][l], kst[:], [kst_r], [], "onk");
                if "onk" not in kb.outsems: kb.outsems.append("onk")
            outs = project(I["w_in_p"][l, 2], 8, hT_rhs)
            for h, (bk, br) in enumerate(outs):
                kb.op("act", lambda e: e.activation(out=vT[:, h * 512:(h + 1) * 512], in_=bk[:, :], func=AF.Copy), [br], [vT_r])
            if P:
                kb.dma("sp", O["nv"][l], vT[:], [vT_r], [], "onv")
                if "onv" not in kb.outsems: kb.outsems.append("onv")
            for tb4 in range(2):
                bk, br = kb.bank("B")
                for i4 in range(4):
                    tb = tb4 * 4 + i4
                    kb.mm(bk[:, i4 * 128:(i4 + 1) * 128], vT[:, tb * 128:(tb + 1) * 128], ident[:], True, True, [vT_r] + CR, br, transpose=True)
                for g_ in range(2):
                    kb.op("dve", lambda e: e.tensor_copy(Vt4[:, koff // 128 + tb4 * 4:koff // 128 + tb4 * 4 + 4, g_, 0:64], bk[:, :].rearrange("p (a g d) -> p a g d", a=4, g=2)[:, :, g_, :]), [br, vt_r], [vt_r])
            if l == 0:
                dbg("kdT" + grp, kdT[:, 0, :], kd_r[0], [128, KT])

            for c in range(4):
                outs = project(I["w_in_p"][l, 3 + c], 8, hT_rhs)
                st["qk_prep"](l, outs, "qn", [qT[:, c, 0:512], qT[:, c, 512:1024]], q_r[c], not P)
            for c in range(4):
                g = c // 2
                NI = 2 if P else KT // 128
                items = [(qh, i) for qh in range(2) for i in range(NI)]
                Sb = {}
                pair = {}

                def emit_S(n):
                    qh, i = items[n]
                    bks = [kb.bank("A"), kb.bank("A")]
                    if P:
                        s_ = qh * 2 + i
                        for kc in range(2):
                            for ph in range(2):
                                ps_ = slice(ph * 64, (ph + 1) * 64)
                                kb.mm(bks[ph][0][:, kc * 256:(kc + 1) * 256], kdT[ps_, g, s_ * 256 + kc * 128:s_ * 256 + (kc + 1) * 128], qT[ps_, c, s_ * 256:(s_ + 1) * 256], True, True, [kd_r[g], q_r[c]], bks[ph][1])
                    else:
                        for ph in range(2):
                            ps_ = slice(ph * 64, (ph + 1) * 64)
                            kb.mm(bks[ph][0][:, :], kdT[ps_, g, i * 128:(i + 1) * 128], qT[ps_, c, qh * 512:(qh + 1) * 512], True, True, [kd_r[g], q_r[c]], bks[ph][1])
                    Sb[n] = bks

                def emit_PV(n):
                    qh, i = items[n]
                    if i == 0:
                        pair[qh] = [kb.bank("B"), kb.bank("B")]
                    bks = Sb.pop(n)
                    pts = []
                    for ph in range(2):
                        pt, ptr = SBF(); pts.append((pt, ptr))
                        kb.op("act", lambda e: e.activation(out=pt[:], in_=bks[ph][0][:, :], func=AF.Exp, scale=0.125), [bks[ph][1]], [ptr])
                    for ph in range(2):
                        nb, nbr = pair[qh][ph]
                        pt, ptr = pts[ph]
                        if P:
                            s_ = qh * 2 + i
                            for kc in range(2):
                                kb.mm(nb[:, i * 256:(i + 1) * 256], Vt4[:, 2 * s_ + kc, g, :], pt[:, kc * 256:(kc + 1) * 256], kc == 0, kc == 1, [vt_r, ptr], nbr)
                        else:
                            kb.mm(nb[:, :], Vt4[:, i, g, :], pt[:], i == 0, i == NI - 1, [vt_r, ptr], nbr)
                    if i == NI - 1:
                        for ph in range(2):
                            nb, nbr = pair[qh][ph]
                            rc, rcr = S()
                            if P or cfg.get("actrc", False):
                                kb.op("act", lambda e: e.activation(out=rc[0:64, :], in_=nb[64:128, :], func=AF.Ln), [nbr], [rcr])
                                kb.op("act", lambda e: e.activation(out=rc[0:64, :], in_=rc[0:64, :], func=AF.Exp, scale=-1.0), [rcr], [rcr])
                            else:
                                kb.op("dve", lambda e: e.reciprocal(rc[0:64, :], nb[64:128, :]), [nbr], [rcr])
                            kb.op("dve", lambda e: e.tensor_tensor(mixT[ph * 64:(ph + 1) * 64, c, qh * 512:(qh + 1) * 512], nb[0:64, :], rc[0:64, :], ALU.mult), [nbr, rcr], [mix_r[c][qh]])

                LA = 1
                for n in range(len(items) + LA):
                    if n < len(items): emit_S(n)
                    if n - LA >= 0: emit_PV(n - LA)

            if l == 0:
                for cdbg in range(4):
                    dbg("att%d" % cdbg + grp, mixT[:, cdbg, :], mix_r[cdbg][1], [128, T])
            chk("attn")
            for c in range(2):
                o_ = WOFF
                lx, o_ = carve(o_, [128, T], F32); xc, o_ = carve(o_, [128, T], F32); xcb, o_ = carve(o_, [128, T], BF16)
                gl, o_ = carve(o_, [128, T], BF16)
                rgs = []; igs = []
                for d in range(2):
                    t_, o_ = carve(o_, [128, T], F32); rgs.append(t_)
                    t_, o_ = carve(o_, [128, T], F32); igs.append(t_)
                lx_r = Res("lx"); xc_r = Res("xc"); xcb_r = Res("xcb"); gl_r = Res("gl")
                rg_rs = [Res("rg0"), Res("rg1")]; ig_rs = [Res("ig0"), Res("ig1")]
                w_phase([lx_r, xc_r, xcb_r, gl_r] + rg_rs + ig_rs)
                hd = [lx, xc]; hd_r = [lx_r, xc_r]
                hd0, hd1 = hd
                outs = project(I["w_in_p"][l, 7 + 2 * c], 8, hT_rhs)
                for h, (bk, br) in enumerate(outs):
                    kb.op("act", lambda e: e.activation(out=lx[:, h * 512:(h + 1) * 512], in_=bk[:, :], func=AF.Copy), [br], [lx_r])
                outs = project(I["w_in_p"][l, 8 + 2 * c], 8, hT_rhs)
                for h, (bk, br) in enumerate(outs):
                    hs = slice(h * 512, (h + 1) * 512)
                    a1, a1r = S()
                    kb.op("act", lambda e: e.activation(out=a1[:], in_=bk[:, :], func=AF.Square), [br], [a1r])
                    kb.op("dve", lambda e: e.tensor_scalar(out=a1[:], in0=a1[:], scalar1=0.044715, scalar2=1.0, op0=ALU.mult, op1=ALU.add), [a1r], [a1r])
                    kb.op("dve", lambda e: e.tensor_tensor(a1[:], a1[:], bk[:, :], ALU.mult), [a1r, br], [a1r])
                    kb.op("act", lambda e: e.activation(out=a1[:], in_=a1[:], func=AF.Sigmoid, scale=1.5957691216), [a1r], [a1r])
                    kb.op("dve", lambda e: e.tensor_tensor(gl[:, hs], a1[:], bk[:, :], ALU.mult), [a1r, br], [gl_r])
                v3 = lambda a: a.rearrange("p (s t) -> p s t", s=nseq)
                lcw = V(("lcw", l), 8)
                dwconv(v3(xc), v3(lx), xc_r, lx_r, lcw[:, 4 + c:5 + c],
                       [(-2, lcw[:, 0 + c:1 + c]), (-1, lcw[:, 2 + c:3 + c]), (1, lcw[:, 6 + c:7 + c])], V(("lcb", l), 2)[:, c:c + 1])
                kb.op("act", lambda e: e.activation(out=xcb, in_=xc, func=AF.Copy), [xc_r], [xcb_r])
                if l == 0 and c == 0: dbg("xc" + grp, xc, xc_r, [128, T])
                for d in range(2):
                    for gt, dst, dst_r in ((0, rgs[d], rg_rs[d]), (1, igs[d], ig_rs[d])):
                        idx = (d * 2 + gt) * 2 + c
                        for h in range(2):
                            hs = slice(h * 512, (h + 1) * 512)
                            bk, br = kb.bank("A")
                            kb.mm(bk[:, :], gwb[:, idx, :], xcb[:, hs], True, True, [gwb_r, xcb_r], br)
                            kb.op("act", lambda e: e.activation(out=dst[:, hs], in_=bk[:, :], func=AF.Sigmoid, bias=V(("lgb", l), 8)[:, idx:idx + 1], scale=1.0), [br] + CR, [dst_r])
                for d in range(2):
                    kb.op("act", lambda e: e.activation(out=rgs[d], in_=rgs[d], func=AF.Exp, scale=cneg[:, d * 2 + c:d * 2 + c + 1]), [rg_rs[d], cneg_r], [rg_rs[d]])
                for d in range(2):
                    kb.op("dve", lambda e: e.tensor_tensor(igs[d], igs[d], xc, ALU.mult), [ig_rs[d], xc_r], [ig_rs[d]])
                a2s = []
                for d in range(2):
                    for h in range(2):
                        hs = slice(h * 512, (h + 1) * 512)
                        a2, a2r = S(); a2s.append((a2, a2r))
                        kb.op("dve", lambda e: e.tensor_tensor(a2[:], rgs[d][:, hs], rgs[d][:, hs], ALU.mult), [rg_rs[d]], [a2r])
                for a2, a2r in a2s:
                    kb.op("act", lambda e: e.activation(out=a2[:], in_=a2[:], func=AF.Sqrt, bias=1.0, scale=-1.0), [a2r], [a2r])
                for d in range(2):
                    for h in range(2):
                        hs = slice(h * 512, (h + 1) * 512)
                        a2, a2r = a2s[d * 2 + h]
                        kb.op("dve", lambda e: e.tensor_tensor(igs[d][:, hs], igs[d][:, hs], a2[:], ALU.mult), [ig_rs[d], a2r], [ig_rs[d]])
                for d in range(2):
                    rg = rgs[d]; ig = igs[d]
                    for s in range(nseq):
                        lo, hi = s * L, (s + 1) * L
                        init = 0.0 if P else pvec[:, PV_H0 + (l * 2 + d) * 2 + c:PV_H0 + (l * 2 + d) * 2 + c + 1]
                        if d == 0:
                            kb.op("dve", lambda e: e.tensor_tensor_scan(hd[d][:, lo:hi], rg[:, lo:hi], ig[:, lo:hi], init, ALU.mult, ALU.add), [rg_rs[d], ig_rs[d]] + CR, [hd_r[d]])
                        else:
                            rv = (lambda a: a[:, hi - 1::-1]) if lo == 0 else (lambda a: a[:, hi - 1:lo - 1:-1])
                            kb.op("dve", lambda e: e.tensor_tensor_scan(rv(hd[d]), rv(rg), rv(ig), init, ALU.mult, ALU.add), [rg_rs[d], ig_rs[d]] + CR, [hd_r[d]])
                    if P:
                        col = ((l * 2 + d) * 2 + c) * 4
                        src = v3(hd[d])[:, :, L - 1] if d == 0 else v3(hd[d])[:, :, 0]
                        kb.op("act", lambda e: e.activation(out=stt[:, col:col + 4], in_=src, func=AF.Copy), [hd_r[d]], [stt_r])
                if l == 0 and c == 0: dbg("hf" + grp, hd0, hd_r[0], [128, T]); dbg("hb" + grp, hd1, hd_r[1], [128, T])
                kb.op("dve", lambda e: e.tensor_tensor(hd0, hd0, hd1, ALU.add), hd_r, [hd_r[0]])
                for h in range(2):
                    hs = slice(h * 512, (h + 1) * 512)
                    kb.op("dve", lambda e: e.tensor_tensor(mixT[:, 4 + c, hs], hd0[:, hs], gl[:, hs], ALU.mult), [hd_r[0], gl_r], [mix_r[4 + c][h]])

            chk("lru")
            o_ = WOFF
            hv, o_ = carve(o_, [128, 2, T], BF16); hx1, o_ = carve(o_, [128, 2, T], BF16); hx2, o_ = carve(o_, [128, 2, T], BF16)
            o_s2 = o_
            raws = []; cvts = []
            for i2 in range(2):
                r_, o_ = carve(o_, [128, T], F32); c_, o_ = carve(o_, [128, T], F32)
                raws.append((r_, Res("raw%d" % i2))); cvts.append((c_, Res("cvt%d" % i2)))
            hv_r = Res("hv"); hx1_r = Res("hx1"); hx2_r = Res("hx2")
            w_phase([hv_r, hx1_r, hx2_r] + [r for _, r in raws] + [r for _, r in cvts])
            v3 = lambda a: a.rearrange("p (s t) -> p s t", s=nseq)
            hcw = V(("hcw", l), 18)
            for i in range(6):
                raw, raw_r = raws[i % 2]; cvt, cvt_r = cvts[i % 2]
                dst, dst_r = ((hv, hv_r), (hx1, hx1_r), (hx2, hx2_r))[i // 2]
                outs = project(I["w_in_p"][l, 11 + i], 8, hT_rhs)
                for h, (bk, br) in enumerate(outs):
                    hs = slice(h * 512, (h + 1) * 512)
                    kb.op("act", lambda e: e.activation(out=raw[:, hs], in_=bk[:, :], func=AF.Copy), [br], [raw_r])
                    kb.op("dve", lambda e: e.tensor_scalar(out=cvt[:, hs], in0=raw[:, hs], scalar1=hcw[:, 6 + i:7 + i], scalar2=None, op0=ALU.mult), [raw_r, cres], [cvt_r])
                r3 = v3(raw); c3 = v3(cvt); d3 = v3(dst[:, i % 2, :])
                kb.op("dve", lambda e: e.scalar_tensor_tensor(out=c3[:, :, 1:L], in0=r3[:, :, 0:L - 1], scalar=hcw[:, i:i + 1], in1=c3[:, :, 1:L], op0=ALU.mult, op1=ALU.add), [raw_r, cvt_r, cres], [cvt_r])
                kb.op("dve", lambda e: e.scalar_tensor_tensor(out=c3[:, :, 0:L - 1], in0=r3[:, :, 1:L], scalar=hcw[:, 12 + i:13 + i], in1=c3[:, :, 0:L - 1], op0=ALU.mult, op1=ALU.add), [raw_r, cvt_r, cres], [cvt_r])
                kb.op("act", lambda e: e.activation(out=dst[:, i % 2, :], in_=cvt, func=AF.Copy), [cvt_r], [dst_r])
            raw_r = raws[0][1]; cvt_r = raws[1][1]; raw_r2 = cvts[0][1]; cvt_r2 = cvts[1][1]
            if l == 0: dbg("hv" + grp, hv[:, 0, :], hv_r, [128, T])
            tok, o_ = carve(o_s2, [128, nP, nseq * 256], BF16); Zr, o_ = carve(o_, [128, nP, nseq * 256], BF16); Zs, o_ = carve(o_, [128, nP, nseq * 256], BF16)
            tok_r = Res("tok"); Zr_r = Res("Zr"); Zs_r = Res("Zs")
            merge_into([tok_r, Zr_r, Zs_r], [[raw_r, cvt_r, raw_r2, cvt_r2]]); live["W"] = [hv_r, hx1_r, hx2_r, tok_r, Zr_r, Zs_r]
            ZN = st.setdefault("ZN", None)
            if ZN is None:
                ZN = st["sb"]("ZN", [1, 1024], BF16); st["ZN"] = ZN; st["ZN_r"] = Res("ZN")
            ZN_r = st["ZN_r"]
            NCOL = nseq * 256; GW = min(512, NCOL); NG = NCOL // GW
            TW = min(L, 512); NTH = L // TW

            PWE = cfg.get("pwe", "pool")

            def longconv(o, u, u_r, epilogue):
                for s in range(nseq):
                    for tc in range(nP):
                        bk, br = kb.bank("B")
                        bkb = bk[:, :].bitcast(BF16)
                        for cc in range(2):
                            kb.mm(bkb[:, cc * 128:(cc + 1) * 128], u[:, cc, s * L + tc * 128:s * L + (tc + 1) * 128], identb[:], True, True, [u_r] + CR, br, transpose=True)
                        kb.op("act", lambda e: e.activation(out=tok[:, tc, s * 256:(s + 1) * 256], in_=bkb[:, 0:256], func=AF.Copy), [br], [tok_r])
                for gi in range(NG):
                    gs = slice(gi * GW, (gi + 1) * GW)
                    un, unr = kb.bank("B")
                    for tc in range(nP):
                        kb.mm(un[0:1, 0:GW], altc[:, 0:1], tok[:, tc, gs], tc == 0, tc == nP - 1, [tok_r] + CR, unr)
                    for si in range(GW // 256):
                        zc = gi * GW + si * 256
                        kb.op("dve", lambda e: e.tensor_tensor(ZN[0:1, zc:zc + 256], un[0:1, si * 256:(si + 1) * 256], KN[0:1, o * 256:(o + 1) * 256], ALU.mult), [unr, KN_r], [ZN_r])
                for m in range(nP):
                    for gi in range(NG):
                        gs = slice(gi * GW, (gi + 1) * GW)
                        ur, urr = kb.bank("A"); us, usr = kb.bank("B")
                        for tc in range(nP):
                            kb.mm(ur[:, 0:GW], Cq[:, tc, m * 128:(m + 1) * 128], tok[:, tc, gs], tc == 0, tc == nP - 1, [tok_r] + CR, urr)
                        for tc in range(nP):
                            kb.mm(us[:, 0:GW], Sq[:, tc, m * 128:(m + 1) * 128], tok[:, tc, gs], tc == 0, tc == nP - 1, [tok_r] + CR, usr)
                        for si in range(GW // 256):
                            cs = slice(si * 256, (si + 1) * 256); zc = gi * GW + si * 256
                            A_ = At[:, m, o * 256:(o + 1) * 256]; B_ = Bt[:, m, o * 256:(o + 1) * 256]
                            t1, t1r = S(); t2, t2r = S(); t3, t3r = S(); t4, t4r = S()
                            kb.op("dve", lambda e: e.tensor_tensor(t1[:, 0:256], ur[:, cs], A_, ALU.mult), [urr, At_r], [t1r])
                            kb.op("dve", lambda e: e.tensor_tensor(t2[:, 0:256], us[:, cs], B_, ALU.mult), [usr, Bt_r], [t2r])
                            kb.op(PWE, lambda e: e.tensor_tensor(Zr[:, m, zc:zc + 256], t1[:, 0:256], t2[:, 0:256], ALU.add), [t1r, t2r], [Zr_r])
                            kb.op("dve", lambda e: e.tensor_tensor(t3[:, 0:256], us[:, cs], A_, ALU.mult), [usr, At_r], [t3r])
                            kb.op("dve", lambda e: e.tensor_tensor(t4[:, 0:256], ur[:, cs], B_, ALU.mult), [urr, Bt_r], [t4r])
                            kb.op(PWE, lambda e: e.tensor_tensor(Zs[:, m, zc:zc + 256], t3[:, 0:256], t4[:, 0:256], ALU.subtract), [t3r, t4r], [Zs_r])
                for s in range(nseq):
                    for cc in range(2):
                        zs_ = slice(s * 256 + cc * 128, s * 256 + (cc + 1) * 128)
                        for th in range(NTH):
                            bk, br = kb.bank("A")
                            for m in range(nP):
                                kb.mm(bk[:, 0:TW], Zr[:, m, zs_], Cq[:, m, th * TW:(th + 1) * TW], m == 0, False, [Zr_r] + CR, br)
                            for m in range(nP):
                                kb.mm(bk[:, 0:TW], Zs[:, m, zs_], Sq[:, m, th * TW:(th + 1) * TW], False, False, [Zs_r] + CR, br)
                            kb.mm(bk[:, 0:TW], ZN[0:1, zs_], altr[0:1, th * TW:(th + 1) * TW], False, True, [ZN_r] + CR, br)
                            epilogue(cc, slice(s * L + th * TW, s * L + (th + 1) * TW), bk[:, 0:TW], br)

            hsk = V(("hsk", l), 4)

            def ep1(cc, ts, ps, br):
                t1, t1r = S()
                kb.op("dve", lambda e: e.scalar_tensor_tensor(out=t1[:, 0:TW], in0=hv[:, cc, ts], scalar=hsk[:, cc:cc + 1], in1=ps, op0=ALU.mult, op1=ALU.add), [hv_r, br] + CR, [t1r])
                kb.op("dve", lambda e: e.tensor_tensor(hx1[:, cc, ts], t1[:, 0:TW], hx1[:, cc, ts], ALU.mult), [t1r, hx1_r], [hx1_r])

            def ep2(cc, ts, ps, br):
                t1, t1r = S()
                kb.op("dve", lambda e: e.scalar_tensor_tensor(out=t1[:, 0:TW], in0=hx1[:, cc, ts], scalar=hsk[:, 2 + cc:3 + cc], in1=ps, op0=ALU.mult, op1=ALU.add), [hx1_r, br] + CR, [t1r])
                h_ = ts.start // 512
                kb.op("dve", lambda e: e.tensor_tensor(mixT[:, 6 + cc, ts], t1[:, 0:TW], hx2[:, cc, ts], ALU.mult), [t1r, hx2_r], [mix_r[6 + cc][h_]])

            longconv(0, hv, hv_r, ep1)
            if l == 0: dbg("z" + grp, hx1[:, 0, :], hx1_r, [128, T])
            longconv(1, hx1, hx1_r, ep2)
            if l == 0:
                for cdbg in range(8):
                    dbg("mix%d" % cdbg + grp, mixT[:, cdbg, :], mix_r[cdbg][1], [128, T])

            chk("hy")
            st["run_pending_mod"](100)
            for jo in range(8):
                outs = project(I["w_out_p"][l, jo], 8, mix_rhs)
                for h, (bk, br) in enumerate(outs):
                    hs = slice(h * 512, (h + 1) * 512)
                    kb.op("dve", lambda e: e.scalar_tensor_tensor(out=x[:, jo, hs], in0=bk[:, :], scalar=modv[:, l, 16 + jo, j:j + 1], in1=x[:, jo, hs], op0=ALU.mult, op1=ALU.add), [br, x_r[jo][h], cres, st["mod_r"][l]], [x_r[jo][h]])
            if l == 0: dbg("x1" + grp, x[:, 0, :], x_r[0][1], [128, T])

            chk("wout")
            st["norm_mod"](l, j, 1)
            G = st["ovl"][:, 0:44 * 1024].bitcast(BF16).rearrange("p (f t) -> p f t", f=NF)
            G_r = [[Res("G") for h in range(2)] for f in range(NF)]
            allG = [r for f in G_r for r in f]
            merge_into(allG, [live["W"], live["M"], live["F"]]); live["G"] = allG; live["W"] = []; live["F"] = []
            modq = list(range(48)) if (P and l + 1 < NLAY) or (not cfg["prompt"] and not P and l + 1 < NLAY) else []
            for f in range(NF):
                for _ in range(2):
                    if modq: st["mod_piece"](l + 1, modq.pop(0))
                o1 = project(I["w1_p"][l, f], 8, hT_rhs, pool="A")
                o3 = project(I["w3_p"][l, f], 8, hT_rhs, pool="B")
                for h in range(2):
                    sg, sgr = S()
                    kb.op("act", lambda e: e.activation(out=sg[:], in_=o1[h][0][:, :], func=AF.Silu), [o1[h][1]], [sgr])
                    kb.op("dve", lambda e: e.tensor_tensor(G[:, f, h * 512:(h + 1) * 512], sg[:], o3[h][0][:, :], ALU.mult), [sgr, o3[h][1]], [G_r[f][h]])
            for jo in range(8):
                if modq:
                    st["mod_piece"](l + 1, modq.pop(0))
                    if not modq: st["mod_finish"](l + 1)
                wa, war = ws.load(I["w2_p"][l, jo][:, 0:11, :], 11)
                wb, wbr = ws.load(I["w2_p"][l, jo][:, 11:22, :], 11)
                for h in range(2):
                    hs = slice(h * 512, (h + 1) * 512)
                    bk, br = kb.bank("A")
                    for k in range(NF):
                        wv, wr = (wa, war) if k < 11 else (wb, wbr)
                        kb.mm(bk[:, :], wv[:, k % 11, :], G[:, k, hs], k == 0, k == NF - 1, [wr, G_r[k][h]], br)
                    kb.op("dve", lambda e: e.scalar_tensor_tensor(out=x[:, jo, hs], in0=bk[:, :], scalar=modv[:, l, 40 + jo, j:j + 1], in1=x[:, jo, hs], op0=ALU.mult, op1=ALU.add), [br, x_r[jo][h], cres, st["mod_r"][l]], [x_r[jo][h]])
        yo = O["yp"] if P else O["ys"]
        for c in range(8):
            kb.dma("sp", yo[:, c, :], x[:, c, :], [x_r[c][0], x_r[c][1]], [], "oy%d" % c)
            if "oy%d" % c not in kb.outsems: kb.outsems.append("oy%d" % c)

    try:
        chk("mod")
        if cfg["prompt"]:
            run_pass("P")
            kb.dma("sp", O["nst"], stt[:], [stt_r], [], "onst"); kb.outsems.append("onst")
        if cfg["sample"]:
            run_pass("S")
    except _Stop:
        pass
    for sk, v in kb.cnt.items():
        if v > 0:
            nc.sync.wait_ge(kb.semobj[sk], v)


_CACHE = {}


def _get_program(cfg):
    key = (cfg["layers"], cfg["prompt"], cfg["sample"], tuple(cfg.get("dbg", [])), cfg.get("stop"))
    if key not in _CACHE:
        st = build_program(cfg)
        emit_passes(st)
        _CACHE[key] = st
    return _CACHE[key]


def kernel(**inputs):
    cfg = CFG
    st = _get_program(cfg)
    nc = st["nc"]
    shared = _host_shared(inputs)
    in_maps = []
    for c in range(8):
        m = dict(shared); m.update(_host_core(inputs, c)); in_maps.append(m)
    res = run_bass_kernel_spmd(nc, in_maps, core_ids=list(range(8)))
    R = res.results
    kernel.last = R
    f32 = np.float32
    y_p = np.zeros((32, 256, D), f32); y_s = np.zeros((2, T, D), f32)
    nk = np.zeros((32, NL, 256, 2, 64), f32); nv = np.zeros((32, NL, 256, 2, 64), f32); nst = np.zeros((32, NL, 2, 256), f32)
    for c in range(8):
        r = R[c]
        y_p[4 * c:4 * c + 4] = r["yp"].transpose(2, 1, 0).reshape(4, 256, D)
        if c < 2:
            y_s[c] = r["ys"].transpose(2, 1, 0).reshape(T, D)
        k = r["nk"].reshape(NL, 2, 64, 4, 256)
        nk[4 * c:4 * c + 4] = k.transpose(3, 0, 4, 1, 2)
        v = r["nv"].reshape(NL, 2, 64, 4, 256)
        nv[4 * c:4 * c + 4] = v.transpose(3, 0, 4, 1, 2)
        s_ = r["nst"].reshape(128, NL, 2, 2, 4)
        nst[4 * c:4 * c + 4] = s_.transpose(4, 1, 2, 3, 0).reshape(4, NL, 2, 256)
    return (y_p, y_s, nk, nv, nst)
```

```python
import math
from contextlib import ExitStack
import numpy as np
import concourse.bass as bass
import concourse.mybir as mybir
from concourse.bass_utils import run_bass_kernel_spmd

F32 = mybir.dt.float32
BF16 = mybir.dt.bfloat16
AF = mybir.ActivationFunctionType
ALU = mybir.AluOpType

D = 1024; NL = 4; T = 1024; HD = 64
OFF_K = 512; OFF_V = 640; OFF_LX = 768; OFF_LG = 1024; OFF_HY = 1280
DFF = 2816; NF = 22
EPS = 1e-6
PI = math.pi

CFG = {"layers": NL, "prompt": True, "sample": True, "dbg": []}

VCOLS = {}
def _mk_vcols():
    o = 0
    def add(n, c):
        nonlocal o
        VCOLS[n] = o; o += c
    for l in range(NL):
        add(("n1", l), 8); add(("n2", l), 8); add(("qn", l), 1); add(("kn", l), 1)
        add(("lcw", l), 8)
        add(("lcb", l), 2)
        add(("lgb", l), 8)
        add(("lam", l), 4)
        add(("hcw", l), 18)
        add(("hsk", l), 4)
        add(("fb1", l), 1); add(("fb2", l), 1)
    for l in range(NL):
        add(("bmod", l), 48)
    add("cw256", 2); add("cw1024", 8); add("altc", 1)
    return o
NV = _mk_vcols()
PV_COND = 0; PV_H0 = 16; NPV = 32


def _fm(v, nchunk):
    return np.ascontiguousarray(np.asarray(v, np.float32).reshape(nchunk, 128).T)


def _host_consts():
    f32 = np.float32
    c = {}
    c["ident"] = np.eye(128, dtype=f32)
    R = np.zeros((128, 128), f32)
    for m in range(128):
        if m % 32 < 16: R[m + 16, m] = -1.0
        else: R[m - 16, m] = 1.0
    c["rrot"] = R
    t = np.arange(T)
    row = (t // 64).astype(f32); col = (t % 64).astype(f32)
    freqs = (f32(10000.0) ** (-(np.arange(16, dtype=f32)) / f32(16))).astype(f32)
    rc = np.zeros((128, T), f32); rs = np.zeros((128, T), f32)
    for p in range(128):
        d = p % 64
        pos = row if d < 32 else col
        ang = (pos * freqs[d % 16]).astype(f32)
        rc[p] = np.cos(ang); rs[p] = np.sin(ang)
    c["ropec"] = rc; c["ropes"] = rs
    for L in (256, 1024):
        tt = np.arange(L, dtype=f32); tn = (tt / f32(L)).astype(f32)
        bands = np.linspace(1e-4, 15, 16, dtype=f32)
        ang = (f32(2.0 * math.pi / L) * tt[:, None] * bands[None, :]).astype(f32)
        z = np.concatenate([tn[:, None], np.cos(ang), -np.sin(ang)], axis=-1).astype(f32)
        c["zT%d" % L] = np.ascontiguousarray(z.T)
        deltas = np.linspace(math.log(1e-2) / 1.5, math.log(1e-2) / 0.3, 256, dtype=f32)
        dec = np.exp(-tn[:, None] * np.abs(deltas)[None, :]).astype(f32)
        decb = dec.copy(); decb[0] = 0.0
        dd = np.stack([dec, decb], axis=1)
        c["dec%d" % L] = np.ascontiguousarray(dd.reshape(L // 128, 128, 2, 256).transpose(1, 0, 2, 3))
        k = np.arange(L, dtype=np.float64)
        th = math.pi / L * np.outer(k, k)
        C = np.cos(th).astype(f32); S = np.sin(th).astype(f32)
        c["dftc%d" % L] = np.ascontiguousarray(C.reshape(L // 128, 128, L).transpose(1, 0, 2))
        c["dfts%d" % L] = np.ascontiguousarray(S.reshape(L // 128, 128, L).transpose(1, 0, 2))
    c["altr"] = ((-1.0) ** np.arange(T)).astype(f32)[None, :]
    c["altcin"] = ((-1.0) ** np.arange(128)).astype(f32)[:, None]
    return c


def _host_shared(inp):
    f32 = np.float32
    g = lambda k: np.asarray(inp[k], f32)
    sh = _host_consts()
    vec = np.zeros((128, NV), f32)
    def put(key, arr):
        vec[:, VCOLS[key]:VCOLS[key] + arr.shape[1]] = arr
    for l in range(NL):
        put(("n1", l), _fm(g("norm1_w")[l], 8)); put(("n2", l), _fm(g("norm2_w")[l], 8))
        put(("qn", l), np.tile(g("q_norm_w")[l], 2)[:, None]); put(("kn", l), np.tile(g("k_norm_w")[l], 2)[:, None])
        put(("lcw", l), np.concatenate([_fm(g("lru_conv_w")[l, k], 2) for k in range(4)], axis=1))
        put(("lcb", l), _fm(g("lru_conv_b")[l], 2))
        put(("lgb", l), np.concatenate([_fm(g("lru_gate_b")[l, d, gt], 2) for d in range(2) for gt in range(2)], axis=1))
        put(("lam", l), np.concatenate([_fm(g("lru_lambda")[l, d], 2) for d in range(2)], axis=1))
        put(("hcw", l), np.concatenate([_fm(g("hy_conv_w")[l, k], 6) for k in range(3)], axis=1))
        put(("hsk", l), np.concatenate([_fm(g("hy_skip")[l, o], 2) for o in range(2)], axis=1))
        b1 = np.zeros((128, 1), f32); b1[:64, 0] = g("hy_filt_b1")[l]; put(("fb1", l), b1)
        b2 = np.zeros((128, 1), f32); b2[:64, 0] = g("hy_filt_b2")[l]; put(("fb2", l), b2)
    for L, key in ((256, "cw256"), (1024, "cw1024")):
        cw = np.full(L, 1.0 / L, f32); cw[0] = 0.5 / L
        put(key, _fm(cw, L // 128))
    put("altc", (((-1.0) ** np.arange(128)).astype(f32))[:, None])
    sh["vec"] = vec
    for l in range(NL):
        put(("bmod", l), _fm(g("b_mod")[l], 48))
    cols = []
    cols += list(range(OFF_K, OFF_K + 64)) * 2
    cols += list(range(OFF_K + 64, OFF_K + 128)) * 2
    cols += list(range(OFF_V, OFF_V + 128))
    cols += list(range(0, 512))
    cols += list(range(OFF_LX, OFF_LX + 128)) + list(range(OFF_LG, OFF_LG + 128))
    cols += list(range(OFF_LX + 128, OFF_LX + 256)) + list(range(OFF_LG + 128, OFF_LG + 256))
    cols += list(range(OFF_HY, OFF_HY + 768))
    cols = np.array(cols)
    def pieces(W, nk, nch):
        return np.ascontiguousarray(W.reshape(NL, nk, 128, nch, 128).transpose(0, 3, 2, 1, 4))
    sh["w_in_p"] = pieces(g("w_in")[:, :, cols], 8, 17)
    sh["w_out_p"] = pieces(g("w_out"), 8, 8)
    sh["w1_p"] = pieces(g("ffn_w1"), 8, NF)
    sh["w3_p"] = pieces(g("ffn_w3"), 8, NF)
    sh["w2_p"] = pieces(g("ffn_w2"), NF, 8)
    sh["w_mod_p"] = pieces(g("w_mod"), 8, 48)
    gw = np.zeros((NL, 128, 8, 128), f32)
    G = g("lru_gate_w")
    for l in range(NL):
        for d in range(2):
            for gt in range(2):
                for c in range(2):
                    idx = (d * 2 + gt) * 2 + c
                    for h in range(2):
                        gw[l, h * 64:(h + 1) * 64, idx, h * 64:(h + 1) * 64] = G[l, d, gt, 2 * c + h]
    sh["gw"] = gw
    sh["hf_w1"] = np.ascontiguousarray(g("hy_filt_w1"))
    sh["hf_w2"] = np.ascontiguousarray(g("hy_filt_w2"))
    sh["hf_w3"] = np.ascontiguousarray(np.concatenate([g("hy_filt_w3"), g("hy_filt_b3")[:, None, :]], axis=1))
    return sh


def _host_core(inp, c):
    f32 = np.float32
    m = {}
    xp = np.asarray(inp["x_prompt"], f32)[4 * c:4 * c + 4].reshape(T, 8, 128)
    m["xp"] = np.ascontiguousarray(xp.transpose(2, 1, 0))
    b = c % 2
    xs = np.asarray(inp["x_sample"], f32)[b].reshape(T, 8, 128)
    m["xs"] = np.ascontiguousarray(xs.transpose(2, 1, 0))
    ck = np.asarray(inp["cache_k"], f32)[b]
    ckT = np.zeros((NL, 2, 128, 256), f32)
    for g_ in range(2):
        kt = ck[:, :, g_, :].transpose(0, 2, 1)
        ckT[:, g_, :64] = kt; ckT[:, g_, 64:] = kt
    m["ckT"] = ckT
    cv = np.asarray(inp["cache_v"], f32)[b].reshape(NL, 2, 128, 128)
    m["cv"] = np.ascontiguousarray(cv.transpose(0, 2, 1, 3))
    pv = np.zeros((128, NPV), f32)
    cc = _fm(np.asarray(inp["c_ctx"], f32), 8); cb = _fm(np.asarray(inp["c"], f32)[b], 8)
    for k in range(8):
        pv[:, PV_COND + 2 * k] = cc[:, k]; pv[:, PV_COND + 2 * k + 1] = cb[:, k]
    st = np.asarray(inp["state_lru"], f32)[b]
    for l in range(NL):
        for d in range(2):
            pv[:, PV_H0 + (l * 2 + d) * 2:PV_H0 + (l * 2 + d) * 2 + 2] = _fm(st[l, d], 2)
    m["pvec"] = pv
    return m


class Res:
    __slots__ = ("w", "r", "name")
    def __init__(self, name=""):
        self.w = None; self.r = {}; self.name = name


class KB:
    def __init__(self, nc, es):
        self.nc = nc; self.es = es
        self.E = {"pe": nc.tensor, "dve": nc.vector, "act": nc.scalar, "pool": nc.gpsimd, "sp": nc.sync}
        self.semobj = {}
        self.cnt = {}
        for k in ("pe", "dve", "act", "pool"):
            self.semobj[k] = nc.alloc_semaphore(name="sem_" + k); self.cnt[k] = 0
        self.waited = {k: {} for k in self.E}
        self.banks = []; self.bres = []
        for i in range(8):
            self.banks.append(es.enter_context(nc.psum_tensor("bank%d" % i, [128, 512], F32)))
            self.bres.append(Res("bank%d" % i))
        self.bptr = {"A": 0, "B": 0}
        self.nsb = 0
        self.outsems = []

    def sb(self, name, shape, dt=F32):
        t = self.es.enter_context(self.nc.sbuf_tensor("s_" + name, list(shape), dt))
        return t

    def bank(self, pool):
        i = self.bptr[pool]; self.bptr[pool] = (i + 1) % 4
        j = i if pool == "A" else 4 + i
        return self.banks[j], self.bres[j]

    def deps_of(self, reads, writes):
        d = {}
        for r in reads:
            if r.w is not None:
                sk, v = r.w; d[sk] = max(d.get(sk, 0), v)
        for w in writes:
            if w.w is not None:
                sk, v = w.w; d[sk] = max(d.get(sk, 0), v)
            for sk, v in w.r.items(): d[sk] = max(d.get(sk, 0), v)
        return list(d.items())

    def _wait(self, e, deps):
        for sk, v in deps:
            if self.waited[e].get(sk, 0) >= v: continue
            self.E[e].wait_ge(self.semobj[sk], v)
            self.waited[e][sk] = v

    def op(self, e, fn, reads=(), writes=()):
        self._wait(e, self.deps_of(reads, writes))
        ins = fn(self.E[e])
        self.cnt[e] += 1
        ins.then_inc(self.semobj[e], 1)
        for r in reads: r.r[e] = self.cnt[e]
        for w in writes: w.w = (e, self.cnt[e]); w.r = {}
        return ins

    def mm(self, out, lhsT, rhs, start, stop, reads, bank, transpose=False):
        deps = [(s, v) for s, v in self.deps_of(reads, [bank]) if s != "pe"]
        self._wait("pe", deps)
        if transpose:
            ins = self.nc.tensor.transpose(out, lhsT, rhs)
        else:
            ins = self.nc.tensor.matmul(out, lhsT, rhs, start=start, stop=stop)
        self.cnt["pe"] += 1
        ins.then_inc(self.semobj["pe"], 1)
        for r in reads: r.r["pe"] = self.cnt["pe"]
        if stop:
            bank.w = ("pe", self.cnt["pe"]); bank.r = {}
        return ins

    def dma(self, q, out, in_, reads, writes, semkey):
        if semkey not in self.semobj:
            self.semobj[semkey] = self.nc.alloc_semaphore(name="dsem_" + semkey); self.cnt[semkey] = 0
        self._wait(q, self.deps_of(reads, writes))
        ins = self.E[q].dma_start(out=out, in_=in_)
        self.cnt[semkey] += 16
        ins.then_inc(self.semobj[semkey], 16)
        v = self.cnt[semkey]
        for r in reads: r.r[semkey] = v
        for w in writes: w.w = (semkey, v); w.r = {}
        return ins


class WStream:
    def __init__(self, kb, nslots=6, width=11 * 128):
        self.kb = kb; self.n = nslots; self.i = 0
        self.slots = [kb.sb("wslot%d" % i, [128, width], BF16) for i in range(nslots)]
        self.res = [Res("wslot%d" % i) for i in range(nslots)]

    def load(self, dram_piece, nk):
        i = self.i; self.i = (i + 1) % self.n
        s = self.slots[i]
        v = s[:, 0:nk * 128].rearrange("p (k n) -> p k n", k=nk)
        self.kb.dma("pool", v, dram_piece, [], [self.res[i]], "w%d" % i)
        return v, self.res[i]


def build_program(cfg):
    nc = bass.Bass("TRN2", target_bir_lowering=False)
    es = ExitStack()
    kb = KB(nc, es)
    NLAY = cfg["layers"]
    dbg_names = cfg.get("dbg", [])

    def din(name, shape, dt=F32):
        return nc.dram_tensor(name, list(shape), dt, kind="ExternalInput").ap()

    def dout(name, shape, dt=F32):
        return nc.dram_tensor(name, list(shape), dt, kind="ExternalOutput").ap()

    I = {}
    I["xp"] = din("xp", [128, 8, T]); I["xs"] = din("xs", [128, 8, T])
    I["ckT"] = din("ckT", [NL, 2, 128, 256]); I["cv"] = din("cv", [NL, 128, 2, 128])
    I["pvec"] = din("pvec", [128, NPV]); I["vec"] = din("vec", [128, NV])
    NLD = cfg.get("nld", NL)
    I["w_in_p"] = din("w_in_p", [NLD, 17, 128, 8, 128]); I["w_out_p"] = din("w_out_p", [NLD, 8, 128, 8, 128])
    I["w1_p"] = din("w1_p", [NLD, NF, 128, 8, 128]); I["w3_p"] = din("w3_p", [NLD, NF, 128, 8, 128])
    I["w2_p"] = din("w2_p", [NLD, 8, 128, NF, 128]); I["w_mod_p"] = din("w_mod_p", [NLD, cfg.get("nf", 48), 128, 8, 128])
    I["gw"] = din("gw", [NL, 128, 8, 128])
    I["hf_w1"] = din("hf_w1", [NL, 33, 64]); I["hf_w2"] = din("hf_w2", [NL, 64, 64]); I["hf_w3"] = din("hf_w3", [NL, 65, 1024])
    I["ident"] = din("ident", [128, 128]); I["rrot"] = din("rrot", [128, 128])
    I["ropec"] = din("ropec", [128, T]); I["ropes"] = din("ropes", [128, T])
    for L in (256, 1024):
        I["zT%d" % L] = din("zT%d" % L, [33, L]); I["dec%d" % L] = din("dec%d" % L, [128, L // 128, 2, 256])
        I["dftc%d" % L] = din("dftc%d" % L, [128, L // 128, L]); I["dfts%d" % L] = din("dfts%d" % L, [128, L // 128, L])
    I["altr"] = din("altr", [1, T]); I["altcin"] = din("altcin", [128, 1])
    O = {}
    O["yp"] = dout("yp", [128, 8, T]); O["ys"] = dout("ys", [128, 8, T])
    O["nk"] = dout("nk", [NL, 128, T]); O["nv"] = dout("nv", [NL, 128, T]); O["nst"] = dout("nst", [128, 64])
    DBG = {}

    def dbg(name, ap, res, shape):
        if name not in dbg_names: return
        o = dout("dbg_" + name, shape, ap.dtype)
        kb.dma("sp", o, ap, [res], [], "dbg_" + name)
        kb.outsems.append("dbg_" + name)

    sb = kb.sb
    x = sb("x", [128, 8, T]); x_r = [[Res("x%d_%d" % (c, h)) for h in range(2)] for c in range(8)]
    hT = sb("hT", [128, 8, T], BF16); hT_r = [[Res() for h in range(2)] for c in range(8)]
    mix_r = [[Res() for h in range(2)] for c in range(8)]
    ovl = sb("ovl", [128, 60 * 1024], mybir.dt.uint8)
    mixT = ovl[:, 0:16 * 1024].bitcast(BF16).rearrange("p (c t) -> p c t", c=8)
    vec = sb("vec", [128, NV]); pvec = sb("pvec", [128, NPV]); cres = Res("consts")
    ident = sb("ident", [128, 128]); rrot = sb("rrot", [128, 128])
    identb = sb("identb", [128, 128], BF16)
    ropec = sb("ropec", [128, T]); ropes = sb("ropes", [128, T])
    onesb = sb("onesb", [128, 128], BF16); onesbd = sb("onesbd", [128, 128], BF16)
    altc = sb("altcb", [128, 1], BF16); altr = sb("altrb", [1, T], BF16)
    dft = {256: (sb("dc256", [128, 2, 256], BF16), sb("ds256", [128, 2, 256], BF16)),
           1024: (sb("dc1024", [128, 8, 1024], BF16), sb("ds1024", [128, 8, 1024], BF16))}
    modv = sb("modv", [128, NL, 48, 2])
    modA = sb("modA", [128, NL, 2, 2, 8])
    sT = sb("sT", [128, 8, 2], BF16)
    stt = sb("stt", [128, 64])
    stt_r = Res("stt")
    gwb = sb("gwb", [128, 8, 128], BF16); gwb_r = Res("gwb")
    fw1 = sb("fw1", [33, 64]); fw2 = sb("fw2", [64, 64]); fw_r = Res("fw")
    rstd_t = [sb("rstd%d" % i, [128, 512]) for i in range(2)]; rstd_r = [Res("rstd%d" % i) for i in range(2)]; rstd_p = [0]
    dect = [sb("dect%d" % i, [128, 2, 256]) for i in range(2)]; dect_r = [Res("dect%d" % i) for i in range(2)]
    KN = sb("KN", [1, 512]); KN_r = Res("KN")
    cneg = sb("cneg", [128, 4]); cneg_r = Res("cneg")
    scr = [sb("scr%d" % i, [128, 512]) for i in range(6)]; scr_r = [Res("scr%d" % i) for i in range(6)]
    scb = [sb("scb%d" % i, [128, 512], BF16) for i in range(4)]; scb_r = [Res("scb%d" % i) for i in range(4)]
    scrp = [0]; scbp = [0]

    def S():
        i = scrp[0]; scrp[0] = (i + 1) % len(scr); return scr[i], scr_r[i]

    def SBF():
        i = scbp[0]; scbp[0] = (i + 1) % len(scb); return scb[i], scb_r[i]

    ws = WStream(kb)

    def carve(off, shape, dt):
        esz = 4 if dt == F32 else 2
        n = int(np.prod(shape[1:])) * esz
        v = ovl[:, off:off + n].bitcast(dt)
        if len(shape) == 3:
            v = v.rearrange("p (a b) -> p a b", a=shape[1])
        elif len(shape) == 4:
            v = v.rearrange("p (a b c) -> p a b c", a=shape[1], b=shape[2])
        return v, off + n

    V = lambda key, n=1: vec[:, VCOLS[key]:VCOLS[key] + n]

    def cload(q, dst, src):
        kb.dma(q, dst, src, [], [cres], "c0")
    cload("sp", vec[:], I["vec"]); cload("sp", pvec[:], I["pvec"])
    cload("sp", ident[:], I["ident"]); cload("sp", rrot[:], I["rrot"])
    cload("sp", ropec[:], I["ropec"]); cload("sp", ropes[:], I["ropes"])
    kb.dma("pool", identb[:], I["ident"], [], [cres], "c1")
    kb.dma("pool", altr[:], I["altr"], [], [cres], "c1")
    kb.dma("pool", dft[256][0][:], I["dftc256"], [], [cres], "c1")
    kb.dma("pool", dft[256][1][:], I["dfts256"], [], [cres], "c1")
    kb.dma("pool", altc[:], I["altcin"], [], [cres], "c1")
    cres2 = Res("c0all"); cres2.w = ("c0", kb.cnt["c0"])
    cres3 = Res("c1all"); cres3.w = ("c1", kb.cnt["c1"])
    CR = [cres2, cres3]
    kb.op("dve", lambda e: e.memset(onesb[:], 1.0), [], [cres])
    kb.op("dve", lambda e: e.memset(onesbd[:], 0.0), [], [cres])
    kb.op("dve", lambda e: e.memset(onesbd[0:64, 0:64], 1.0), [], [cres])
    kb.op("dve", lambda e: e.memset(onesbd[64:128, 64:128], 1.0), [], [cres])
    kb.op("dve", lambda e: e.memset(stt[:], 0.0), [], [stt_r])
    CR.append(cres)

    nmod = NLAY if cfg.get("stop") not in ("const",) else 0
    kb.op("act", lambda e: e.activation(out=sT[:].rearrange("p k j -> p (k j)"), in_=pvec[:, PV_COND:PV_COND + 16], func=AF.Silu), CR, [cres])
    mod_r = [Res("mod%d" % l) for l in range(NL)]

    def mod_piece(l, f):
        wv, wr = ws.load(I["w_mod_p"][l, f], 8)
        bk, br = kb.bank("A")
        for k in range(8):
            kb.mm(bk[:, 0:2], wv[:, k, :], sT[:, k, :], k == 0, k == 7, [wr, cres], br)
        kb.op("dve", lambda e: e.tensor_scalar(out=modv[:, l, f, :], in0=bk[:, 0:2], scalar1=V(("bmod", l), 48)[:, f:f + 1], scalar2=None, op0=ALU.add), [br] + CR, [mod_r[l]])

    def mod_finish(l, whs=(0, 1)):
        for j in range(2):
            for wh in whs:
                sc = modv[:, l, (8 if wh == 0 else 32):(16 if wh == 0 else 40), j]
                nw = V(("n1" if wh == 0 else "n2", l), 8)
                kb.op("dve", lambda e: e.scalar_tensor_tensor(out=modA[:, l, j, wh, :], in0=sc, scalar=1.0, in1=nw, op0=ALU.add, op1=ALU.mult), CR + [mod_r[l]], [mod_r[l]])

    pending_mod = []
    if nmod > 0:
        nf0 = cfg.get("nf", 48)
        for f in range(min(16, nf0)):
            mod_piece(0, f)
        mod_finish(0, (0,))
        pending_mod.extend(("p", 0, f) for f in range(16, nf0))
        pending_mod.append(("f", 0, (1,)))

    def run_pending_mod(n):
        for _ in range(n):
            if not pending_mod: return
            it = pending_mod.pop(0)
            if it[0] == "p": mod_piece(it[1], it[2])
            else: mod_finish(it[1], it[2])
    dbg("modv", modv[:].rearrange("p l f j -> p (l f j)"), mod_r[0], [128, NL * 96])

    def rms_bc(src_fn, src_res, nchunk, ones_t, inv_n, half_w=512):
        bk, br = kb.bank("B")
        for c in range(nchunk):
            sq, sqr = SBF()
            a, ar = src_fn(c)
            if nchunk > 1 and c % 2 == 0 and cfg.get("poolsq", True):
                kb.op("pool", lambda e: e.tensor_tensor(sq[:, 0:half_w], a, a, ALU.mult), [ar], [sqr])
            else:
                kb.op("act", lambda e: e.activation(out=sq[:, 0:half_w], in_=a, func=AF.Square), [ar], [sqr])
            kb.mm(bk[:, 0:half_w], ones_t[:], sq[:, 0:half_w], c == 0, c == nchunk - 1, [sqr, cres], br)
        i_ = rstd_p[0]; rstd_p[0] = 1 - i_
        s1, s1r = rstd_t[i_], rstd_r[i_]
        kb.op("act", lambda e: e.activation(out=s1[:, 0:half_w], in_=bk[:, 0:half_w], func=AF.Sqrt, bias=EPS, scale=inv_n), [br], [s1r])
        kb.op("dve", lambda e: e.reciprocal(s1[:, 0:half_w], s1[:, 0:half_w]), [s1r], [s1r])
        return s1, s1r

    def norm_mod(l, j, wh):
        shoff = 0 if wh == 0 else 24
        bks = [kb.bank("B"), kb.bank("B")]
        for c in range(8):
            for h in range(2):
                hs = slice(h * 512, (h + 1) * 512)
                sq, sqr = SBF()
                if (c + h) % 2 == 0:
                    kb.op("pool", lambda e: e.tensor_tensor(sq[:], x[:, c, hs], x[:, c, hs], ALU.mult), [x_r[c][h]], [sqr])
                else:
                    kb.op("act", lambda e: e.activation(out=sq[:], in_=x[:, c, hs], func=AF.Square), [x_r[c][h]], [sqr])
                kb.mm(bks[h][0][:, :], onesb[:], sq[:], c == 0, c == 7, [sqr, cres], bks[h][1])
        for h in range(2):
            kb.op("act", lambda e: e.activation(out=rstd_t[h][:], in_=bks[h][0][:, :], func=AF.Ln, bias=EPS, scale=1.0 / D), [bks[h][1]], [rstd_r[h]])
        for h in range(2):
            kb.op("act", lambda e: e.activation(out=rstd_t[h][:], in_=rstd_t[h][:], func=AF.Exp, scale=-0.5), [rstd_r[h]], [rstd_r[h]])
        for c in range(8):
            for h in range(2):
                hs = slice(h * 512, (h + 1) * 512)
                t1, t1r = S()
                kb.op("dve", lambda e: e.scalar_tensor_tensor(out=t1[:], in0=x[:, c, hs], scalar=modA[:, l, j, wh, c:c + 1], in1=rstd_t[h][:], op0=ALU.mult, op1=ALU.mult), [x_r[c][h], rstd_r[h], cres, mod_r[l]], [t1r])
                kb.op("act", lambda e: e.activation(out=hT[:, c, hs], in_=t1[:], func=AF.Identity, bias=modv[:, l, shoff + c, j:j + 1], scale=1.0), [t1r, cres, mod_r[l]], [hT_r[c][h]])

    def project(piece, nk, rhs_fn, pool="A"):
        wv, wr = ws.load(piece, nk)
        outs = []
        for h in range(2):
            bk, br = kb.bank(pool)
            for k in range(nk):
                a, ar = rhs_fn(k, h)
                kb.mm(bk[:, :], wv[:, k, :], a, k == 0, k == nk - 1, [wr, ar], br)
            outs.append((bk, br))
        run_pending_mod(2)
        return outs

    hT_rhs = lambda k, h: (hT[:, k, h * 512:(h + 1) * 512], hT_r[k][h])
    mix_rhs = lambda k, h: (mixT[:, k, h * 512:(h + 1) * 512], mix_r[k][h])

    def headnorm(l, bk, br, wkey):
        rstd, rr = rms_bc(lambda c: (bk[:, :], br), None, 1, onesbd, 1.0 / HD)
        o, o_r = S()
        kb.op("dve", lambda e: e.scalar_tensor_tensor(out=o[:], in0=bk[:, :], scalar=V((wkey, l)), in1=rstd[:], op0=ALU.mult, op1=ALU.mult), [br, rr, cres], [o_r])
        return o, o_r

    def rope(src, src_r, h, dst, dst_r):
        hs = slice(h * 512, (h + 1) * 512)
        bk, br = kb.bank("B")
        kb.mm(bk[:, :], rrot[:], src[:], True, True, [src_r, cres], br)
        t1, t1r = S()
        kb.op("dve", lambda e: e.tensor_tensor(t1[:], bk[:, :], ropes[:, hs], ALU.mult), [br, cres], [t1r])
        t2, t2r = S()
        kb.op("dve", lambda e: e.tensor_tensor(t2[:], src[:], ropec[:, hs], ALU.mult), [src_r, cres], [t2r])
        kb.op("dve", lambda e: e.tensor_tensor(dst, t1[:], t2[:], ALU.add), [t1r, t2r], [dst_r])

    def qk_prep(l, outs, wkey, dsts, dst_r, do_rope, extra=None):
        sq = []
        for h in range(2):
            t_, r_ = SBF(); sq.append((t_, r_))
            kb.op("act", lambda e: e.activation(out=t_[:], in_=outs[h][0][:, :], func=AF.Square), [outs[h][1]], [r_])
        rb = []
        for h in range(2):
            b_, br_ = kb.bank("B"); rb.append((b_, br_))
            kb.mm(b_[:, :], onesbd[:], sq[h][0][:], True, True, [sq[h][1], cres], br_)
        for h in range(2):
            kb.op("act", lambda e: e.activation(out=rstd_t[h][:], in_=rb[h][0][:, :], func=AF.Ln, bias=EPS, scale=1.0 / HD), [rb[h][1]], [rstd_r[h]])
        for h in range(2):
            kb.op("act", lambda e: e.activation(out=rstd_t[h][:], in_=rstd_t[h][:], func=AF.Exp, scale=-0.5), [rstd_r[h]], [rstd_r[h]])
        o = []
        for h in range(2):
            t_, r_ = S(); o.append((t_, r_))
            kb.op("dve", lambda e: e.scalar_tensor_tensor(out=t_[:], in0=outs[h][0][:, :], scalar=V((wkey, l)), in1=rstd_t[h][:], op0=ALU.mult, op1=ALU.mult), [outs[h][1], rstd_r[h], cres], [r_])
        if extra is not None:
            for h in range(2): extra(h, o[h][0], o[h][1])
        if not do_rope:
            for h in range(2):
                kb.op("act", lambda e: e.activation(out=dsts[h], in_=o[h][0][:], func=AF.Copy), [o[h][1]], [dst_r])
            return
        pb = []
        for h in range(2):
            b_, br_ = kb.bank("B"); pb.append((b_, br_))
            kb.mm(b_[:, :], rrot[:], o[h][0][:], True, True, [o[h][1], cres], br_)
        t1 = []; t2 = []
        for h in range(2):
            t_, r_ = S(); t1.append((t_, r_))
            kb.op("dve", lambda e: e.tensor_tensor(t_[:], pb[h][0][:, :], ropes[:, h * 512:(h + 1) * 512], ALU.mult), [pb[h][1], cres], [r_])
        for h in range(2):
            t_, r_ = S(); t2.append((t_, r_))
            kb.op("dve", lambda e: e.tensor_tensor(t_[:], o[h][0][:], ropec[:, h * 512:(h + 1) * 512], ALU.mult), [o[h][1], cres], [r_])
        for h in range(2):
            kb.op("dve", lambda e: e.tensor_tensor(dsts[h], t1[h][0][:], t2[h][0][:], ALU.add), [t1[h][1], t2[h][1]], [dst_r])

    def dwconv(out3, in3, res_out, res_in, center, taps, bias):
        L = in3.shape[-1]
        kb.op("act", lambda e: e.activation(out=out3, in_=in3, func=AF.Identity, bias=bias, scale=center), [res_in, cres], [res_out])
        for sh, wcol in taps:
            if sh < 0:
                o_ = out3[:, :, -sh:L]; i_ = in3[:, :, 0:L + sh]
            else:
                o_ = out3[:, :, 0:L - sh]; i_ = in3[:, :, sh:L]
            kb.op("dve", lambda e: e.scalar_tensor_tensor(out=o_, in0=i_, scalar=wcol, in1=o_, op0=ALU.mult, op1=ALU.add), [res_in, res_out, cres], [res_out])

    st = dict(nc=nc, kb=kb, es=es, I=I, O=O, x=x, x_r=x_r, hT=hT, hT_r=hT_r, mixT=mixT, mix_r=mix_r, ws=ws,
              V=V, CR=CR, cres=cres, S=S, SBF=SBF, carve=carve, dbg=dbg, norm_mod=norm_mod, project=project,
              hT_rhs=hT_rhs, mix_rhs=mix_rhs, mod_piece=mod_piece, mod_finish=mod_finish, mod_r=mod_r, rstd_t=rstd_t, rstd_r=rstd_r, run_pending_mod=run_pending_mod, headnorm=headnorm, rope=rope, qk_prep=qk_prep, dwconv=dwconv, modv=modv, pvec=pvec,
              ident=ident, identb=identb, onesb=onesb, altc=altc, altr=altr, dft=dft, stt=stt, stt_r=stt_r,
              gwb=gwb, gwb_r=gwb_r, fw1=fw1, fw2=fw2, fw_r=fw_r, cfg=cfg, NLAY=NLAY, ovl=ovl, dect=dect, dect_r=dect_r,
              KN=KN, KN_r=KN_r, cneg=cneg, cneg_r=cneg_r, ropec=ropec, ropes=ropes, sb=sb)
    return st


class _Stop(Exception):
    pass


def emit_passes(st):
    nc = st["nc"]; kb = st["kb"]; I = st["I"]; O = st["O"]; x = st["x"]; x_r = st["x_r"]
    hT = st["hT"]; hT_r = st["hT_r"]; mixT = st["mixT"]; mix_r = st["mix_r"]; ws = st["ws"]
    V = st["V"]; CR = st["CR"]; cres = st["cres"]; S = st["S"]; SBF = st["SBF"]; carve = st["carve"]; dbg = st["dbg"]
    project = st["project"]; hT_rhs = st["hT_rhs"]; mix_rhs = st["mix_rhs"]; headnorm = st["headnorm"]; rope = st["rope"]
    dwconv = st["dwconv"]; modv = st["modv"]; pvec = st["pvec"]; ident = st["ident"]; identb = st["identb"]
    onesb = st["onesb"]; altc = st["altc"]; altr = st["altr"]; dft = st["dft"]; stt = st["stt"]; stt_r = st["stt_r"]
    gwb = st["gwb"]; gwb_r = st["gwb_r"]; fw1 = st["fw1"]; fw2 = st["fw2"]; fw_r = st["fw_r"]
    dect = st["dect"]; dect_r = st["dect_r"]; KN = st["KN"]; KN_r = st["KN_r"]; cneg = st["cneg"]; cneg_r = st["cneg_r"]
    cfg = st["cfg"]; NLAY = st["NLAY"]
    WOFF = 16 * 1024; FOFF = 44 * 1024

    live = {"M": [], "W": [], "F": [], "G": []}

    def pe_warm(n):
        for i in range(n):
            bk, br = kb.bank("A")
            kb.mm(bk[:, :], onesb[:], hT[:, 0, 0:512], True, True, CR, br)

    STAGES = ["const", "mod", "filt", "norm", "attn", "lru", "hy", "wout"]

    def chk(name):
        sp = cfg.get("stop")
        if sp is not None and STAGES.index(name) >= STAGES.index(sp): raise _Stop()

    def merge_into(new_list, old_lists):
        ev = {}
        for ol in old_lists:
            for r in ol:
                if r.w is not None: ev[r.w[0]] = max(ev.get(r.w[0], 0), r.w[1])
                for sk, v in r.r.items(): ev[sk] = max(ev.get(sk, 0), v)
        for r in new_list:
            r.w = None; r.r = dict(ev)

    def w_phase(new_list):
        merge_into(new_list, [live["W"]]); live["W"] = list(new_list)

    allmix = [r for c in mix_r for r in c]
    live["M"] = allmix

    def run_pass(grp):
        P = grp == "P"; j = 0 if P else 1
        nseq, L = (4, 256) if P else (1, 1024); nP = L // 128
        KT = T if P else T + 256
        Cq, Sq = dft[L]
        xin = I["xp"] if P else I["xs"]
        for c in range(8):
            kb.dma("sp", x[:, c, :], xin[:, c, :], [], [x_r[c][0], x_r[c][1]], "xin%d" % c)
        if cfg["sample"] and "c2" not in kb.cnt:
            c2w = Res("c2w")
            kb.dma("pool", dft[1024][0][:], I["dftc1024"], [], [c2w], "c2")
            kb.dma("pool", dft[1024][1][:], I["dfts1024"], [], [c2w], "c2")
            st["c2all"] = Res("c2all"); st["c2all"].w = ("c2", kb.cnt["c2"])
        if not P:
            CR.append(st["c2all"])

        FIF = cfg.get("filt_in_ffn", True)

        def gen_filters(l, ffn_mode, oldG):
            At, _ = carve(FOFF, [128, nP, 512], BF16); Bt, _ = carve(FOFF + nP * 1024, [128, nP, 512], BF16)
            At_r = Res("At"); Bt_r = Res("Bt")
            merge_into([At_r, Bt_r], [live["F"]]); live["F"] = [At_r, Bt_r]
            st["tab_r"] = (At_r, Bt_r)
            if not ffn_mode:
                o_ = WOFF
                h2a = st["ovl"][0:65, o_:o_ + L * 2].bitcast(BF16); o_ += L * 4
                fw3 = st["ovl"][0:65, o_:o_ + 2048].bitcast(BF16); o_ += 4096
                Pq, o_ = carve(o_, [128, nP, 512], BF16); Qq, o_ = carve(o_, [128, nP, 512], BF16)
                h2a_r = Res("h2a"); fw3_r = Res("fw3"); Pq_r = Res("Pq"); Qq_r = Res("Qq")
                merge_into([h2a_r, fw3_r, Pq_r, Qq_r], [live["W"], oldG]); live["W"] = [h2a_r, fw3_r, Pq_r, Qq_r]
            else:
                h2a = st["rstd_t"][0][0:65, :].bitcast(BF16)[:, 0:L]
                fw3 = st["rstd_t"][1][0:65, :].bitcast(BF16)
                hTf = hT[:].rearrange("p c t -> p (c t)")
                Pq = hTf[:, 0:nP * 512].rearrange("p (a b) -> p a b", a=nP)
                Qq = hTf[:, 4096:4096 + nP * 512].rearrange("p (a b) -> p a b", a=nP)
                h2a_r = st["rstd_r"][0]; fw3_r = st["rstd_r"][1]; Pq_r = Res("PqF"); Qq_r = Res("QqF")
                allh = [r for c_ in hT_r for r in c_]
                merge_into([Pq_r, Qq_r], [allh])
            kb.dma("sp", fw1[:], I["hf_w1"][l], [], [fw_r], "fw")
            kb.dma("sp", fw2[:], I["hf_w2"][l], [], [fw_r], "fw")
            kb.dma("pool", fw3, I["hf_w3"][l], [], [fw3_r], "fw3")
            fwa = Res("fwall"); fwa.w = ("fw", kb.cnt["fw"])
            kb.op("dve", lambda e: e.memset(h2a[64:65, :], 1.0), [], [h2a_r])

            def sin_evac(ps, br, bcol, dst, dst_r, w):
                xx, xr = S()
                kb.op("dve", lambda e: e.tensor_scalar(out=xx[0:64, 0:w], in0=ps, scalar1=bcol, scalar2=None, op0=ALU.add), [br, cres], [xr])
                m1, m1r = S()
                kb.op("dve", lambda e: e.tensor_scalar(out=m1[0:64, 0:w], in0=xx[0:64, 0:w], scalar1=PI, scalar2=-2 * PI, op0=ALU.is_gt, op1=ALU.mult), [xr], [m1r])
                m2, m2r = S()
                kb.op("dve", lambda e: e.tensor_scalar(out=m2[0:64, 0:w], in0=xx[0:64, 0:w], scalar1=-PI, scalar2=2 * PI, op0=ALU.is_lt, op1=ALU.mult), [xr], [m2r])
                kb.op("dve", lambda e: e.tensor_tensor(xx[0:64, 0:w], xx[0:64, 0:w], m1[0:64, 0:w], ALU.add), [xr, m1r], [xr])
                kb.op("dve", lambda e: e.tensor_tensor(xx[0:64, 0:w], xx[0:64, 0:w], m2[0:64, 0:w], ALU.add), [xr, m2r], [xr])
                kb.op("dve", lambda e: e.tensor_scalar(out=xx[0:64, 0:w], in0=xx[0:64, 0:w], scalar1=3.1415925, scalar2=-3.1415925, op0=ALU.min, op1=ALU.max), [xr], [xr])
                kb.op("act", lambda e: e.activation(out=dst, in_=xx[0:64, 0:w], func=AF.Sin), [xr], [dst_r])

            for cb in range((L + 511) // 512):
                w = min(512, L - cb * 512)
                zt, ztr = S()
                kb.dma("sp", zt[0:33, 0:w], I["zT%d" % L][:, cb * 512:cb * 512 + w], [], [ztr], "zt")
                bk, br = kb.bank("A")
                kb.mm(bk[0:64, 0:w], fw1[0:33, 0:64], zt[0:33, 0:w], True, True, [fwa, ztr], br)
                h1, h1r = S()
                sin_evac(bk[0:64, 0:w], br, V(("fb1", l))[0:64, :], h1[0:64, 0:w], h1r, w)
                yield
                bk2, br2 = kb.bank("A")
                kb.mm(bk2[0:64, 0:w], fw2[0:64, 0:64], h1[0:64, 0:w], True, True, [fwa, h1r], br2)
                sin_evac(bk2[0:64, 0:w], br2, V(("fb2", l))[0:64, :], h2a[0:64, cb * 512:cb * 512 + w], h2a_r, w)
                yield
            for pc in range(nP):
                di = pc % 2
                kb.dma("sp", dect[di][:], I["dec%d" % L][:, pc], [], [dect_r[di]], "dec%d" % di)
                for ob in range(2):
                    bk, br = kb.bank("A")
                    kb.mm(bk[:, :], h2a[0:65, pc * 128:(pc + 1) * 128], fw3[0:65, ob * 512:(ob + 1) * 512], True, True, [h2a_r, fw3_r], br)
                    f_, fr = S()
                    kb.op("dve", lambda e: e.tensor_tensor(f_[:], bk[:, :], dect[di][:].rearrange("p a b -> p (a b)"), ALU.mult), [br, dect_r[di]], [fr])
                    kb.op("dve", lambda e: e.tensor_tensor(Pq[:, pc, ob * 256:(ob + 1) * 256], f_[:, 0:256], f_[:, 256:512], ALU.add), [fr], [Pq_r])
                    kb.op("dve", lambda e: e.tensor_tensor(Qq[:, pc, ob * 256:(ob + 1) * 256], f_[:, 256:512], f_[:, 0:256], ALU.subtract), [fr], [Qq_r])
                yield
            cwk = "cw%d" % L
            for m in range(nP):
                for (tab, tq, src, src_r, dst, dst_r) in ((Cq, 0, Pq, Pq_r, At, At_r), (Sq, 1, Qq, Qq_r, Bt, Bt_r)):
                    bk, br = kb.bank("A")
                    for pc in range(nP):
                        kb.mm(bk[:, :], tab[:, pc, m * 128:(m + 1) * 128], src[:, pc, :], pc == 0, pc == nP - 1, [src_r] + CR, br)
                    kb.op("act", lambda e: e.activation(out=dst[:, m, :], in_=bk[:, :], func=AF.Identity, scale=V(cwk, nP)[:, m:m + 1]), [br] + CR, [dst_r])
                yield
            bk, br = kb.bank("A")
            for pc in range(nP):
                kb.mm(bk[0:1, :], altc[:, 0:1], Pq[:, pc, :], pc == 0, pc == nP - 1, [Pq_r] + CR, br)
            kb.op("act", lambda e: e.activation(out=KN[0:1, :], in_=bk[0:1, :], func=AF.Identity, scale=0.5 / L), [br], [KN_r])
            if l == 0:
                dbg("At" + grp, At[:].rearrange("p a b -> p (a b)"), At_r, [128, nP * 512])
                dbg("KN" + grp, KN[:], KN_r, [1, 512])

            if ffn_mode:
                merge_into(allh, [[Pq_r, Qq_r]])

        for l in range(NLAY):
            oldG = live["G"]; live["G"] = []
            merge_into(allmix, [oldG])
            At, _ = carve(FOFF, [128, nP, 512], BF16); Bt, _ = carve(FOFF + nP * 1024, [128, nP, 512], BF16)
            if l == 0 or not FIF:
                for _ in gen_filters(l, False, oldG): pass
            else:
                live["W"] = list(oldG)
            At_r, Bt_r = st["tab_r"]
            kb.dma("pool", gwb[:], I["gw"][l], [], [gwb_r], "gw")
            chk("filt")
            kb.op("act", lambda e: e.activation(out=cneg[:], in_=V(("lam", l), 4), func=AF.Exp, scale=-1.0), CR, [cneg_r])
            kb.op("act", lambda e: e.activation(out=cneg[:], in_=cneg[:], func=AF.Ln, bias=1.0, scale=1.0), [cneg_r], [cneg_r])
            kb.op("dve", lambda e: e.tensor_scalar(out=cneg[:], in0=cneg[:], scalar1=-8.0, scalar2=None, op0=ALU.mult), [cneg_r], [cneg_r])

            st["norm_mod"](l, j, 0)
            if l == 0: dbg("hT" + grp, hT[:, 0, :], hT_r[0][1], [128, T])

            chk("norm")
            o_ = WOFF
            qT, o_ = carve(o_, [128, 4, T], BF16); kdT, o_ = carve(o_, [128, 2, KT], BF16)
            Vtok, o_ = carve(o_, [128, KT // 128, 256], BF16); vT, o_ = carve(o_, [128, T], F32); kst, o_ = carve(o_, [128, T], F32)
            q_r = [Res("q%d" % c) for c in range(4)]; kd_r = [Res("kd0"), Res("kd1")]; vt_r = Res("Vtok"); vT_r = Res("vT"); kst_r = Res("kst")
            w_phase(q_r + kd_r + [vt_r, vT_r, kst_r])
            koff = 0 if P else 256
            Vt4 = Vtok.rearrange("p k (g e) -> p k g e", g=2)
            kb.op("dve", lambda e: e.memset(Vtok[:], 1.0), [], [vt_r])
            if not P:
                for g in range(2):
                    kb.dma("pool", kdT[:, g, 0:256], I["ckT"][l, g], [], [kd_r[g]], "ck%d" % g)
                kb.dma("pool", Vt4[:, 0:2, :, 0:64], I["cv"][l].rearrange("p c (g d) -> p c g d", g=2), [vt_r], [vt_r], "cv")
            for g in range(2):
                outs = project(I["w_in_p"][l, g], 8, hT_rhs)
                if P:
                    def kextra(h, o_, o_r_, g=g):
                        kb.op("act", lambda e: e.activation(out=kst[g * 64:(g + 1) * 64, h * 512:(h + 1) * 512], in_=o_[0:64, :], func=AF.Copy), [o_r_], [kst_r])
                    st["qk_prep"](l, outs, "kn", [kdT[:, g, 0:512], kdT[:, g, 512:1024]], kd_r[g], False, kextra)
                else:
                    st["qk_prep"](l, outs, "kn", [kdT[:, g, 256:768], kdT[:, g, 768:1280]], kd_r[g], True)
            if P:
                kb.dma("sp", O["nk"][l], kst[:], [kst_r], [], "onk");
                if "onk" not in kb.outsems: kb.outsems.append("onk")
            outs = project(I["w_in_p"][l, 2], 8, hT_rhs)
            for h, (bk, br) in enumerate(outs):
                kb.op("act", lambda e: e.activation(out=vT[:, h * 512:(h + 1) * 512], in_=bk[:, :], func=AF.Copy), [br], [vT_r])
            if P:
                kb.dma("sp", O["nv"][l], vT[:], [vT_r], [], "onv")
                if "onv" not in kb.outsems: kb.outsems.append("onv")
            for tb4 in range(2):
                bk, br = kb.bank("B")
                for i4 in range(4):
                    tb = tb4 * 4 + i4
                    kb.mm(bk[:, i4 * 128:(i4 + 1) * 128], vT[:, tb * 128:(tb + 1) * 128], ident[:], True, True, [vT_r] + CR, br, transpose=True)
                for g_ in range(2):
                    kb.op("dve", lambda e: e.tensor_copy(Vt4[:, koff // 128 + tb4 * 4:koff // 128 + tb4 * 4 + 4, g_, 0:64], bk[:, :].rearrange("p (a g d) -> p a g d", a=4, g=2)[:, :, g_, :]), [br, vt_r], [vt_r])
            if l == 0:
                dbg("kdT" + grp, kdT[:, 0, :], kd_r[0], [128, KT])

            for c in range(4):
                outs = project(I["w_in_p"][l, 3 + c], 8, hT_rhs)
                st["qk_prep"](l, outs, "qn", [qT[:, c, 0:512], qT[:, c, 512:1024]], q_r[c], not P)
            for c in range(4):
                g = c // 2
                NI = 2 if P else KT // 128
                items = [(qh, i) for qh in range(2) for i in range(NI)]
                Sb = {}
                pair = {}

                def emit_S(n):
                    qh, i = items[n]
                    bks = [kb.bank("A"), kb.bank("A")]
                    if P:
                        s_ = qh * 2 + i
                        for kc in range(2):
                            for ph in range(2):
                                ps_ = slice(ph * 64, (ph + 1) * 64)
                                kb.mm(bks[ph][0][:, kc * 256:(kc + 1) * 256], kdT[ps_, g, s_ * 256 + kc * 128:s_ * 256 + (kc + 1) * 128], qT[ps_, c, s_ * 256:(s_ + 1) * 256], True, True, [kd_r[g], q_r[c]], bks[ph][1])
                    else:
                        for ph in range(2):
                            ps_ = slice(ph * 64, (ph + 1) * 64)
                            kb.mm(bks[ph][0][:, :], kdT[ps_, g, i * 128:(i + 1) * 128], qT[ps_, c, qh * 512:(qh + 1) * 512], True, True, [kd_r[g], q_r[c]], bks[ph][1])
                    Sb[n] = bks

                def emit_PV(n):
                    qh, i = items[n]
                    if i == 0:
                        pair[qh] = [kb.bank("B"), kb.bank("B")]
                    bks = Sb.pop(n)
                    pts = []
                    for ph in range(2):
                        pt, ptr = SBF(); pts.append((pt, ptr))
                        kb.op("act", lambda e: e.activation(out=pt[:], in_=bks[ph][0][:, :], func=AF.Exp, scale=0.125), [bks[ph][1]], [ptr])
                    for ph in range(2):
                        nb, nbr = pair[qh][ph]
                        pt, ptr = pts[ph]
                        if P:
                            s_ = qh * 2 + i
                            for kc in range(2):
                                kb.mm(nb[:, i * 256:(i + 1) * 256], Vt4[:, 2 * s_ + kc, g, :], pt[:, kc * 256:(kc + 1) * 256], kc == 0, kc == 1, [vt_r, ptr], nbr)
                        else:
                            kb.mm(nb[:, :], Vt4[:, i, g, :], pt[:], i == 0, i == NI - 1, [vt_r, ptr], nbr)
                    if i == NI - 1:
                        for ph in range(2):
                            nb, nbr = pair[qh][ph]
                            rc, rcr = S()
                            if P or cfg.get("actrc", False):
                                kb.op("act", lambda e: e.activation(out=rc[0:64, :], in_=nb[64:128, :], func=AF.Ln), [nbr], [rcr])
                                kb.op("act", lambda e: e.activation(out=rc[0:64, :], in_=rc[0:64, :], func=AF.Exp, scale=-1.0), [rcr], [rcr])
                            else:
                                kb.op("dve", lambda e: e.reciprocal(rc[0:64, :], nb[64:128, :]), [nbr], [rcr])
                            kb.op("dve", lambda e: e.tensor_tensor(mixT[ph * 64:(ph + 1) * 64, c, qh * 512:(qh + 1) * 512], nb[0:64, :], rc[0:64, :], ALU.mult), [nbr, rcr], [mix_r[c][qh]])

                LA = 1
                for n in range(len(items) + LA):
                    if n < len(items): emit_S(n)
                    if n - LA >= 0: emit_PV(n - LA)

            if l == 0:
                for cdbg in range(4):
                    dbg("att%d" % cdbg + grp, mixT[:, cdbg, :], mix_r[cdbg][1], [128, T])
            chk("attn")
            for c in range(2):
                o_ = WOFF
                lx, o_ = carve(o_, [128, T], F32); xc, o_ = carve(o_, [128, T], F32); xcb, o_ = carve(o_, [128, T], BF16)
                gl, o_ = carve(o_, [128, T], BF16); rg, o_ = carve(o_, [128, T], F32); ig, o_ = carve(o_, [128, T], F32)
                hd0, o_ = carve(o_, [128, T], F32); hd1, o_ = carve(o_, [128, T], F32)
                lx_r = Res("lx"); xc_r = Res("xc"); xcb_r = Res("xcb"); gl_r = Res("gl"); rg_r = Res("rg"); ig_r = Res("ig"); hd_r = [Res("hd0"), Res("hd1")]
                w_phase([lx_r, xc_r, xcb_r, gl_r, rg_r, ig_r] + hd_r)
                hd = [hd0, hd1]
                outs = project(I["w_in_p"][l, 7 + 2 * c], 8, hT_rhs)
                for h, (bk, br) in enumerate(outs):
                    kb.op("act", lambda e: e.activation(out=lx[:, h * 512:(h + 1) * 512], in_=bk[:, :], func=AF.Copy), [br], [lx_r])
                outs = project(I["w_in_p"][l, 8 + 2 * c], 8, hT_rhs)
                for h, (bk, br) in enumerate(outs):
                    hs = slice(h * 512, (h + 1) * 512)
                    a1, a1r = S()
                    kb.op("act", lambda e: e.activation(out=a1[:], in_=bk[:, :], func=AF.Square), [br], [a1r])
                    kb.op("dve", lambda e: e.tensor_scalar(out=a1[:], in0=a1[:], scalar1=0.044715, scalar2=1.0, op0=ALU.mult, op1=ALU.add), [a1r], [a1r])
                    kb.op("dve", lambda e: e.tensor_tensor(a1[:], a1[:], bk[:, :], ALU.mult), [a1r, br], [a1r])
                    kb.op("act", lambda e: e.activation(out=a1[:], in_=a1[:], func=AF.Sigmoid, scale=1.5957691216), [a1r], [a1r])
                    kb.op("dve", lambda e: e.tensor_tensor(gl[:, hs], a1[:], bk[:, :], ALU.mult), [a1r, br], [gl_r])
                v3 = lambda a: a.rearrange("p (s t) -> p s t", s=nseq)
                lcw = V(("lcw", l), 8)
                dwconv(v3(xc), v3(lx), xc_r, lx_r, lcw[:, 4 + c:5 + c],
                       [(-2, lcw[:, 0 + c:1 + c]), (-1, lcw[:, 2 + c:3 + c]), (1, lcw[:, 6 + c:7 + c])], V(("lcb", l), 2)[:, c:c + 1])
                kb.op("act", lambda e: e.activation(out=xcb, in_=xc, func=AF.Copy), [xc_r], [xcb_r])
                if l == 0 and c == 0: dbg("xc" + grp, xc, xc_r, [128, T])
                for d in range(2):
                    for gt, dst, dst_r in ((0, rg, rg_r), (1, ig, ig_r)):
                        idx = (d * 2 + gt) * 2 + c
                        for h in range(2):
                            hs = slice(h * 512, (h + 1) * 512)
                            bk, br = kb.bank("A")
                            kb.mm(bk[:, :], gwb[:, idx, :], xcb[:, hs], True, True, [gwb_r, xcb_r], br)
                            kb.op("act", lambda e: e.activation(out=dst[:, hs], in_=bk[:, :], func=AF.Sigmoid, bias=V(("lgb", l), 8)[:, idx:idx + 1], scale=1.0), [br] + CR, [dst_r])
                    kb.op("act", lambda e: e.activation(out=rg, in_=rg, func=AF.Exp, scale=cneg[:, d * 2 + c:d * 2 + c + 1]), [rg_r, cneg_r], [rg_r])
                    kb.op("dve", lambda e: e.tensor_tensor(ig, ig, xc, ALU.mult), [ig_r, xc_r], [ig_r])
                    for h in range(2):
                        hs = slice(h * 512, (h + 1) * 512)
                        a2, a2r = S()
                        kb.op("act", lambda e: e.activation(out=a2[:], in_=rg[:, hs], func=AF.Square), [rg_r], [a2r])
                        kb.op("act", lambda e: e.activation(out=a2[:], in_=a2[:], func=AF.Sqrt, bias=1.0, scale=-1.0), [a2r], [a2r])
                        kb.op("dve", lambda e: e.tensor_tensor(ig[:, hs], ig[:, hs], a2[:], ALU.mult), [ig_r, a2r], [ig_r])
                    for s in range(nseq):
                        lo, hi = s * L, (s + 1) * L
                        init = 0.0 if P else pvec[:, PV_H0 + (l * 2 + d) * 2 + c:PV_H0 + (l * 2 + d) * 2 + c + 1]
                        if d == 0:
                            kb.op("dve", lambda e: e.tensor_tensor_scan(hd[d][:, lo:hi], rg[:, lo:hi], ig[:, lo:hi], init, ALU.mult, ALU.add), [rg_r, ig_r] + CR, [hd_r[d]])
                        else:
                            rv = (lambda a: a[:, hi - 1::-1]) if lo == 0 else (lambda a: a[:, hi - 1:lo - 1:-1])
                            kb.op("dve", lambda e: e.tensor_tensor_scan(rv(hd[d]), rv(rg), rv(ig), init, ALU.mult, ALU.add), [rg_r, ig_r] + CR, [hd_r[d]])
                    if P:
                        col = ((l * 2 + d) * 2 + c) * 4
                        src = v3(hd[d])[:, :, L - 1] if d == 0 else v3(hd[d])[:, :, 0]
                        kb.op("act", lambda e: e.activation(out=stt[:, col:col + 4], in_=src, func=AF.Copy), [hd_r[d]], [stt_r])
                if l == 0 and c == 0: dbg("hf" + grp, hd0, hd_r[0], [128, T]); dbg("hb" + grp, hd1, hd_r[1], [128, T])
                kb.op("dve", lambda e: e.tensor_tensor(hd0, hd0, hd1, ALU.add), hd_r, [hd_r[0]])
                for h in range(2):
                    hs = slice(h * 512, (h + 1) * 512)
                    kb.op("dve", lambda e: e.tensor_tensor(mixT[:, 4 + c, hs], hd0[:, hs], gl[:, hs], ALU.mult), [hd_r[0], gl_r], [mix_r[4 + c][h]])

            chk("lru")
            o_ = WOFF
            hv, o_ = carve(o_, [128, 2, T], BF16); hx1, o_ = carve(o_, [128, 2, T], BF16); hx2, o_ = carve(o_, [128, 2, T], BF16)
            o_s2 = o_
            raws = []; cvts = []
            for i2 in range(2):
                r_, o_ = carve(o_, [128, T], F32); c_, o_ = carve(o_, [128, T], F32)
                raws.append((r_, Res("raw%d" % i2))); cvts.append((c_, Res("cvt%d" % i2)))
            hv_r = Res("hv"); hx1_r = Res("hx1"); hx2_r = Res("hx2")
            w_phase([hv_r, hx1_r, hx2_r] + [r for _, r in raws] + [r for _, r in cvts])
            v3 = lambda a: a.rearrange("p (s t) -> p s t", s=nseq)
            hcw = V(("hcw", l), 18)
            for i in range(6):
                raw, raw_r = raws[i % 2]; cvt, cvt_r = cvts[i % 2]
                dst, dst_r = ((hv, hv_r), (hx1, hx1_r), (hx2, hx2_r))[i // 2]
                outs = project(I["w_in_p"][l, 11 + i], 8, hT_rhs)
                for h, (bk, br) in enumerate(outs):
                    hs = slice(h * 512, (h + 1) * 512)
                    kb.op("act", lambda e: e.activation(out=raw[:, hs], in_=bk[:, :], func=AF.Copy), [br], [raw_r])
                    kb.op("dve", lambda e: e.tensor_scalar(out=cvt[:, hs], in0=raw[:, hs], scalar1=hcw[:, 6 + i:7 + i], scalar2=None, op0=ALU.mult), [raw_r, cres], [cvt_r])
                r3 = v3(raw); c3 = v3(cvt); d3 = v3(dst[:, i % 2, :])
                kb.op("dve", lambda e: e.scalar_tensor_tensor(out=c3[:, :, 1:L], in0=r3[:, :, 0:L - 1], scalar=hcw[:, i:i + 1], in1=c3[:, :, 1:L], op0=ALU.mult, op1=ALU.add), [raw_r, cvt_r, cres], [cvt_r])
                kb.op("dve", lambda e: e.scalar_tensor_tensor(out=c3[:, :, 0:L - 1], in0=r3[:, :, 1:L], scalar=hcw[:, 12 + i:13 + i], in1=c3[:, :, 0:L - 1], op0=ALU.mult, op1=ALU.add), [raw_r, cvt_r, cres], [cvt_r])
                kb.op("act", lambda e: e.activation(out=dst[:, i % 2, :], in_=cvt, func=AF.Copy), [cvt_r], [dst_r])
            raw_r = raws[0][1]; cvt_r = raws[1][1]; raw_r2 = cvts[0][1]; cvt_r2 = cvts[1][1]
            if l == 0: dbg("hv" + grp, hv[:, 0, :], hv_r, [128, T])
            tok, o_ = carve(o_s2, [128, nP, nseq * 256], BF16); Zr, o_ = carve(o_, [128, nP, nseq * 256], BF16); Zs, o_ = carve(o_, [128, nP, nseq * 256], BF16)
            tok_r = Res("tok"); Zr_r = Res("Zr"); Zs_r = Res("Zs")
            merge_into([tok_r, Zr_r, Zs_r], [[raw_r, cvt_r, raw_r2, cvt_r2]]); live["W"] = [hv_r, hx1_r, hx2_r, tok_r, Zr_r, Zs_r]
            ZN = st.setdefault("ZN", None)
            if ZN is None:
                ZN = st["sb"]("ZN", [1, 1024], BF16); st["ZN"] = ZN; st["ZN_r"] = Res("ZN")
            ZN_r = st["ZN_r"]
            NCOL = nseq * 256; GW = min(512, NCOL); NG = NCOL // GW
            TW = min(L, 512); NTH = L // TW

            PWE = cfg.get("pwe", "pool")

            def longconv(o, u, u_r, epilogue):
                for s in range(nseq):
                    for tc in range(nP):
                        bk, br = kb.bank("B")
                        bkb = bk[:, :].bitcast(BF16)
                        for cc in range(2):
                            kb.mm(bkb[:, cc * 128:(cc + 1) * 128], u[:, cc, s * L + tc * 128:s * L + (tc + 1) * 128], identb[:], True, True, [u_r] + CR, br, transpose=True)
                        kb.op("act", lambda e: e.activation(out=tok[:, tc, s * 256:(s + 1) * 256], in_=bkb[:, 0:256], func=AF.Copy), [br], [tok_r])
                for gi in range(NG):
                    gs = slice(gi * GW, (gi + 1) * GW)
                    un, unr = kb.bank("B")
                    for tc in range(nP):
                        kb.mm(un[0:1, 0:GW], altc[:, 0:1], tok[:, tc, gs], tc == 0, tc == nP - 1, [tok_r] + CR, unr)
                    for si in range(GW // 256):
                        zc = gi * GW + si * 256
                        kb.op("dve", lambda e: e.tensor_tensor(ZN[0:1, zc:zc + 256], un[0:1, si * 256:(si + 1) * 256], KN[0:1, o * 256:(o + 1) * 256], ALU.mult), [unr, KN_r], [ZN_r])
                for m in range(nP):
                    for gi in range(NG):
                        gs = slice(gi * GW, (gi + 1) * GW)
                        ur, urr = kb.bank("A"); us, usr = kb.bank("B")
                        for tc in range(nP):
                            kb.mm(ur[:, 0:GW], Cq[:, tc, m * 128:(m + 1) * 128], tok[:, tc, gs], tc == 0, tc == nP - 1, [tok_r] + CR, urr)
                        for tc in range(nP):
                            kb.mm(us[:, 0:GW], Sq[:, tc, m * 128:(m + 1) * 128], tok[:, tc, gs], tc == 0, tc == nP - 1, [tok_r] + CR, usr)
                        for si in range(GW // 256):
                            cs = slice(si * 256, (si + 1) * 256); zc = gi * GW + si * 256
                            A_ = At[:, m, o * 256:(o + 1) * 256]; B_ = Bt[:, m, o * 256:(o + 1) * 256]
                            t1, t1r = S(); t2, t2r = S(); t3, t3r = S(); t4, t4r = S()
                            kb.op("dve", lambda e: e.tensor_tensor(t1[:, 0:256], ur[:, cs], A_, ALU.mult), [urr, At_r], [t1r])
                            kb.op("dve", lambda e: e.tensor_tensor(t2[:, 0:256], us[:, cs], B_, ALU.mult), [usr, Bt_r], [t2r])
                            kb.op(PWE, lambda e: e.tensor_tensor(Zr[:, m, zc:zc + 256], t1[:, 0:256], t2[:, 0:256], ALU.add), [t1r, t2r], [Zr_r])
                            kb.op("dve", lambda e: e.tensor_tensor(t3[:, 0:256], us[:, cs], A_, ALU.mult), [usr, At_r], [t3r])
                            kb.op("dve", lambda e: e.tensor_tensor(t4[:, 0:256], ur[:, cs], B_, ALU.mult), [urr, Bt_r], [t4r])
                            kb.op(PWE, lambda e: e.tensor_tensor(Zs[:, m, zc:zc + 256], t3[:, 0:256], t4[:, 0:256], ALU.subtract), [t3r, t4r], [Zs_r])
                for s in range(nseq):
                    for cc in range(2):
                        zs_ = slice(s * 256 + cc * 128, s * 256 + (cc + 1) * 128)
                        for th in range(NTH):
                            bk, br = kb.bank("A")
                            for m in range(nP):
                                kb.mm(bk[:, 0:TW], Zr[:, m, zs_], Cq[:, m, th * TW:(th + 1) * TW], m == 0, False, [Zr_r] + CR, br)
                            for m in range(nP):
                                kb.mm(bk[:, 0:TW], Zs[:, m, zs_], Sq[:, m, th * TW:(th + 1) * TW], False, False, [Zs_r] + CR, br)
                            kb.mm(bk[:, 0:TW], ZN[0:1, zs_], altr[0:1, th * TW:(th + 1) * TW], False, True, [ZN_r] + CR, br)
                            epilogue(cc, slice(s * L + th * TW, s * L + (th + 1) * TW), bk[:, 0:TW], br)

            hsk = V(("hsk", l), 4)

            def ep1(cc, ts, ps, br):
                t1, t1r = S()
                kb.op("dve", lambda e: e.scalar_tensor_tensor(out=t1[:, 0:TW], in0=hv[:, cc, ts], scalar=hsk[:, cc:cc + 1], in1=ps, op0=ALU.mult, op1=ALU.add), [hv_r, br] + CR, [t1r])
                kb.op("dve", lambda e: e.tensor_tensor(hx1[:, cc, ts], t1[:, 0:TW], hx1[:, cc, ts], ALU.mult), [t1r, hx1_r], [hx1_r])

            def ep2(cc, ts, ps, br):
                t1, t1r = S()
                kb.op("dve", lambda e: e.scalar_tensor_tensor(out=t1[:, 0:TW], in0=hx1[:, cc, ts], scalar=hsk[:, 2 + cc:3 + cc], in1=ps, op0=ALU.mult, op1=ALU.add), [hx1_r, br] + CR, [t1r])
                h_ = ts.start // 512
                kb.op("dve", lambda e: e.tensor_tensor(mixT[:, 6 + cc, ts], t1[:, 0:TW], hx2[:, cc, ts], ALU.mult), [t1r, hx2_r], [mix_r[6 + cc][h_]])

            longconv(0, hv, hv_r, ep1)
            if l == 0: dbg("z" + grp, hx1[:, 0, :], hx1_r, [128, T])
            longconv(1, hx1, hx1_r, ep2)
            if l == 0:
                for cdbg in range(8):
                    dbg("mix%d" % cdbg + grp, mixT[:, cdbg, :], mix_r[cdbg][1], [128, T])

            chk("hy")
            st["run_pending_mod"](100)
            for jo in range(8):
                outs = project(I["w_out_p"][l, jo], 8, mix_rhs)
                for h, (bk, br) in enumerate(outs):
                    hs = slice(h * 512, (h + 1) * 512)
                    kb.op("dve", lambda e: e.scalar_tensor_tensor(out=x[:, jo, hs], in0=bk[:, :], scalar=modv[:, l, 16 + jo, j:j + 1], in1=x[:, jo, hs], op0=ALU.mult, op1=ALU.add), [br, x_r[jo][h], cres, st["mod_r"][l]], [x_r[jo][h]])
            if l == 0: dbg("x1" + grp, x[:, 0, :], x_r[0][1], [128, T])

            chk("wout")
            st["norm_mod"](l, j, 1)
            G = st["ovl"][:, 0:44 * 1024].bitcast(BF16).rearrange("p (f t) -> p f t", f=NF)
            G_r = [[Res("G") for h in range(2)] for f in range(NF)]
            allG = [r for f in G_r for r in f]
            merge_into(allG, [live["W"], live["M"], live["F"]]); live["G"] = allG; live["W"] = []
            modq = list(range(48)) if (P and l + 1 < NLAY) or (not cfg["prompt"] and not P and l + 1 < NLAY) else []
            for f in range(NF):
                for _ in range(2):
                    if modq: st["mod_piece"](l + 1, modq.pop(0))
                o1 = project(I["w1_p"][l, f], 8, hT_rhs, pool="A")
                o3 = project(I["w3_p"][l, f], 8, hT_rhs, pool="B")
                for h in range(2):
                    sg, sgr = S()
                    kb.op("act", lambda e: e.activation(out=sg[:], in_=o1[h][0][:, :], func=AF.Silu), [o1[h][1]], [sgr])
                    kb.op("dve", lambda e: e.tensor_tensor(G[:, f, h * 512:(h + 1) * 512], sg[:], o3[h][0][:, :], ALU.mult), [sgr, o3[h][1]], [G_r[f][h]])
            fgen = gen_filters(l + 1, True, None) if (FIF and l + 1 < NLAY) else iter(())
            for jo in range(8):
                if modq:
                    st["mod_piece"](l + 1, modq.pop(0))
                    if not modq: st["mod_finish"](l + 1)
                wa, war = ws.load(I["w2_p"][l, jo][:, 0:11, :], 11)
                wb, wbr = ws.load(I["w2_p"][l, jo][:, 11:22, :], 11)
                for h in range(2):
                    hs = slice(h * 512, (h + 1) * 512)
                    bk, br = kb.bank("A")
                    for k in range(NF):
                        wv, wr = (wa, war) if k < 11 else (wb, wbr)
                        kb.mm(bk[:, :], wv[:, k % 11, :], G[:, k, hs], k == 0, k == NF - 1, [wr, G_r[k][h]], br)
                    kb.op("dve", lambda e: e.scalar_tensor_tensor(out=x[:, jo, hs], in0=bk[:, :], scalar=modv[:, l, 40 + jo, j:j + 1], in1=x[:, jo, hs], op0=ALU.mult, op1=ALU.add), [br, x_r[jo][h], cres, st["mod_r"][l]], [x_r[jo][h]])
                    next(fgen, None)
            for _ in fgen: pass
        yo = O["yp"] if P else O["ys"]
        for c in range(8):
            kb.dma("sp", yo[:, c, :], x[:, c, :], [x_r[c][0], x_r[c][1]], [], "oy%d" % c)
            if "oy%d" % c not in kb.outsems: kb.outsems.append("oy%d" % c)

    try:
        chk("mod")
        if cfg["prompt"]:
            run_pass("P")
            kb.dma("sp", O["nst"], stt[:], [stt_r], [], "onst"); kb.outsems.append("onst")
        if cfg["sample"]:
            run_pass("S")
    except _Stop:
        pass
    for sk, v in kb.cnt.items():
        if v > 0:
            nc.sync.wait_ge(kb.semobj[sk], v)


_CACHE = {}


def _get_program(cfg):
    key = (cfg["layers"], cfg["prompt"], cfg["sample"], tuple(cfg.get("dbg", [])), cfg.get("stop"))
    if key not in _CACHE:
        st = build_program(cfg)
        emit_passes(st)
        _CACHE[key] = st
    return _CACHE[key]


def kernel(**inputs):
    cfg = CFG
    st = _get_program(cfg)
    nc = st["nc"]
    shared = _host_shared(inputs)
    in_maps = []
    for c in range(8):
        m = dict(shared); m.update(_host_core(inputs, c)); in_maps.append(m)
    res = run_bass_kernel_spmd(nc, in_maps, core_ids=list(range(8)))
    R = res.results
    kernel.last = R
    f32 = np.float32
    y_p = np.zeros((32, 256, D), f32); y_s = np.zeros((2, T, D), f32)
    nk = np.zeros((32, NL, 256, 2, 64), f32); nv = np.zeros((32, NL, 256, 2, 64), f32); nst = np.zeros((32, NL, 2, 256), f32)
    for c in range(8):
        r = R[c]
        y_p[4 * c:4 * c + 4] = r["yp"].transpose(2, 1, 0).reshape(4, 256, D)
        if c < 2:
            y_s[c] = r["ys"].transpose(2, 1, 0).reshape(T, D)
        k = r["nk"].reshape(NL, 2, 64, 4, 256)
        nk[4 * c:4 * c + 4] = k.transpose(3, 0, 4, 1, 2)
        v = r["nv"].reshape(NL, 2, 64, 4, 256)
        nv[4 * c:4 * c + 4] = v.transpose(3, 0, 4, 1, 2)
        s_ = r["nst"].reshape(128, NL, 2, 2, 4)
        nst[4 * c:4 * c + 4] = s_.transpose(4, 1, 2, 3, 0).reshape(4, NL, 2, 256)
    return (y_p, y_s, nk, nv, nst)
```

```python
import math
from contextlib import ExitStack
import numpy as np
import concourse.bass as bass
import concourse.mybir as mybir
from concourse.bass_utils import run_bass_kernel_spmd

F32 = mybir.dt.float32
BF16 = mybir.dt.bfloat16
AF = mybir.ActivationFunctionType
ALU = mybir.AluOpType

D = 1024; NL = 4; T = 1024; HD = 64
OFF_K = 512; OFF_V = 640; OFF_LX = 768; OFF_LG = 1024; OFF_HY = 1280
DFF = 2816; NF = 22
EPS = 1e-6
PI = math.pi

CFG = {"layers": NL, "prompt": True, "sample": True, "dbg": []}

VCOLS = {}
def _mk_vcols():
    o = 0
    def add(n, c):
        nonlocal o
        VCOLS[n] = o; o += c
    for l in range(NL):
        add(("n1", l), 8); add(("n2", l), 8); add(("qn", l), 1); add(("kn", l), 1)
        add(("lcw", l), 8)
        add(("lcb", l), 2)
        add(("lgb", l), 8)
        add(("lam", l), 4)
        add(("hcw", l), 18)
        add(("hsk", l), 4)
        add(("fb1", l), 1); add(("fb2", l), 1)
    for l in range(NL):
        add(("bmod", l), 48)
    add("cw256", 2); add("cw1024", 8); add("altc", 1)
    return o
NV = _mk_vcols()
PV_COND = 0; PV_H0 = 16; NPV = 32


def _fm(v, nchunk):
    return np.ascontiguousarray(np.asarray(v, np.float32).reshape(nchunk, 128).T)


def _host_consts():
    f32 = np.float32
    c = {}
    c["ident"] = np.eye(128, dtype=f32)
    R = np.zeros((128, 128), f32)
    for m in range(128):
        if m % 32 < 16: R[m + 16, m] = -1.0
        else: R[m - 16, m] = 1.0
    c["rrot"] = R
    t = np.arange(T)
    row = (t // 64).astype(f32); col = (t % 64).astype(f32)
    freqs = (f32(10000.0) ** (-(np.arange(16, dtype=f32)) / f32(16))).astype(f32)
    rc = np.zeros((128, T), f32); rs = np.zeros((128, T), f32)
    for p in range(128):
        d = p % 64
        pos = row if d < 32 else col
        ang = (pos * freqs[d % 16]).astype(f32)
        rc[p] = np.cos(ang); rs[p] = np.sin(ang)
    c["ropec"] = rc; c["ropes"] = rs
    for L in (256, 1024):
        tt = np.arange(L, dtype=f32); tn = (tt / f32(L)).astype(f32)
        bands = np.linspace(1e-4, 15, 16, dtype=f32)
        ang = (f32(2.0 * math.pi / L) * tt[:, None] * bands[None, :]).astype(f32)
        z = np.concatenate([tn[:, None], np.cos(ang), -np.sin(ang)], axis=-1).astype(f32)
        c["zT%d" % L] = np.ascontiguousarray(z.T)
        deltas = np.linspace(math.log(1e-2) / 1.5, math.log(1e-2) / 0.3, 256, dtype=f32)
        dec = np.exp(-tn[:, None] * np.abs(deltas)[None, :]).astype(f32)
        decb = dec.copy(); decb[0] = 0.0
        dd = np.stack([dec, decb], axis=1)
        c["dec%d" % L] = np.ascontiguousarray(dd.reshape(L // 128, 128, 2, 256).transpose(1, 0, 2, 3))
        k = np.arange(L, dtype=np.float64)
        th = math.pi / L * np.outer(k, k)
        C = np.cos(th).astype(f32); S = np.sin(th).astype(f32)
        c["dftc%d" % L] = np.ascontiguousarray(C.reshape(L // 128, 128, L).transpose(1, 0, 2))
        c["dfts%d" % L] = np.ascontiguousarray(S.reshape(L // 128, 128, L).transpose(1, 0, 2))
    c["altr"] = ((-1.0) ** np.arange(T)).astype(f32)[None, :]
    c["altcin"] = ((-1.0) ** np.arange(128)).astype(f32)[:, None]
    return c


def _host_shared(inp):
    f32 = np.float32
    g = lambda k: np.asarray(inp[k], f32)
    sh = _host_consts()
    vec = np.zeros((128, NV), f32)
    def put(key, arr):
        vec[:, VCOLS[key]:VCOLS[key] + arr.shape[1]] = arr
    for l in range(NL):
        put(("n1", l), _fm(g("norm1_w")[l], 8)); put(("n2", l), _fm(g("norm2_w")[l], 8))
        put(("qn", l), np.tile(g("q_norm_w")[l], 2)[:, None]); put(("kn", l), np.tile(g("k_norm_w")[l], 2)[:, None])
        put(("lcw", l), np.concatenate([_fm(g("lru_conv_w")[l, k], 2) for k in range(4)], axis=1))
        put(("lcb", l), _fm(g("lru_conv_b")[l], 2))
        put(("lgb", l), np.concatenate([_fm(g("lru_gate_b")[l, d, gt], 2) for d in range(2) for gt in range(2)], axis=1))
        put(("lam", l), np.concatenate([_fm(g("lru_lambda")[l, d], 2) for d in range(2)], axis=1))
        put(("hcw", l), np.concatenate([_fm(g("hy_conv_w")[l, k], 6) for k in range(3)], axis=1))
        put(("hsk", l), np.concatenate([_fm(g("hy_skip")[l, o], 2) for o in range(2)], axis=1))
        b1 = np.zeros((128, 1), f32); b1[:64, 0] = g("hy_filt_b1")[l]; put(("fb1", l), b1)
        b2 = np.zeros((128, 1), f32); b2[:64, 0] = g("hy_filt_b2")[l]; put(("fb2", l), b2)
    for L, key in ((256, "cw256"), (1024, "cw1024")):
        cw = np.full(L, 1.0 / L, f32); cw[0] = 0.5 / L
        put(key, _fm(cw, L // 128))
    put("altc", (((-1.0) ** np.arange(128)).astype(f32))[:, None])
    sh["vec"] = vec
    for l in range(NL):
        put(("bmod", l), _fm(g("b_mod")[l], 48))
    cols = []
    cols += list(range(OFF_K, OFF_K + 64)) * 2
    cols += list(range(OFF_K + 64, OFF_K + 128)) * 2
    cols += list(range(OFF_V, OFF_V + 128))
    cols += list(range(0, 512))
    cols += list(range(OFF_LX, OFF_LX + 128)) + list(range(OFF_LG, OFF_LG + 128))
    cols += list(range(OFF_LX + 128, OFF_LX + 256)) + list(range(OFF_LG + 128, OFF_LG + 256))
    cols += list(range(OFF_HY, OFF_HY + 768))
    cols = np.array(cols)
    def pieces(W, nk, nch):
        return np.ascontiguousarray(W.reshape(NL, nk, 128, nch, 128).transpose(0, 3, 2, 1, 4))
    sh["w_in_p"] = pieces(g("w_in")[:, :, cols], 8, 17)
    sh["w_out_p"] = pieces(g("w_out"), 8, 8)
    sh["w1_p"] = pieces(g("ffn_w1"), 8, NF)
    sh["w3_p"] = pieces(g("ffn_w3"), 8, NF)
    sh["w2_p"] = pieces(g("ffn_w2"), NF, 8)
    sh["w_mod_p"] = pieces(g("w_mod"), 8, 48)
    gw = np.zeros((NL, 128, 8, 128), f32)
    G = g("lru_gate_w")
    for l in range(NL):
        for d in range(2):
            for gt in range(2):
                for c in range(2):
                    idx = (d * 2 + gt) * 2 + c
                    for h in range(2):
                        gw[l, h * 64:(h + 1) * 64, idx, h * 64:(h + 1) * 64] = G[l, d, gt, 2 * c + h]
    sh["gw"] = gw
    sh["hf_w1"] = np.ascontiguousarray(g("hy_filt_w1"))
    sh["hf_w2"] = np.ascontiguousarray(g("hy_filt_w2"))
    sh["hf_w3"] = np.ascontiguousarray(np.concatenate([g("hy_filt_w3"), g("hy_filt_b3")[:, None, :]], axis=1))
    return sh


def _host_core(inp, c):
    f32 = np.float32
    m = {}
    xp = np.asarray(inp["x_prompt"], f32)[4 * c:4 * c + 4].reshape(T, 8, 128)
    m["xp"] = np.ascontiguousarray(xp.transpose(2, 1, 0))
    b = c % 2
    xs = np.asarray(inp["x_sample"], f32)[b].reshape(T, 8, 128)
    m["xs"] = np.ascontiguousarray(xs.transpose(2, 1, 0))
    ck = np.asarray(inp["cache_k"], f32)[b]
    ckT = np.zeros((NL, 2, 128, 256), f32)
    for g_ in range(2):
        kt = ck[:, :, g_, :].transpose(0, 2, 1)
        ckT[:, g_, :64] = kt; ckT[:, g_, 64:] = kt
    m["ckT"] = ckT
    cv = np.asarray(inp["cache_v"], f32)[b].reshape(NL, 2, 128, 128)
    m["cv"] = np.ascontiguousarray(cv.transpose(0, 2, 1, 3))
    pv = np.zeros((128, NPV), f32)
    cc = _fm(np.asarray(inp["c_ctx"], f32), 8); cb = _fm(np.asarray(inp["c"], f32)[b], 8)
    for k in range(8):
        pv[:, PV_COND + 2 * k] = cc[:, k]; pv[:, PV_COND + 2 * k + 1] = cb[:, k]
    st = np.asarray(inp["state_lru"], f32)[b]
    for l in range(NL):
        for d in range(2):
            pv[:, PV_H0 + (l * 2 + d) * 2:PV_H0 + (l * 2 + d) * 2 + 2] = _fm(st[l, d], 2)
    m["pvec"] = pv
    return m


class Res:
    __slots__ = ("w", "r", "name")
    def __init__(self, name=""):
        self.w = None; self.r = {}; self.name = name


class KB:
    def __init__(self, nc, es):
        self.nc = nc; self.es = es
        self.E = {"pe": nc.tensor, "dve": nc.vector, "act": nc.scalar, "pool": nc.gpsimd, "sp": nc.sync}
        self.semobj = {}
        self.cnt = {}
        for k in ("pe", "dve", "act", "pool"):
            self.semobj[k] = nc.alloc_semaphore(name="sem_" + k); self.cnt[k] = 0
        self.waited = {k: {} for k in self.E}
        self.banks = []; self.bres = []
        for i in range(8):
            self.banks.append(es.enter_context(nc.psum_tensor("bank%d" % i, [128, 512], F32)))
            self.bres.append(Res("bank%d" % i))
        self.bptr = {"A": 0, "B": 0}
        self.nsb = 0
        self.outsems = []

    def sb(self, name, shape, dt=F32):
        t = self.es.enter_context(self.nc.sbuf_tensor("s_" + name, list(shape), dt))
        return t

    def bank(self, pool):
        i = self.bptr[pool]; self.bptr[pool] = (i + 1) % 4
        j = i if pool == "A" else 4 + i
        return self.banks[j], self.bres[j]

    def deps_of(self, reads, writes):
        d = {}
        for r in reads:
            if r.w is not None:
                sk, v = r.w; d[sk] = max(d.get(sk, 0), v)
        for w in writes:
            if w.w is not None:
                sk, v = w.w; d[sk] = max(d.get(sk, 0), v)
            for sk, v in w.r.items(): d[sk] = max(d.get(sk, 0), v)
        return list(d.items())

    def _wait(self, e, deps):
        for sk, v in deps:
            if self.waited[e].get(sk, 0) >= v: continue
            self.E[e].wait_ge(self.semobj[sk], v)
            self.waited[e][sk] = v

    def op(self, e, fn, reads=(), writes=()):
        self._wait(e, self.deps_of(reads, writes))
        ins = fn(self.E[e])
        self.cnt[e] += 1
        ins.then_inc(self.semobj[e], 1)
        for r in reads: r.r[e] = self.cnt[e]
        for w in writes: w.w = (e, self.cnt[e]); w.r = {}
        return ins

    def mm(self, out, lhsT, rhs, start, stop, reads, bank, transpose=False):
        deps = [(s, v) for s, v in self.deps_of(reads, [bank]) if s != "pe"]
        self._wait("pe", deps)
        if transpose:
            ins = self.nc.tensor.transpose(out, lhsT, rhs)
        else:
            ins = self.nc.tensor.matmul(out, lhsT, rhs, start=start, stop=stop)
        self.cnt["pe"] += 1
        ins.then_inc(self.semobj["pe"], 1)
        for r in reads: r.r["pe"] = self.cnt["pe"]
        if stop:
            bank.w = ("pe", self.cnt["pe"]); bank.r = {}
        return ins

    def dma(self, q, out, in_, reads, writes, semkey):
        if semkey not in self.semobj:
            self.semobj[semkey] = self.nc.alloc_semaphore(name="dsem_" + semkey); self.cnt[semkey] = 0
        self._wait(q, self.deps_of(reads, writes))
        ins = self.E[q].dma_start(out=out, in_=in_)
        self.cnt[semkey] += 16
        ins.then_inc(self.semobj[semkey], 16)
        v = self.cnt[semkey]
        for r in reads: r.r[semkey] = v
        for w in writes: w.w = (semkey, v); w.r = {}
        return ins


class WStream:
    def __init__(self, kb, nslots=6, width=11 * 128):
        self.kb = kb; self.n = nslots; self.i = 0
        self.slots = [kb.sb("wslot%d" % i, [128, width], BF16) for i in range(nslots)]
        self.res = [Res("wslot%d" % i) for i in range(nslots)]

    def load(self, dram_piece, nk):
        i = self.i; self.i = (i + 1) % self.n
        s = self.slots[i]
        v = s[:, 0:nk * 128].rearrange("p (k n) -> p k n", k=nk)
        self.kb.dma("pool", v, dram_piece, [], [self.res[i]], "w%d" % i)
        return v, self.res[i]


def build_program(cfg):
    nc = bass.Bass("TRN2", target_bir_lowering=False)
    es = ExitStack()
    kb = KB(nc, es)
    NLAY = cfg["layers"]
    dbg_names = cfg.get("dbg", [])

    def din(name, shape, dt=F32):
        return nc.dram_tensor(name, list(shape), dt, kind="ExternalInput").ap()

    def dout(name, shape, dt=F32):
        return nc.dram_tensor(name, list(shape), dt, kind="ExternalOutput").ap()

    I = {}
    I["xp"] = din("xp", [128, 8, T]); I["xs"] = din("xs", [128, 8, T])
    I["ckT"] = din("ckT", [NL, 2, 128, 256]); I["cv"] = din("cv", [NL, 128, 2, 128])
    I["pvec"] = din("pvec", [128, NPV]); I["vec"] = din("vec", [128, NV])
    NLD = cfg.get("nld", NL)
    I["w_in_p"] = din("w_in_p", [NLD, 17, 128, 8, 128]); I["w_out_p"] = din("w_out_p", [NLD, 8, 128, 8, 128])
    I["w1_p"] = din("w1_p", [NLD, NF, 128, 8, 128]); I["w3_p"] = din("w3_p", [NLD, NF, 128, 8, 128])
    I["w2_p"] = din("w2_p", [NLD, 8, 128, NF, 128]); I["w_mod_p"] = din("w_mod_p", [NLD, cfg.get("nf", 48), 128, 8, 128])
    I["gw"] = din("gw", [NL, 128, 8, 128])
    I["hf_w1"] = din("hf_w1", [NL, 33, 64]); I["hf_w2"] = din("hf_w2", [NL, 64, 64]); I["hf_w3"] = din("hf_w3", [NL, 65, 1024])
    I["ident"] = din("ident", [128, 128]); I["rrot"] = din("rrot", [128, 128])
    I["ropec"] = din("ropec", [128, T]); I["ropes"] = din("ropes", [128, T])
    for L in (256, 1024):
        I["zT%d" % L] = din("zT%d" % L, [33, L]); I["dec%d" % L] = din("dec%d" % L, [128, L // 128, 2, 256])
        I["dftc%d" % L] = din("dftc%d" % L, [128, L // 128, L]); I["dfts%d" % L] = din("dfts%d" % L, [128, L // 128, L])
    I["altr"] = din("altr", [1, T]); I["altcin"] = din("altcin", [128, 1])
    O = {}
    O["yp"] = dout("yp", [128, 8, T]); O["ys"] = dout("ys", [128, 8, T])
    O["nk"] = dout("nk", [NL, 128, T]); O["nv"] = dout("nv", [NL, 128, T]); O["nst"] = dout("nst", [128, 64])
    DBG = {}

    def dbg(name, ap, res, shape):
        if name not in dbg_names: return
        o = dout("dbg_" + name, shape, ap.dtype)
        kb.dma("sp", o, ap, [res], [], "dbg_" + name)
        kb.outsems.append("dbg_" + name)

    sb = kb.sb
    x = sb("x", [128, 8, T]); x_r = [[Res("x%d_%d" % (c, h)) for h in range(2)] for c in range(8)]
    hT = sb("hT", [128, 8, T], BF16); hT_r = [[Res() for h in range(2)] for c in range(8)]
    mix_r = [[Res() for h in range(2)] for c in range(8)]
    ovl = sb("ovl", [128, 60 * 1024], mybir.dt.uint8)
    mixT = ovl[:, 0:16 * 1024].bitcast(BF16).rearrange("p (c t) -> p c t", c=8)
    vec = sb("vec", [128, NV]); pvec = sb("pvec", [128, NPV]); cres = Res("consts")
    ident = sb("ident", [128, 128]); rrot = sb("rrot", [128, 128])
    identb = sb("identb", [128, 128], BF16)
    ropec = sb("ropec", [128, T]); ropes = sb("ropes", [128, T])
    onesb = sb("onesb", [128, 128], BF16); onesbd = sb("onesbd", [128, 128], BF16)
    altc = sb("altcb", [128, 1], BF16); altr = sb("altrb", [1, T], BF16)
    dft = {256: (sb("dc256", [128, 2, 256], BF16), sb("ds256", [128, 2, 256], BF16)),
           1024: (sb("dc1024", [128, 8, 1024], BF16), sb("ds1024", [128, 8, 1024], BF16))}
    modv = sb("modv", [128, NL, 48, 2])
    modA = sb("modA", [128, NL, 2, 2, 8])
    sT = sb("sT", [128, 8, 2], BF16)
    stt = sb("stt", [128, 64])
    stt_r = Res("stt")
    gwb = sb("gwb", [128, 8, 128], BF16); gwb_r = Res("gwb")
    fw1 = sb("fw1", [33, 64]); fw2 = sb("fw2", [64, 64]); fw_r = Res("fw")
    rstd_t = [sb("rstd%d" % i, [128, 512]) for i in range(2)]; rstd_r = [Res("rstd%d" % i) for i in range(2)]; rstd_p = [0]
    dect = [sb("dect%d" % i, [128, 2, 256]) for i in range(2)]; dect_r = [Res("dect%d" % i) for i in range(2)]
    KN = sb("KN", [1, 512]); KN_r = Res("KN")
    cneg = sb("cneg", [128, 4]); cneg_r = Res("cneg")
    scr = [sb("scr%d" % i, [128, 512]) for i in range(6)]; scr_r = [Res("scr%d" % i) for i in range(6)]
    scb = [sb("scb%d" % i, [128, 512], BF16) for i in range(4)]; scb_r = [Res("scb%d" % i) for i in range(4)]
    scrp = [0]; scbp = [0]

    def S():
        i = scrp[0]; scrp[0] = (i + 1) % len(scr); return scr[i], scr_r[i]

    def SBF():
        i = scbp[0]; scbp[0] = (i + 1) % len(scb); return scb[i], scb_r[i]

    ws = WStream(kb)

    def carve(off, shape, dt):
        esz = 4 if dt == F32 else 2
        n = int(np.prod(shape[1:])) * esz
        v = ovl[:, off:off + n].bitcast(dt)
        if len(shape) == 3:
            v = v.rearrange("p (a b) -> p a b", a=shape[1])
        elif len(shape) == 4:
            v = v.rearrange("p (a b c) -> p a b c", a=shape[1], b=shape[2])
        return v, off + n

    V = lambda key, n=1: vec[:, VCOLS[key]:VCOLS[key] + n]

    def cload(q, dst, src):
        kb.dma(q, dst, src, [], [cres], "c0")
    cload("sp", vec[:], I["vec"]); cload("sp", pvec[:], I["pvec"])
    cload("sp", ident[:], I["ident"]); cload("sp", rrot[:], I["rrot"])
    cload("sp", ropec[:], I["ropec"]); cload("sp", ropes[:], I["ropes"])
    kb.dma("pool", identb[:], I["ident"], [], [cres], "c1")
    kb.dma("pool", altr[:], I["altr"], [], [cres], "c1")
    kb.dma("pool", dft[256][0][:], I["dftc256"], [], [cres], "c1")
    kb.dma("pool", dft[256][1][:], I["dfts256"], [], [cres], "c1")
    kb.dma("pool", altc[:], I["altcin"], [], [cres], "c1")
    cres2 = Res("c0all"); cres2.w = ("c0", kb.cnt["c0"])
    cres3 = Res("c1all"); cres3.w = ("c1", kb.cnt["c1"])
    CR = [cres2, cres3]
    kb.op("dve", lambda e: e.memset(onesb[:], 1.0), [], [cres])
    kb.op("dve", lambda e: e.memset(onesbd[:], 0.0), [], [cres])
    kb.op("dve", lambda e: e.memset(onesbd[0:64, 0:64], 1.0), [], [cres])
    kb.op("dve", lambda e: e.memset(onesbd[64:128, 64:128], 1.0), [], [cres])
    kb.op("dve", lambda e: e.memset(stt[:], 0.0), [], [stt_r])
    CR.append(cres)

    nmod = NLAY if cfg.get("stop") not in ("const",) else 0
    kb.op("act", lambda e: e.activation(out=sT[:].rearrange("p k j -> p (k j)"), in_=pvec[:, PV_COND:PV_COND + 16], func=AF.Silu), CR, [cres])
    mod_r = [Res("mod%d" % l) for l in range(NL)]

    def mod_piece(l, f):
        wv, wr = ws.load(I["w_mod_p"][l, f], 8)
        bk, br = kb.bank("A")
        for k in range(8):
            kb.mm(bk[:, 0:2], wv[:, k, :], sT[:, k, :], k == 0, k == 7, [wr, cres], br)
        kb.op("dve", lambda e: e.tensor_scalar(out=modv[:, l, f, :], in0=bk[:, 0:2], scalar1=V(("bmod", l), 48)[:, f:f + 1], scalar2=None, op0=ALU.add), [br] + CR, [mod_r[l]])

    def mod_finish(l, whs=(0, 1)):
        for j in range(2):
            for wh in whs:
                sc = modv[:, l, (8 if wh == 0 else 32):(16 if wh == 0 else 40), j]
                nw = V(("n1" if wh == 0 else "n2", l), 8)
                kb.op("dve", lambda e: e.scalar_tensor_tensor(out=modA[:, l, j, wh, :], in0=sc, scalar=1.0, in1=nw, op0=ALU.add, op1=ALU.mult), CR + [mod_r[l]], [mod_r[l]])

    pending_mod = []
    if nmod > 0:
        nf0 = cfg.get("nf", 48)
        for f in range(min(16, nf0)):
            mod_piece(0, f)
        mod_finish(0, (0,))
        pending_mod.extend(("p", 0, f) for f in range(16, nf0))
        pending_mod.append(("f", 0, (1,)))

    def run_pending_mod(n):
        for _ in range(n):
            if not pending_mod: return
            it = pending_mod.pop(0)
            if it[0] == "p": mod_piece(it[1], it[2])
            else: mod_finish(it[1], it[2])
    dbg("modv", modv[:].rearrange("p l f j -> p (l f j)"), mod_r[0], [128, NL * 96])

    def rms_bc(src_fn, src_res, nchunk, ones_t, inv_n, half_w=512):
        bk, br = kb.bank("B")
        for c in range(nchunk):
            sq, sqr = SBF()
            a, ar = src_fn(c)
            if nchunk > 1 and c % 2 == 0 and cfg.get("poolsq", True):
                kb.op("pool", lambda e: e.tensor_tensor(sq[:, 0:half_w], a, a, ALU.mult), [ar], [sqr])
            else:
                kb.op("act", lambda e: e.activation(out=sq[:, 0:half_w], in_=a, func=AF.Square), [ar], [sqr])
            kb.mm(bk[:, 0:half_w], ones_t[:], sq[:, 0:half_w], c == 0, c == nchunk - 1, [sqr, cres], br)
        i_ = rstd_p[0]; rstd_p[0] = 1 - i_
        s1, s1r = rstd_t[i_], rstd_r[i_]
        kb.op("act", lambda e: e.activation(out=s1[:, 0:half_w], in_=bk[:, 0:half_w], func=AF.Sqrt, bias=EPS, scale=inv_n), [br], [s1r])
        kb.op("dve", lambda e: e.reciprocal(s1[:, 0:half_w], s1[:, 0:half_w]), [s1r], [s1r])
        return s1, s1r

    def norm_mod(l, j, wh):
        shoff = 0 if wh == 0 else 24
        bks = [kb.bank("B"), kb.bank("B")]
        for c in range(8):
            for h in range(2):
                hs = slice(h * 512, (h + 1) * 512)
                sq, sqr = SBF()
                if (c + h) % 2 == 0:
                    kb.op("pool", lambda e: e.tensor_tensor(sq[:], x[:, c, hs], x[:, c, hs], ALU.mult), [x_r[c][h]], [sqr])
                else:
                    kb.op("act", lambda e: e.activation(out=sq[:], in_=x[:, c, hs], func=AF.Square), [x_r[c][h]], [sqr])
                kb.mm(bks[h][0][:, :], onesb[:], sq[:], c == 0, c == 7, [sqr, cres], bks[h][1])
        for h in range(2):
            kb.op("act", lambda e: e.activation(out=rstd_t[h][:], in_=bks[h][0][:, :], func=AF.Ln, bias=EPS, scale=1.0 / D), [bks[h][1]], [rstd_r[h]])
        for h in range(2):
            kb.op("act", lambda e: e.activation(out=rstd_t[h][:], in_=rstd_t[h][:], func=AF.Exp, scale=-0.5), [rstd_r[h]], [rstd_r[h]])
        for c in range(8):
            for h in range(2):
                hs = slice(h * 512, (h + 1) * 512)
                t1, t1r = S()
                kb.op("dve", lambda e: e.scalar_tensor_tensor(out=t1[:], in0=x[:, c, hs], scalar=modA[:, l, j, wh, c:c + 1], in1=rstd_t[h][:], op0=ALU.mult, op1=ALU.mult), [x_r[c][h], rstd_r[h], cres, mod_r[l]], [t1r])
                kb.op("act", lambda e: e.activation(out=hT[:, c, hs], in_=t1[:], func=AF.Identity, bias=modv[:, l, shoff + c, j:j + 1], scale=1.0), [t1r, cres, mod_r[l]], [hT_r[c][h]])

    def project(piece, nk, rhs_fn, pool="A"):
        wv, wr = ws.load(piece, nk)
        outs = []
        for h in range(2):
            bk, br = kb.bank(pool)
            for k in range(nk):
                a, ar = rhs_fn(k, h)
                kb.mm(bk[:, :], wv[:, k, :], a, k == 0, k == nk - 1, [wr, ar], br)
            outs.append((bk, br))
        run_pending_mod(2)
        return outs

    hT_rhs = lambda k, h: (hT[:, k, h * 512:(h + 1) * 512], hT_r[k][h])
    mix_rhs = lambda k, h: (mixT[:, k, h * 512:(h + 1) * 512], mix_r[k][h])

    def headnorm(l, bk, br, wkey):
        rstd, rr = rms_bc(lambda c: (bk[:, :], br), None, 1, onesbd, 1.0 / HD)
        o, o_r = S()
        kb.op("dve", lambda e: e.scalar_tensor_tensor(out=o[:], in0=bk[:, :], scalar=V((wkey, l)), in1=rstd[:], op0=ALU.mult, op1=ALU.mult), [br, rr, cres], [o_r])
        return o, o_r

    def rope(src, src_r, h, dst, dst_r):
        hs = slice(h * 512, (h + 1) * 512)
        bk, br = kb.bank("B")
        kb.mm(bk[:, :], rrot[:], src[:], True, True, [src_r, cres], br)
        t1, t1r = S()
        kb.op("dve", lambda e: e.tensor_tensor(t1[:], bk[:, :], ropes[:, hs], ALU.mult), [br, cres], [t1r])
        t2, t2r = S()
        kb.op("dve", lambda e: e.tensor_tensor(t2[:], src[:], ropec[:, hs], ALU.mult), [src_r, cres], [t2r])
        kb.op("dve", lambda e: e.tensor_tensor(dst, t1[:], t2[:], ALU.add), [t1r, t2r], [dst_r])

    def qk_prep(l, outs, wkey, dsts, dst_r, do_rope, extra=None):
        sq = []
        for h in range(2):
            t_, r_ = SBF(); sq.append((t_, r_))
            kb.op("act", lambda e: e.activation(out=t_[:], in_=outs[h][0][:, :], func=AF.Square), [outs[h][1]], [r_])
        rb = []
        for h in range(2):
            b_, br_ = kb.bank("B"); rb.append((b_, br_))
            kb.mm(b_[:, :], onesbd[:], sq[h][0][:], True, True, [sq[h][1], cres], br_)
        for h in range(2):
            kb.op("act", lambda e: e.activation(out=rstd_t[h][:], in_=rb[h][0][:, :], func=AF.Ln, bias=EPS, scale=1.0 / HD), [rb[h][1]], [rstd_r[h]])
        for h in range(2):
            kb.op("act", lambda e: e.activation(out=rstd_t[h][:], in_=rstd_t[h][:], func=AF.Exp, scale=-0.5), [rstd_r[h]], [rstd_r[h]])
        o = []
        for h in range(2):
            t_, r_ = S(); o.append((t_, r_))
            kb.op("dve", lambda e: e.scalar_tensor_tensor(out=t_[:], in0=outs[h][0][:, :], scalar=V((wkey, l)), in1=rstd_t[h][:], op0=ALU.mult, op1=ALU.mult), [outs[h][1], rstd_r[h], cres], [r_])
        if extra is not None:
            for h in range(2): extra(h, o[h][0], o[h][1])
        if not do_rope:
            for h in range(2):
                kb.op("act", lambda e: e.activation(out=dsts[h], in_=o[h][0][:], func=AF.Copy), [o[h][1]], [dst_r])
            return
        pb = []
        for h in range(2):
            b_, br_ = kb.bank("B"); pb.append((b_, br_))
            kb.mm(b_[:, :], rrot[:], o[h][0][:], True, True, [o[h][1], cres], br_)
        t1 = []; t2 = []
        for h in range(2):
            t_, r_ = S(); t1.append((t_, r_))
            kb.op("dve", lambda e: e.tensor_tensor(t_[:], pb[h][0][:, :], ropes[:, h * 512:(h + 1) * 512], ALU.mult), [pb[h][1], cres], [r_])
        for h in range(2):
            t_, r_ = S(); t2.append((t_, r_))
            kb.op("dve", lambda e: e.tensor_tensor(t_[:], o[h][0][:], ropec[:, h * 512:(h + 1) * 512], ALU.mult), [o[h][1], cres], [r_])
        for h in range(2):
            kb.op("dve", lambda e: e.tensor_tensor(dsts[h], t1[h][0][:], t2[h][0][:], ALU.add), [t1[h][1], t2[h][1]], [dst_r])

    def dwconv(out3, in3, res_out, res_in, center, taps, bias):
        L = in3.shape[-1]
        kb.op("act", lambda e: e.activation(out=out3, in_=in3, func=AF.Identity, bias=bias, scale=center), [res_in, cres], [res_out])
        for sh, wcol in taps:
            if sh < 0:
                o_ = out3[:, :, -sh:L]; i_ = in3[:, :, 0:L + sh]
            else:
                o_ = out3[:, :, 0:L - sh]; i_ = in3[:, :, sh:L]
            kb.op("dve", lambda e: e.scalar_tensor_tensor(out=o_, in0=i_, scalar=wcol, in1=o_, op0=ALU.mult, op1=ALU.add), [res_in, res_out, cres], [res_out])

    st = dict(nc=nc, kb=kb, es=es, I=I, O=O, x=x, x_r=x_r, hT=hT, hT_r=hT_r, mixT=mixT, mix_r=mix_r, ws=ws,
              V=V, CR=CR, cres=cres, S=S, SBF=SBF, carve=carve, dbg=dbg, norm_mod=norm_mod, project=project,
              hT_rhs=hT_rhs, mix_rhs=mix_rhs, mod_piece=mod_piece, mod_finish=mod_finish, mod_r=mod_r, rstd_t=rstd_t, rstd_r=rstd_r, run_pending_mod=run_pending_mod, headnorm=headnorm, rope=rope, qk_prep=qk_prep, dwconv=dwconv, modv=modv, pvec=pvec,
              ident=ident, identb=identb, onesb=onesb, altc=altc, altr=altr, dft=dft, stt=stt, stt_r=stt_r,
              gwb=gwb, gwb_r=gwb_r, fw1=fw1, fw2=fw2, fw_r=fw_r, cfg=cfg, NLAY=NLAY, ovl=ovl, dect=dect, dect_r=dect_r,
              KN=KN, KN_r=KN_r, cneg=cneg, cneg_r=cneg_r, ropec=ropec, ropes=ropes, sb=sb)
    return st


class _Stop(Exception):
    pass


def emit_passes(st):
    nc = st["nc"]; kb = st["kb"]; I = st["I"]; O = st["O"]; x = st["x"]; x_r = st["x_r"]
    hT = st["hT"]; hT_r = st["hT_r"]; mixT = st["mixT"]; mix_r = st["mix_r"]; ws = st["ws"]
    V = st["V"]; CR = st["CR"]; cres = st["cres"]; S = st["S"]; SBF = st["SBF"]; carve = st["carve"]; dbg = st["dbg"]
    project = st["project"]; hT_rhs = st["hT_rhs"]; mix_rhs = st["mix_rhs"]; headnorm = st["headnorm"]; rope = st["rope"]
    dwconv = st["dwconv"]; modv = st["modv"]; pvec = st["pvec"]; ident = st["ident"]; identb = st["identb"]
    onesb = st["onesb"]; altc = st["altc"]; altr = st["altr"]; dft = st["dft"]; stt = st["stt"]; stt_r = st["stt_r"]
    gwb = st["gwb"]; gwb_r = st["gwb_r"]; fw1 = st["fw1"]; fw2 = st["fw2"]; fw_r = st["fw_r"]
    dect = st["dect"]; dect_r = st["dect_r"]; KN = st["KN"]; KN_r = st["KN_r"]; cneg = st["cneg"]; cneg_r = st["cneg_r"]
    cfg = st["cfg"]; NLAY = st["NLAY"]
    WOFF = 16 * 1024; FOFF = 44 * 1024

    live = {"M": [], "W": [], "F": [], "G": []}

    def pe_warm(n):
        for i in range(n):
            bk, br = kb.bank("A")
            kb.mm(bk[:, :], onesb[:], hT[:, 0, 0:512], True, True, CR, br)

    STAGES = ["const", "mod", "filt", "norm", "attn", "lru", "hy", "wout"]

    def chk(name):
        sp = cfg.get("stop")
        if sp is not None and STAGES.index(name) >= STAGES.index(sp): raise _Stop()

    def merge_into(new_list, old_lists):
        ev = {}
        for ol in old_lists:
            for r in ol:
                if r.w is not None: ev[r.w[0]] = max(ev.get(r.w[0], 0), r.w[1])
                for sk, v in r.r.items(): ev[sk] = max(ev.get(sk, 0), v)
        for r in new_list:
            r.w = None; r.r = dict(ev)

    def w_phase(new_list):
        merge_into(new_list, [live["W"]]); live["W"] = list(new_list)

    allmix = [r for c in mix_r for r in c]
    live["M"] = allmix

    def run_pass(grp):
        P = grp == "P"; j = 0 if P else 1
        nseq, L = (4, 256) if P else (1, 1024); nP = L // 128
        KT = T if P else T + 256
        Cq, Sq = dft[L]
        xin = I["xp"] if P else I["xs"]
        for c in range(8):
            kb.dma("sp", x[:, c, :], xin[:, c, :], [], [x_r[c][0], x_r[c][1]], "xin%d" % c)
        if cfg["sample"] and "c2" not in kb.cnt:
            c2w = Res("c2w")
            kb.dma("pool", dft[1024][0][:], I["dftc1024"], [], [c2w], "c2")
            kb.dma("pool", dft[1024][1][:], I["dfts1024"], [], [c2w], "c2")
            st["c2all"] = Res("c2all"); st["c2all"].w = ("c2", kb.cnt["c2"])
        if not P:
            CR.append(st["c2all"])

        FIF = cfg.get("filt_in_ffn", True)

        def gen_filters(l, ffn_mode, oldG, L=L, nP=nP, Cq=Cq, Sq=Sq, grp=grp):
            At, _ = carve(FOFF, [128, nP, 512], BF16); Bt, _ = carve(FOFF + nP * 1024, [128, nP, 512], BF16)
            At_r = Res("At"); Bt_r = Res("Bt")
            merge_into([At_r, Bt_r], [live["F"]]); live["F"] = [At_r, Bt_r]
            st["tab_r"] = (At_r, Bt_r)
            if not ffn_mode:
                o_ = WOFF
                h2a = st["ovl"][0:65, o_:o_ + L * 2].bitcast(BF16); o_ += L * 4
                fw3 = st["ovl"][0:65, o_:o_ + 2048].bitcast(BF16); o_ += 4096
                Pq, o_ = carve(o_, [128, nP, 512], BF16); Qq, o_ = carve(o_, [128, nP, 512], BF16)
                h2a_r = Res("h2a"); fw3_r = Res("fw3"); Pq_r = Res("Pq"); Qq_r = Res("Qq")
                merge_into([h2a_r, fw3_r, Pq_r, Qq_r], [live["W"], oldG]); live["W"] = [h2a_r, fw3_r, Pq_r, Qq_r]
            else:
                h2a = st["rstd_t"][0][0:65, :].bitcast(BF16)[:, 0:L]
                fw3 = st["rstd_t"][1][0:65, :].bitcast(BF16)
                hTf = hT[:].rearrange("p c t -> p (c t)")
                Pq = hTf[:, 0:nP * 512].rearrange("p (a b) -> p a b", a=nP)
                Qq = hTf[:, 4096:4096 + nP * 512].rearrange("p (a b) -> p a b", a=nP)
                h2a_r = st["rstd_r"][0]; fw3_r = st["rstd_r"][1]; Pq_r = Res("PqF"); Qq_r = Res("QqF")
                allh = [r for c_ in hT_r for r in c_]
                merge_into([Pq_r, Qq_r], [allh])
            kb.dma("sp", fw1[:], I["hf_w1"][l], [], [fw_r], "fw")
            kb.dma("sp", fw2[:], I["hf_w2"][l], [], [fw_r], "fw")
            kb.dma("pool", fw3, I["hf_w3"][l], [], [fw3_r], "fw3")
            fwa = Res("fwall"); fwa.w = ("fw", kb.cnt["fw"])
            kb.op("dve", lambda e: e.memset(h2a[64:65, :], 1.0), [], [h2a_r])

            def sin_evac(ps, br, bcol, dst, dst_r, w):
                xx, xr = S()
                kb.op("dve", lambda e: e.tensor_scalar(out=xx[0:64, 0:w], in0=ps, scalar1=bcol, scalar2=None, op0=ALU.add), [br, cres], [xr])
                m1, m1r = S()
                kb.op("dve", lambda e: e.tensor_scalar(out=m1[0:64, 0:w], in0=xx[0:64, 0:w], scalar1=PI, scalar2=-2 * PI, op0=ALU.is_gt, op1=ALU.mult), [xr], [m1r])
                m2, m2r = S()
                kb.op("dve", lambda e: e.tensor_scalar(out=m2[0:64, 0:w], in0=xx[0:64, 0:w], scalar1=-PI, scalar2=2 * PI, op0=ALU.is_lt, op1=ALU.mult), [xr], [m2r])
                kb.op("dve", lambda e: e.tensor_tensor(xx[0:64, 0:w], xx[0:64, 0:w], m1[0:64, 0:w], ALU.add), [xr, m1r], [xr])
                kb.op("dve", lambda e: e.tensor_tensor(xx[0:64, 0:w], xx[0:64, 0:w], m2[0:64, 0:w], ALU.add), [xr, m2r], [xr])
                kb.op("dve", lambda e: e.tensor_scalar(out=xx[0:64, 0:w], in0=xx[0:64, 0:w], scalar1=3.1415925, scalar2=-3.1415925, op0=ALU.min, op1=ALU.max), [xr], [xr])
                kb.op("act", lambda e: e.activation(out=dst, in_=xx[0:64, 0:w], func=AF.Sin), [xr], [dst_r])

            for cb in range((L + 511) // 512):
                w = min(512, L - cb * 512)
                zt, ztr = S()
                kb.dma("sp", zt[0:33, 0:w], I["zT%d" % L][:, cb * 512:cb * 512 + w], [], [ztr], "zt")
                bk, br = kb.bank("A")
                kb.mm(bk[0:64, 0:w], fw1[0:33, 0:64], zt[0:33, 0:w], True, True, [fwa, ztr], br)
                h1, h1r = S()
                sin_evac(bk[0:64, 0:w], br, V(("fb1", l))[0:64, :], h1[0:64, 0:w], h1r, w)
                yield
                bk2, br2 = kb.bank("A")
                kb.mm(bk2[0:64, 0:w], fw2[0:64, 0:64], h1[0:64, 0:w], True, True, [fwa, h1r], br2)
                sin_evac(bk2[0:64, 0:w], br2, V(("fb2", l))[0:64, :], h2a[0:64, cb * 512:cb * 512 + w], h2a_r, w)
                yield
            for pc in range(nP):
                di = pc % 2
                kb.dma("sp", dect[di][:], I["dec%d" % L][:, pc], [], [dect_r[di]], "dec%d" % di)
                for ob in range(2):
                    bk, br = kb.bank("A")
                    kb.mm(bk[:, :], h2a[0:65, pc * 128:(pc + 1) * 128], fw3[0:65, ob * 512:(ob + 1) * 512], True, True, [h2a_r, fw3_r], br)
                    f_, fr = S()
                    kb.op("dve", lambda e: e.tensor_tensor(f_[:], bk[:, :], dect[di][:].rearrange("p a b -> p (a b)"), ALU.mult), [br, dect_r[di]], [fr])
                    kb.op("dve", lambda e: e.tensor_tensor(Pq[:, pc, ob * 256:(ob + 1) * 256], f_[:, 0:256], f_[:, 256:512], ALU.add), [fr], [Pq_r])
                    kb.op("dve", lambda e: e.tensor_tensor(Qq[:, pc, ob * 256:(ob + 1) * 256], f_[:, 256:512], f_[:, 0:256], ALU.subtract), [fr], [Qq_r])
                yield
            cwk = "cw%d" % L
            for m in range(nP):
                for (tab, tq, src, src_r, dst, dst_r) in ((Cq, 0, Pq, Pq_r, At, At_r), (Sq, 1, Qq, Qq_r, Bt, Bt_r)):
                    bk, br = kb.bank("A")
                    for pc in range(nP):
                        kb.mm(bk[:, :], tab[:, pc, m * 128:(m + 1) * 128], src[:, pc, :], pc == 0, pc == nP - 1, [src_r] + CR, br)
                    kb.op("act", lambda e: e.activation(out=dst[:, m, :], in_=bk[:, :], func=AF.Identity, scale=V(cwk, nP)[:, m:m + 1]), [br] + CR, [dst_r])
                yield
            bk, br = kb.bank("A")
            for pc in range(nP):
                kb.mm(bk[0:1, :], altc[:, 0:1], Pq[:, pc, :], pc == 0, pc == nP - 1, [Pq_r] + CR, br)
            kb.op("act", lambda e: e.activation(out=KN[0:1, :], in_=bk[0:1, :], func=AF.Identity, scale=0.5 / L), [br], [KN_r])
            if l == 0:
                dbg("At" + grp, At[:].rearrange("p a b -> p (a b)"), At_r, [128, nP * 512])
                dbg("KN" + grp, KN[:], KN_r, [1, 512])

            if ffn_mode:
                merge_into(allh, [[Pq_r, Qq_r]])

        for l in range(NLAY):
            oldG = live["G"]; live["G"] = []
            merge_into(allmix, [oldG])
            At, _ = carve(FOFF, [128, nP, 512], BF16); Bt, _ = carve(FOFF + nP * 1024, [128, nP, 512], BF16)
            if (l == 0 and not (st.get("pregen") and not P)) or not FIF:
                for _ in gen_filters(l, False, oldG): pass
            else:
                live["W"] = list(oldG)
            At_r, Bt_r = st["tab_r"]
            kb.dma("pool", gwb[:], I["gw"][l], [], [gwb_r], "gw")
            chk("filt")
            kb.op("act", lambda e: e.activation(out=cneg[:], in_=V(("lam", l), 4), func=AF.Exp, scale=-1.0), CR, [cneg_r])
            kb.op("act", lambda e: e.activation(out=cneg[:], in_=cneg[:], func=AF.Ln, bias=1.0, scale=1.0), [cneg_r], [cneg_r])
            kb.op("dve", lambda e: e.tensor_scalar(out=cneg[:], in0=cneg[:], scalar1=-8.0, scalar2=None, op0=ALU.mult), [cneg_r], [cneg_r])

            st["norm_mod"](l, j, 0)
            if l == 0: dbg("hT" + grp, hT[:, 0, :], hT_r[0][1], [128, T])

            chk("norm")
            o_ = WOFF
            qT, o_ = carve(o_, [128, 4, T], BF16); kdT, o_ = carve(o_, [128, 2, KT], BF16)
            Vtok, o_ = carve(o_, [128, KT // 128, 256], BF16); vT, o_ = carve(o_, [128, T], F32); kst, o_ = carve(o_, [128, T], F32)
            q_r = [Res("q%d" % c) for c in range(4)]; kd_r = [Res("kd0"), Res("kd1")]; vt_r = Res("Vtok"); vT_r = Res("vT"); kst_r = Res("kst")
            w_phase(q_r + kd_r + [vt_r, vT_r, kst_r])
            koff = 0 if P else 256
            Vt4 = Vtok.rearrange("p k (g e) -> p k g e", g=2)
            kb.op("dve", lambda e: e.memset(Vtok[:], 1.0), [], [vt_r])
            if not P:
                for g in range(2):
                    kb.dma("pool", kdT[:, g, 0:256], I["ckT"][l, g], [], [kd_r[g]], "ck%d" % g)
                kb.dma("pool", Vt4[:, 0:2, :, 0:64], I["cv"][l].rearrange("p c (g d) -> p c g d", g=2), [vt_r], [vt_r], "cv")
            for g in range(2):
                outs = project(I["w_in_p"][l, g], 8, hT_rhs)
                if P:
                    def kextra(h, o_, o_r_, g=g):
                        kb.op("act", lambda e: e.activation(out=kst[g * 64:(g + 1) * 64, h * 512:(h + 1) * 512], in_=o_[0:64, :], func=AF.Copy), [o_r_], [kst_r])
                    st["qk_prep"](l, outs, "kn", [kdT[:, g, 0:512], kdT[:, g, 512:1024]], kd_r[g], False, kextra)
                else:
                    st["qk_prep"](l, outs, "kn", [kdT[:, g, 256:768], kdT[:, g, 768:1280]], kd_r[g], True)
            if P:
                kb.dma("sp", O["nk"][l], kst[:], [kst_r], [], "onk");
                if "onk" not in kb.outsems: kb.outsems.append("onk")
            outs = project(I["w_in_p"][l, 2], 8, hT_rhs)
            for h, (bk, br) in enumerate(outs):
                kb.op("act", lambda e: e.activation(out=vT[:, h * 512:(h + 1) * 512], in_=bk[:, :], func=AF.Copy), [br], [vT_r])
            if P:
                kb.dma("sp", O["nv"][l], vT[:], [vT_r], [], "onv")
                if "onv" not in kb.outsems: kb.outsems.append("onv")
            for tb4 in range(2):
                bk, br = kb.bank("B")
                for i4 in range(4):
                    tb = tb4 * 4 + i4
                    kb.mm(bk[:, i4 * 128:(i4 + 1) * 128], vT[:, tb * 128:(tb + 1) * 128], ident[:], True, True, [vT_r] + CR, br, transpose=True)
                for g_ in range(2):
                    kb.op("dve", lambda e: e.tensor_copy(Vt4[:, koff // 128 + tb4 * 4:koff // 128 + tb4 * 4 + 4, g_, 0:64], bk[:, :].rearrange("p (a g d) -> p a g d", a=4, g=2)[:, :, g_, :]), [br, vt_r], [vt_r])
            if l == 0:
                dbg("kdT" + grp, kdT[:, 0, :], kd_r[0], [128, KT])

            for c in range(4):
                outs = project(I["w_in_p"][l, 3 + c], 8, hT_rhs)
                st["qk_prep"](l, outs, "qn", [qT[:, c, 0:512], qT[:, c, 512:1024]], q_r[c], not P)
            for c in range(4):
                g = c // 2
                NI = 2 if P else KT // 128
                items = [(qh, i) for qh in range(2) for i in range(NI)]
                Sb = {}
                pair = {}

                def emit_S(n):
                    qh, i = items[n]
                    bks = [kb.bank("A"), kb.bank("A")]
                    if P:
                        s_ = qh * 2 + i
                        for kc in range(2):
                            for ph in range(2):
                                ps_ = slice(ph * 64, (ph + 1) * 64)
                                kb.mm(bks[ph][0][:, kc * 256:(kc + 1) * 256], kdT[ps_, g, s_ * 256 + kc * 128:s_ * 256 + (kc + 1) * 128], qT[ps_, c, s_ * 256:(s_ + 1) * 256], True, True, [kd_r[g], q_r[c]], bks[ph][1])
                    else:
                        for ph in range(2):
                            ps_ = slice(ph * 64, (ph + 1) * 64)
                            kb.mm(bks[ph][0][:, :], kdT[ps_, g, i * 128:(i + 1) * 128], qT[ps_, c, qh * 512:(qh + 1) * 512], True, True, [kd_r[g], q_r[c]], bks[ph][1])
                    Sb[n] = bks

                def emit_PV(n):
                    qh, i = items[n]
                    if i == 0:
                        pair[qh] = [kb.bank("B"), kb.bank("B")]
                    bks = Sb.pop(n)
                    pts = []
                    for ph in range(2):
                        pt, ptr = SBF(); pts.append((pt, ptr))
                        kb.op("act", lambda e: e.activation(out=pt[:], in_=bks[ph][0][:, :], func=AF.Exp, scale=0.125), [bks[ph][1]], [ptr])
                    for ph in range(2):
                        nb, nbr = pair[qh][ph]
                        pt, ptr = pts[ph]
                        if P:
                            s_ = qh * 2 + i
                            for kc in range(2):
                                kb.mm(nb[:, i * 256:(i + 1) * 256], Vt4[:, 2 * s_ + kc, g, :], pt[:, kc * 256:(kc + 1) * 256], kc == 0, kc == 1, [vt_r, ptr], nbr)
                        else:
                            kb.mm(nb[:, :], Vt4[:, i, g, :], pt[:], i == 0, i == NI - 1, [vt_r, ptr], nbr)
                    if i == NI - 1:
                        for ph in range(2):
                            nb, nbr = pair[qh][ph]
                            rc, rcr = S()
                            if P or cfg.get("actrc", False):
                                kb.op("act", lambda e: e.activation(out=rc[0:64, :], in_=nb[64:128, :], func=AF.Ln), [nbr], [rcr])
                                kb.op("act", lambda e: e.activation(out=rc[0:64, :], in_=rc[0:64, :], func=AF.Exp, scale=-1.0), [rcr], [rcr])
                            else:
                                kb.op("dve", lambda e: e.reciprocal(rc[0:64, :], nb[64:128, :]), [nbr], [rcr])
                            kb.op("dve", lambda e: e.tensor_tensor(mixT[ph * 64:(ph + 1) * 64, c, qh * 512:(qh + 1) * 512], nb[0:64, :], rc[0:64, :], ALU.mult), [nbr, rcr], [mix_r[c][qh]])

                LA = 1
                for n in range(len(items) + LA):
                    if n < len(items): emit_S(n)
                    if n - LA >= 0: emit_PV(n - LA)

            if l == 0:
                for cdbg in range(4):
                    dbg("att%d" % cdbg + grp, mixT[:, cdbg, :], mix_r[cdbg][1], [128, T])
            chk("attn")
            for c in range(2):
                o_ = WOFF
                lx, o_ = carve(o_, [128, T], F32); xc, o_ = carve(o_, [128, T], F32); xcb, o_ = carve(o_, [128, T], BF16)
                gl, o_ = carve(o_, [128, T], BF16); rg, o_ = carve(o_, [128, T], F32); ig, o_ = carve(o_, [128, T], F32)
                hd0, o_ = carve(o_, [128, T], F32); hd1, o_ = carve(o_, [128, T], F32)
                lx_r = Res("lx"); xc_r = Res("xc"); xcb_r = Res("xcb"); gl_r = Res("gl"); rg_r = Res("rg"); ig_r = Res("ig"); hd_r = [Res("hd0"), Res("hd1")]
                w_phase([lx_r, xc_r, xcb_r, gl_r, rg_r, ig_r] + hd_r)
                hd = [hd0, hd1]
                outs = project(I["w_in_p"][l, 7 + 2 * c], 8, hT_rhs)
                for h, (bk, br) in enumerate(outs):
                    kb.op("act", lambda e: e.activation(out=lx[:, h * 512:(h + 1) * 512], in_=bk[:, :], func=AF.Copy), [br], [lx_r])
                outs = project(I["w_in_p"][l, 8 + 2 * c], 8, hT_rhs)
                for h, (bk, br) in enumerate(outs):
                    hs = slice(h * 512, (h + 1) * 512)
                    a1, a1r = S()
                    kb.op("act", lambda e: e.activation(out=a1[:], in_=bk[:, :], func=AF.Square), [br], [a1r])
                    kb.op("dve", lambda e: e.tensor_scalar(out=a1[:], in0=a1[:], scalar1=0.044715, scalar2=1.0, op0=ALU.mult, op1=ALU.add), [a1r], [a1r])
                    kb.op("dve", lambda e: e.tensor_tensor(a1[:], a1[:], bk[:, :], ALU.mult), [a1r, br], [a1r])
                    kb.op("act", lambda e: e.activation(out=a1[:], in_=a1[:], func=AF.Sigmoid, scale=1.5957691216), [a1r], [a1r])
                    kb.op("dve", lambda e: e.tensor_tensor(gl[:, hs], a1[:], bk[:, :], ALU.mult), [a1r, br], [gl_r])
                v3 = lambda a: a.rearrange("p (s t) -> p s t", s=nseq)
                lcw = V(("lcw", l), 8)
                dwconv(v3(xc), v3(lx), xc_r, lx_r, lcw[:, 4 + c:5 + c],
                       [(-2, lcw[:, 0 + c:1 + c]), (-1, lcw[:, 2 + c:3 + c]), (1, lcw[:, 6 + c:7 + c])], V(("lcb", l), 2)[:, c:c + 1])
                kb.op("act", lambda e: e.activation(out=xcb, in_=xc, func=AF.Copy), [xc_r], [xcb_r])
                if l == 0 and c == 0: dbg("xc" + grp, xc, xc_r, [128, T])
                for d in range(2):
                    for gt, dst, dst_r in ((0, rg, rg_r), (1, ig, ig_r)):
                        idx = (d * 2 + gt) * 2 + c
                        for h in range(2):
                            hs = slice(h * 512, (h + 1) * 512)
                            bk, br = kb.bank("A")
                            kb.mm(bk[:, :], gwb[:, idx, :], xcb[:, hs], True, True, [gwb_r, xcb_r], br)
                            kb.op("act", lambda e: e.activation(out=dst[:, hs], in_=bk[:, :], func=AF.Sigmoid, bias=V(("lgb", l), 8)[:, idx:idx + 1], scale=1.0), [br] + CR, [dst_r])
                    kb.op("act", lambda e: e.activation(out=rg, in_=rg, func=AF.Exp, scale=cneg[:, d * 2 + c:d * 2 + c + 1]), [rg_r, cneg_r], [rg_r])
                    kb.op("dve", lambda e: e.tensor_tensor(ig, ig, xc, ALU.mult), [ig_r, xc_r], [ig_r])
                    for h in range(2):
                        hs = slice(h * 512, (h + 1) * 512)
                        a2, a2r = S()
                        kb.op("act", lambda e: e.activation(out=a2[:], in_=rg[:, hs], func=AF.Square), [rg_r], [a2r])
                        kb.op("act", lambda e: e.activation(out=a2[:], in_=a2[:], func=AF.Sqrt, bias=1.0, scale=-1.0), [a2r], [a2r])
                        kb.op("dve", lambda e: e.tensor_tensor(ig[:, hs], ig[:, hs], a2[:], ALU.mult), [ig_r, a2r], [ig_r])
                    for s in range(nseq):
                        lo, hi = s * L, (s + 1) * L
                        init = 0.0 if P else pvec[:, PV_H0 + (l * 2 + d) * 2 + c:PV_H0 + (l * 2 + d) * 2 + c + 1]
                        if d == 0:
                            kb.op("dve", lambda e: e.tensor_tensor_scan(hd[d][:, lo:hi], rg[:, lo:hi], ig[:, lo:hi], init, ALU.mult, ALU.add), [rg_r, ig_r] + CR, [hd_r[d]])
                        else:
                            rv = (lambda a: a[:, hi - 1::-1]) if lo == 0 else (lambda a: a[:, hi - 1:lo - 1:-1])
                            kb.op("dve", lambda e: e.tensor_tensor_scan(rv(hd[d]), rv(rg), rv(ig), init, ALU.mult, ALU.add), [rg_r, ig_r] + CR, [hd_r[d]])
                    if P:
                        col = ((l * 2 + d) * 2 + c) * 4
                        src = v3(hd[d])[:, :, L - 1] if d == 0 else v3(hd[d])[:, :, 0]
                        kb.op("act", lambda e: e.activation(out=stt[:, col:col + 4], in_=src, func=AF.Copy), [hd_r[d]], [stt_r])
                if l == 0 and c == 0: dbg("hf" + grp, hd0, hd_r[0], [128, T]); dbg("hb" + grp, hd1, hd_r[1], [128, T])
                kb.op("dve", lambda e: e.tensor_tensor(hd0, hd0, hd1, ALU.add), hd_r, [hd_r[0]])
                for h in range(2):
                    hs = slice(h * 512, (h + 1) * 512)
                    kb.op("dve", lambda e: e.tensor_tensor(mixT[:, 4 + c, hs], hd0[:, hs], gl[:, hs], ALU.mult), [hd_r[0], gl_r], [mix_r[4 + c][h]])

            chk("lru")
            o_ = WOFF
            hv, o_ = carve(o_, [128, 2, T], BF16); hx1, o_ = carve(o_, [128, 2, T], BF16); hx2, o_ = carve(o_, [128, 2, T], BF16)
            o_s2 = o_
            raws = []; cvts = []
            for i2 in range(2):
                r_, o_ = carve(o_, [128, T], F32); c_, o_ = carve(o_, [128, T], F32)
                raws.append((r_, Res("raw%d" % i2))); cvts.append((c_, Res("cvt%d" % i2)))
            hv_r = Res("hv"); hx1_r = Res("hx1"); hx2_r = Res("hx2")
            w_phase([hv_r, hx1_r, hx2_r] + [r for _, r in raws] + [r for _, r in cvts])
            v3 = lambda a: a.rearrange("p (s t) -> p s t", s=nseq)
            hcw = V(("hcw", l), 18)
            for i in range(6):
                raw, raw_r = raws[i % 2]; cvt, cvt_r = cvts[i % 2]
                dst, dst_r = ((hv, hv_r), (hx1, hx1_r), (hx2, hx2_r))[i // 2]
                outs = project(I["w_in_p"][l, 11 + i], 8, hT_rhs)
                for h, (bk, br) in enumerate(outs):
                    hs = slice(h * 512, (h + 1) * 512)
                    kb.op("act", lambda e: e.activation(out=raw[:, hs], in_=bk[:, :], func=AF.Copy), [br], [raw_r])
                    kb.op("dve", lambda e: e.tensor_scalar(out=cvt[:, hs], in0=raw[:, hs], scalar1=hcw[:, 6 + i:7 + i], scalar2=None, op0=ALU.mult), [raw_r, cres], [cvt_r])
                r3 = v3(raw); c3 = v3(cvt); d3 = v3(dst[:, i % 2, :])
                kb.op("dve", lambda e: e.scalar_tensor_tensor(out=c3[:, :, 1:L], in0=r3[:, :, 0:L - 1], scalar=hcw[:, i:i + 1], in1=c3[:, :, 1:L], op0=ALU.mult, op1=ALU.add), [raw_r, cvt_r, cres], [cvt_r])
                kb.op("dve", lambda e: e.scalar_tensor_tensor(out=c3[:, :, 0:L - 1], in0=r3[:, :, 1:L], scalar=hcw[:, 12 + i:13 + i], in1=c3[:, :, 0:L - 1], op0=ALU.mult, op1=ALU.add), [raw_r, cvt_r, cres], [cvt_r])
                kb.op("act", lambda e: e.activation(out=dst[:, i % 2, :], in_=cvt, func=AF.Copy), [cvt_r], [dst_r])
            raw_r = raws[0][1]; cvt_r = raws[1][1]; raw_r2 = cvts[0][1]; cvt_r2 = cvts[1][1]
            if l == 0: dbg("hv" + grp, hv[:, 0, :], hv_r, [128, T])
            tok, o_ = carve(o_s2, [128, nP, nseq * 256], BF16); Zr, o_ = carve(o_, [128, nP, nseq * 256], BF16); Zs, o_ = carve(o_, [128, nP, nseq * 256], BF16)
            tok_r = Res("tok"); Zr_r = Res("Zr"); Zs_r = Res("Zs")
            merge_into([tok_r, Zr_r, Zs_r], [[raw_r, cvt_r, raw_r2, cvt_r2]]); live["W"] = [hv_r, hx1_r, hx2_r, tok_r, Zr_r, Zs_r]
            ZN = st.setdefault("ZN", None)
            if ZN is None:
                ZN = st["sb"]("ZN", [1, 1024], BF16); st["ZN"] = ZN; st["ZN_r"] = Res("ZN")
            ZN_r = st["ZN_r"]
            NCOL = nseq * 256; GW = min(512, NCOL); NG = NCOL // GW
            TW = min(L, 512); NTH = L // TW

            PWE = cfg.get("pwe", "pool")

            def longconv(o, u, u_r, epilogue):
                for s in range(nseq):
                    for tc in range(nP):
                        bk, br = kb.bank("B")
                        bkb = bk[:, :].bitcast(BF16)
                        for cc in range(2):
                            kb.mm(bkb[:, cc * 128:(cc + 1) * 128], u[:, cc, s * L + tc * 128:s * L + (tc + 1) * 128], identb[:], True, True, [u_r] + CR, br, transpose=True)
                        kb.op("act", lambda e: e.activation(out=tok[:, tc, s * 256:(s + 1) * 256], in_=bkb[:, 0:256], func=AF.Copy), [br], [tok_r])
                for gi in range(NG):
                    gs = slice(gi * GW, (gi + 1) * GW)
                    un, unr = kb.bank("B")
                    for tc in range(nP):
                        kb.mm(un[0:1, 0:GW], altc[:, 0:1], tok[:, tc, gs], tc == 0, tc == nP - 1, [tok_r] + CR, unr)
                    for si in range(GW // 256):
                        zc = gi * GW + si * 256
                        kb.op("dve", lambda e: e.tensor_tensor(ZN[0:1, zc:zc + 256], un[0:1, si * 256:(si + 1) * 256], KN[0:1, o * 256:(o + 1) * 256], ALU.mult), [unr, KN_r], [ZN_r])
                for m in range(nP):
                    for gi in range(NG):
                        gs = slice(gi * GW, (gi + 1) * GW)
                        ur, urr = kb.bank("A"); us, usr = kb.bank("B")
                        for tc in range(nP):
                            kb.mm(ur[:, 0:GW], Cq[:, tc, m * 128:(m + 1) * 128], tok[:, tc, gs], tc == 0, tc == nP - 1, [tok_r] + CR, urr)
                        for tc in range(nP):
                            kb.mm(us[:, 0:GW], Sq[:, tc, m * 128:(m + 1) * 128], tok[:, tc, gs], tc == 0, tc == nP - 1, [tok_r] + CR, usr)
                        for si in range(GW // 256):
                            cs = slice(si * 256, (si + 1) * 256); zc = gi * GW + si * 256
                            A_ = At[:, m, o * 256:(o + 1) * 256]; B_ = Bt[:, m, o * 256:(o + 1) * 256]
                            t1, t1r = S(); t2, t2r = S(); t3, t3r = S(); t4, t4r = S()
                            kb.op("dve", lambda e: e.tensor_tensor(t1[:, 0:256], ur[:, cs], A_, ALU.mult), [urr, At_r], [t1r])
                            kb.op("dve", lambda e: e.tensor_tensor(t2[:, 0:256], us[:, cs], B_, ALU.mult), [usr, Bt_r], [t2r])
                            kb.op(PWE, lambda e: e.tensor_tensor(Zr[:, m, zc:zc + 256], t1[:, 0:256], t2[:, 0:256], ALU.add), [t1r, t2r], [Zr_r])
                            kb.op("dve", lambda e: e.tensor_tensor(t3[:, 0:256], us[:, cs], A_, ALU.mult), [usr, At_r], [t3r])
                            kb.op("dve", lambda e: e.tensor_tensor(t4[:, 0:256], ur[:, cs], B_, ALU.mult), [urr, Bt_r], [t4r])
                            kb.op(PWE, lambda e: e.tensor_tensor(Zs[:, m, zc:zc + 256], t3[:, 0:256], t4[:, 0:256], ALU.subtract), [t3r, t4r], [Zs_r])
                for s in range(nseq):
                    for cc in range(2):
                        zs_ = slice(s * 256 + cc * 128, s * 256 + (cc + 1) * 128)
                        for th in range(NTH):
                            bk, br = kb.bank("A")
                            for m in range(nP):
                                kb.mm(bk[:, 0:TW], Zr[:, m, zs_], Cq[:, m, th * TW:(th + 1) * TW], m == 0, False, [Zr_r] + CR, br)
                            for m in range(nP):
                                kb.mm(bk[:, 0:TW], Zs[:, m, zs_], Sq[:, m, th * TW:(th + 1) * TW], False, False, [Zs_r] + CR, br)
                            kb.mm(bk[:, 0:TW], ZN[0:1, zs_], altr[0:1, th * TW:(th + 1) * TW], False, True, [ZN_r] + CR, br)
                            epilogue(cc, slice(s * L + th * TW, s * L + (th + 1) * TW), bk[:, 0:TW], br)

            hsk = V(("hsk", l), 4)

            def ep1(cc, ts, ps, br):
                t1, t1r = S()
                kb.op("dve", lambda e: e.scalar_tensor_tensor(out=t1[:, 0:TW], in0=hv[:, cc, ts], scalar=hsk[:, cc:cc + 1], in1=ps, op0=ALU.mult, op1=ALU.add), [hv_r, br] + CR, [t1r])
                kb.op("dve", lambda e: e.tensor_tensor(hx1[:, cc, ts], t1[:, 0:TW], hx1[:, cc, ts], ALU.mult), [t1r, hx1_r], [hx1_r])

            def ep2(cc, ts, ps, br):
                t1, t1r = S()
                kb.op("dve", lambda e: e.scalar_tensor_tensor(out=t1[:, 0:TW], in0=hx1[:, cc, ts], scalar=hsk[:, 2 + cc:3 + cc], in1=ps, op0=ALU.mult, op1=ALU.add), [hx1_r, br] + CR, [t1r])
                h_ = ts.start // 512
                kb.op("dve", lambda e: e.tensor_tensor(mixT[:, 6 + cc, ts], t1[:, 0:TW], hx2[:, cc, ts], ALU.mult), [t1r, hx2_r], [mix_r[6 + cc][h_]])

            longconv(0, hv, hv_r, ep1)
            if l == 0: dbg("z" + grp, hx1[:, 0, :], hx1_r, [128, T])
            longconv(1, hx1, hx1_r, ep2)
            if l == 0:
                for cdbg in range(8):
                    dbg("mix%d" % cdbg + grp, mixT[:, cdbg, :], mix_r[cdbg][1], [128, T])

            chk("hy")
            st["run_pending_mod"](100)
            for jo in range(8):
                outs = project(I["w_out_p"][l, jo], 8, mix_rhs)
                for h, (bk, br) in enumerate(outs):
                    hs = slice(h * 512, (h + 1) * 512)
                    kb.op("dve", lambda e: e.scalar_tensor_tensor(out=x[:, jo, hs], in0=bk[:, :], scalar=modv[:, l, 16 + jo, j:j + 1], in1=x[:, jo, hs], op0=ALU.mult, op1=ALU.add), [br, x_r[jo][h], cres, st["mod_r"][l]], [x_r[jo][h]])
            if l == 0: dbg("x1" + grp, x[:, 0, :], x_r[0][1], [128, T])

            chk("wout")
            st["norm_mod"](l, j, 1)
            G = st["ovl"][:, 0:44 * 1024].bitcast(BF16).rearrange("p (f t) -> p f t", f=NF)
            G_r = [[Res("G") for h in range(2)] for f in range(NF)]
            allG = [r for f in G_r for r in f]
            merge_into(allG, [live["W"], live["M"], live["F"]]); live["G"] = allG; live["W"] = []
            modq = list(range(48)) if (P and l + 1 < NLAY) or (not cfg["prompt"] and not P and l + 1 < NLAY) else []
            for f in range(NF):
                for _ in range(2):
                    if modq: st["mod_piece"](l + 1, modq.pop(0))
                o1 = project(I["w1_p"][l, f], 8, hT_rhs, pool="A")
                o3 = project(I["w3_p"][l, f], 8, hT_rhs, pool="B")
                for h in range(2):
                    sg, sgr = S()
                    kb.op("act", lambda e: e.activation(out=sg[:], in_=o1[h][0][:, :], func=AF.Silu), [o1[h][1]], [sgr])
                    kb.op("dve", lambda e: e.tensor_tensor(G[:, f, h * 512:(h + 1) * 512], sg[:], o3[h][0][:, :], ALU.mult), [sgr, o3[h][1]], [G_r[f][h]])
            if FIF and l + 1 < NLAY:
                fgen = gen_filters(l + 1, True, None)
            elif FIF and P and cfg["sample"] and cfg.get("pregen", True):
                if st["c2all"] not in CR: CR.append(st["c2all"])
                fgen = gen_filters(0, True, None, L=1024, nP=8, Cq=dft[1024][0], Sq=dft[1024][1], grp="S")
                st["pregen"] = True
            else:
                fgen = iter(())
            for jo in range(8):
                if modq:
                    st["mod_piece"](l + 1, modq.pop(0))
                    if not modq: st["mod_finish"](l + 1)
                wa, war = ws.load(I["w2_p"][l, jo][:, 0:11, :], 11)
                wb, wbr = ws.load(I["w2_p"][l, jo][:, 11:22, :], 11)
                for h in range(2):
                    hs = slice(h * 512, (h + 1) * 512)
                    bk, br = kb.bank("A")
                    for k in range(NF):
                        wv, wr = (wa, war) if k < 11 else (wb, wbr)
                        kb.mm(bk[:, :], wv[:, k % 11, :], G[:, k, hs], k == 0, k == NF - 1, [wr, G_r[k][h]], br)
                    kb.op("dve", lambda e: e.scalar_tensor_tensor(out=x[:, jo, hs], in0=bk[:, :], scalar=modv[:, l, 40 + jo, j:j + 1], in1=x[:, jo, hs], op0=ALU.mult, op1=ALU.add), [br, x_r[jo][h], cres, st["mod_r"][l]], [x_r[jo][h]])
                    next(fgen, None)
            for _ in fgen: pass
        yo = O["yp"] if P else O["ys"]
        for c in range(8):
            kb.dma("sp", yo[:, c, :], x[:, c, :], [x_r[c][0], x_r[c][1]], [], "oy%d" % c)
            if "oy%d" % c not in kb.outsems: kb.outsems.append("oy%d" % c)

    try:
        chk("mod")
        if cfg["prompt"]:
            run_pass("P")
            kb.dma("sp", O["nst"], stt[:], [stt_r], [], "onst"); kb.outsems.append("onst")
        if cfg["sample"]:
            run_pass("S")
    except _Stop:
        pass
    for sk, v in kb.cnt.items():
        if v > 0:
            nc.sync.wait_ge(kb.semobj[sk], v)


_CACHE = {}


def _get_program(cfg):
    key = (cfg["layers"], cfg["prompt"], cfg["sample"], tuple(cfg.get("dbg", [])), cfg.get("stop"))
    if key not in _CACHE:
        st = build_program(cfg)
        emit_passes(st)
        _CACHE[key] = st
    return _CACHE[key]


def kernel(**inputs):
    cfg = CFG
    st = _get_program(cfg)
    nc = st["nc"]
    shared = _host_shared(inputs)
    in_maps = []
    for c in range(8):
        m = dict(shared); m.update(_host_core(inputs, c)); in_maps.append(m)
    res = run_bass_kernel_spmd(nc, in_maps, core_ids=list(range(8)))
    R = res.results
    kernel.last = R
    f32 = np.float32
    y_p = np.zeros((32, 256, D), f32); y_s = np.zeros((2, T, D), f32)
    nk = np.zeros((32, NL, 256, 2, 64), f32); nv = np.zeros((32, NL, 256, 2, 64), f32); nst = np.zeros((32, NL, 2, 256), f32)
    for c in range(8):
        r = R[c]
        y_p[4 * c:4 * c + 4] = r["yp"].transpose(2, 1, 0).reshape(4, 256, D)
        if c < 2:
            y_s[c] = r["ys"].transpose(2, 1, 0).reshape(T, D)
        k = r["nk"].reshape(NL, 2, 64, 4, 256)
        nk[4 * c:4 * c + 4] = k.transpose(3, 0, 4, 1, 2)
        v = r["nv"].reshape(NL, 2, 64, 4, 256)
        nv[4 * c:4 * c + 4] = v.transpose(3, 0, 4, 1, 2)
        s_ = r["nst"].reshape(128, NL, 2, 2, 4)
        nst[4 * c:4 * c + 4] = s_.transpose(4, 1, 2, 3, 0).reshape(4, NL, 2, 256)
    return (y_p, y_s, nk, nv, nst)
```

```python
import math
from contextlib import ExitStack
import numpy as np
import concourse.bass as bass
import concourse.mybir as mybir
from concourse.bass_utils import run_bass_kernel_spmd

F32 = mybir.dt.float32
BF16 = mybir.dt.bfloat16
AF = mybir.ActivationFunctionType
ALU = mybir.AluOpType

D = 1024; NL = 4; T = 1024; HD = 64
OFF_K = 512; OFF_V = 640; OFF_LX = 768; OFF_LG = 1024; OFF_HY = 1280
DFF = 2816; NF = 22
EPS = 1e-6
PI = math.pi

CFG = {"layers": NL, "prompt": True, "sample": True, "dbg": []}

VCOLS = {}
def _mk_vcols():
    o = 0
    def add(n, c):
        nonlocal o
        VCOLS[n] = o; o += c
    for l in range(NL):
        add(("n1", l), 8); add(("n2", l), 8); add(("qn", l), 1); add(("kn", l), 1)
        add(("lcw", l), 8)
        add(("lcb", l), 2)
        add(("lgb", l), 8)
        add(("lam", l), 4)
        add(("hcw", l), 18)
        add(("hsk", l), 4)
        add(("fb1", l), 1); add(("fb2", l), 1)
    for l in range(NL):
        add(("bmod", l), 48)
    add("cw256", 2); add("cw1024", 8); add("altc", 1)
    return o
NV = _mk_vcols()
PV_COND = 0; PV_H0 = 16; NPV = 32


def _fm(v, nchunk):
    return np.ascontiguousarray(np.asarray(v, np.float32).reshape(nchunk, 128).T)


def _host_consts():
    f32 = np.float32
    c = {}
    c["ident"] = np.eye(128, dtype=f32)
    R = np.zeros((128, 128), f32)
    for m in range(128):
        if m % 32 < 16: R[m + 16, m] = -1.0
        else: R[m - 16, m] = 1.0
    c["rrot"] = R
    t = np.arange(T)
    row = (t // 64).astype(f32); col = (t % 64).astype(f32)
    freqs = (f32(10000.0) ** (-(np.arange(16, dtype=f32)) / f32(16))).astype(f32)
    rc = np.zeros((128, T), f32); rs = np.zeros((128, T), f32)
    for p in range(128):
        d = p % 64
        pos = row if d < 32 else col
        ang = (pos * freqs[d % 16]).astype(f32)
        rc[p] = np.cos(ang); rs[p] = np.sin(ang)
    c["ropec"] = rc; c["ropes"] = rs
    for L in (256, 1024):
        tt = np.arange(L, dtype=f32); tn = (tt / f32(L)).astype(f32)
        bands = np.linspace(1e-4, 15, 16, dtype=f32)
        ang = (f32(2.0 * math.pi / L) * tt[:, None] * bands[None, :]).astype(f32)
        z = np.concatenate([tn[:, None], np.cos(ang), -np.sin(ang)], axis=-1).astype(f32)
        c["zT%d" % L] = np.ascontiguousarray(z.T)
        deltas = np.linspace(math.log(1e-2) / 1.5, math.log(1e-2) / 0.3, 256, dtype=f32)
        dec = np.exp(-tn[:, None] * np.abs(deltas)[None, :]).astype(f32)
        decb = dec.copy(); decb[0] = 0.0
        dd = np.stack([dec, decb], axis=1)
        c["dec%d" % L] = np.ascontiguousarray(dd.reshape(L // 128, 128, 2, 256).transpose(1, 0, 2, 3))
        k = np.arange(L, dtype=np.float64)
        th = math.pi / L * np.outer(k, k)
        C = np.cos(th).astype(f32); S = np.sin(th).astype(f32)
        c["dftc%d" % L] = np.ascontiguousarray(C.reshape(L // 128, 128, L).transpose(1, 0, 2))
        c["dfts%d" % L] = np.ascontiguousarray(S.reshape(L // 128, 128, L).transpose(1, 0, 2))
    c["altr"] = ((-1.0) ** np.arange(T)).astype(f32)[None, :]
    c["altcin"] = ((-1.0) ** np.arange(128)).astype(f32)[:, None]
    return c


def _host_shared(inp):
    f32 = np.float32
    g = lambda k: np.asarray(inp[k], f32)
    sh = _host_consts()
    vec = np.zeros((128, NV), f32)
    def put(key, arr):
        vec[:, VCOLS[key]:VCOLS[key] + arr.shape[1]] = arr
    for l in range(NL):
        put(("n1", l), _fm(g("norm1_w")[l], 8)); put(("n2", l), _fm(g("norm2_w")[l], 8))
        put(("qn", l), np.tile(g("q_norm_w")[l], 2)[:, None]); put(("kn", l), np.tile(g("k_norm_w")[l], 2)[:, None])
        put(("lcw", l), np.concatenate([_fm(g("lru_conv_w")[l, k], 2) for k in range(4)], axis=1))
        put(("lcb", l), _fm(g("lru_conv_b")[l], 2))
        put(("lgb", l), np.concatenate([_fm(g("lru_gate_b")[l, d, gt], 2) for d in range(2) for gt in range(2)], axis=1))
        put(("lam", l), np.concatenate([_fm(g("lru_lambda")[l, d], 2) for d in range(2)], axis=1))
        put(("hcw", l), np.concatenate([_fm(g("hy_conv_w")[l, k], 6) for k in range(3)], axis=1))
        put(("hsk", l), np.concatenate([_fm(g("hy_skip")[l, o], 2) for o in range(2)], axis=1))
        b1 = np.zeros((128, 1), f32); b1[:64, 0] = g("hy_filt_b1")[l]; put(("fb1", l), b1)
        b2 = np.zeros((128, 1), f32); b2[:64, 0] = g("hy_filt_b2")[l]; put(("fb2", l), b2)
    for L, key in ((256, "cw256"), (1024, "cw1024")):
        cw = np.full(L, 1.0 / L, f32); cw[0] = 0.5 / L
        put(key, _fm(cw, L // 128))
    put("altc", (((-1.0) ** np.arange(128)).astype(f32))[:, None])
    sh["vec"] = vec
    for l in range(NL):
        put(("bmod", l), _fm(g("b_mod")[l], 48))
    cols = []
    cols += list(range(OFF_K, OFF_K + 64)) * 2
    cols += list(range(OFF_K + 64, OFF_K + 128)) * 2
    cols += list(range(OFF_V, OFF_V + 128))
    cols += list(range(0, 512))
    cols += list(range(OFF_LX, OFF_LX + 128)) + list(range(OFF_LG, OFF_LG + 128))
    cols += list(range(OFF_LX + 128, OFF_LX + 256)) + list(range(OFF_LG + 128, OFF_LG + 256))
    cols += list(range(OFF_HY, OFF_HY + 768))
    cols = np.array(cols)
    def pieces(W, nk, nch):
        return np.ascontiguousarray(W.reshape(NL, nk, 128, nch, 128).transpose(0, 3, 2, 1, 4))
    sh["w_in_p"] = pieces(g("w_in")[:, :, cols], 8, 17)
    sh["w_out_p"] = pieces(g("w_out"), 8, 8)
    sh["w1_p"] = pieces(g("ffn_w1"), 8, NF)
    sh["w3_p"] = pieces(g("ffn_w3"), 8, NF)
    sh["w2_p"] = pieces(g("ffn_w2"), NF, 8)
    sh["w_mod_p"] = pieces(g("w_mod"), 8, 48)
    gw = np.zeros((NL, 128, 8, 128), f32)
    G = g("lru_gate_w")
    for l in range(NL):
        for d in range(2):
            for gt in range(2):
                for c in range(2):
                    idx = (d * 2 + gt) * 2 + c
                    for h in range(2):
                        gw[l, h * 64:(h + 1) * 64, idx, h * 64:(h + 1) * 64] = G[l, d, gt, 2 * c + h]
    sh["gw"] = gw
    sh["hf_w1"] = np.ascontiguousarray(g("hy_filt_w1"))
    sh["hf_w2"] = np.ascontiguousarray(g("hy_filt_w2"))
    sh["hf_w3"] = np.ascontiguousarray(np.concatenate([g("hy_filt_w3"), g("hy_filt_b3")[:, None, :]], axis=1))
    return sh


def _host_core(inp, c):
    f32 = np.float32
    m = {}
    xp = np.asarray(inp["x_prompt"], f32)[4 * c:4 * c + 4].reshape(T, 8, 128)
    m["xp"] = np.ascontiguousarray(xp.transpose(2, 1, 0))
    b = c % 2
    xs = np.asarray(inp["x_sample"], f32)[b].reshape(T, 8, 128)
    m["xs"] = np.ascontiguousarray(xs.transpose(2, 1, 0))
    ck = np.asarray(inp["cache_k"], f32)[b]
    ckT = np.zeros((NL, 2, 128, 256), f32)
    for g_ in range(2):
        kt = ck[:, :, g_, :].transpose(0, 2, 1)
        ckT[:, g_, :64] = kt; ckT[:, g_, 64:] = kt
    m["ckT"] = ckT
    cv = np.asarray(inp["cache_v"], f32)[b].reshape(NL, 2, 128, 128)
    m["cv"] = np.ascontiguousarray(cv.transpose(0, 2, 1, 3))
    pv = np.zeros((128, NPV), f32)
    cc = _fm(np.asarray(inp["c_ctx"], f32), 8); cb = _fm(np.asarray(inp["c"], f32)[b], 8)
    for k in range(8):
        pv[:, PV_COND + 2 * k] = cc[:, k]; pv[:, PV_COND + 2 * k + 1] = cb[:, k]
    st = np.asarray(inp["state_lru"], f32)[b]
    for l in range(NL):
        for d in range(2):
            pv[:, PV_H0 + (l * 2 + d) * 2:PV_H0 + (l * 2 + d) * 2 + 2] = _fm(st[l, d], 2)
    m["pvec"] = pv
    return m


class Res:
    __slots__ = ("w", "r", "name")
    def __init__(self, name=""):
        self.w = None; self.r = {}; self.name = name


class KB:
    def __init__(self, nc, es):
        self.nc = nc; self.es = es
        self.E = {"pe": nc.tensor, "dve": nc.vector, "act": nc.scalar, "pool": nc.gpsimd, "sp": nc.sync}
        self.semobj = {}
        self.cnt = {}
        for k in ("pe", "dve", "act", "pool"):
            self.semobj[k] = nc.alloc_semaphore(name="sem_" + k); self.cnt[k] = 0
        self.waited = {k: {} for k in self.E}
        self.banks = []; self.bres = []
        for i in range(8):
            self.banks.append(es.enter_context(nc.psum_tensor("bank%d" % i, [128, 512], F32)))
            self.bres.append(Res("bank%d" % i))
        self.bptr = {"A": 0, "B": 0}
        self.nsb = 0
        self.outsems = []

    def sb(self, name, shape, dt=F32):
        t = self.es.enter_context(self.nc.sbuf_tensor("s_" + name, list(shape), dt))
        return t

    def bank(self, pool):
        i = self.bptr[pool]; self.bptr[pool] = (i + 1) % 4
        j = i if pool == "A" else 4 + i
        return self.banks[j], self.bres[j]

    def deps_of(self, reads, writes):
        d = {}
        for r in reads:
            if r.w is not None:
                sk, v = r.w; d[sk] = max(d.get(sk, 0), v)
        for w in writes:
            if w.w is not None:
                sk, v = w.w; d[sk] = max(d.get(sk, 0), v)
            for sk, v in w.r.items(): d[sk] = max(d.get(sk, 0), v)
        return list(d.items())

    def _wait(self, e, deps):
        for sk, v in deps:
            if self.waited[e].get(sk, 0) >= v: continue
            self.E[e].wait_ge(self.semobj[sk], v)
            self.waited[e][sk] = v

    def op(self, e, fn, reads=(), writes=()):
        self._wait(e, self.deps_of(reads, writes))
        ins = fn(self.E[e])
        self.cnt[e] += 1
        ins.then_inc(self.semobj[e], 1)
        for r in reads: r.r[e] = self.cnt[e]
        for w in writes: w.w = (e, self.cnt[e]); w.r = {}
        return ins

    def mm(self, out, lhsT, rhs, start, stop, reads, bank, transpose=False):
        deps = [(s, v) for s, v in self.deps_of(reads, [bank]) if s != "pe"]
        self._wait("pe", deps)
        if transpose:
            ins = self.nc.tensor.transpose(out, lhsT, rhs)
        else:
            ins = self.nc.tensor.matmul(out, lhsT, rhs, start=start, stop=stop)
        self.cnt["pe"] += 1
        ins.then_inc(self.semobj["pe"], 1)
        for r in reads: r.r["pe"] = self.cnt["pe"]
        if stop:
            bank.w = ("pe", self.cnt["pe"]); bank.r = {}
        return ins

    def dma(self, q, out, in_, reads, writes, semkey):
        if semkey not in self.semobj:
            self.semobj[semkey] = self.nc.alloc_semaphore(name="dsem_" + semkey); self.cnt[semkey] = 0
        self._wait(q, self.deps_of(reads, writes))
        ins = self.E[q].dma_start(out=out, in_=in_)
        self.cnt[semkey] += 16
        ins.then_inc(self.semobj[semkey], 16)
        v = self.cnt[semkey]
        for r in reads: r.r[semkey] = v
        for w in writes: w.w = (semkey, v); w.r = {}
        return ins


class WStream:
    def __init__(self, kb, nslots=6, width=11 * 128):
        self.kb = kb; self.n = nslots; self.i = 0
        self.slots = [kb.sb("wslot%d" % i, [128, width], BF16) for i in range(nslots)]
        self.res = [Res("wslot%d" % i) for i in range(nslots)]

    def load(self, dram_piece, nk):
        i = self.i; self.i = (i + 1) % self.n
        s = self.slots[i]
        v = s[:, 0:nk * 128].rearrange("p (k n) -> p k n", k=nk)
        self.kb.dma("pool", v, dram_piece, [], [self.res[i]], "w%d" % i)
        return v, self.res[i]


def build_program(cfg):
    nc = bass.Bass("TRN2", target_bir_lowering=False)
    es = ExitStack()
    kb = KB(nc, es)
    NLAY = cfg["layers"]
    dbg_names = cfg.get("dbg", [])

    def din(name, shape, dt=F32):
        return nc.dram_tensor(name, list(shape), dt, kind="ExternalInput").ap()

    def dout(name, shape, dt=F32):
        return nc.dram_tensor(name, list(shape), dt, kind="ExternalOutput").ap()

    I = {}
    I["xp"] = din("xp", [128, 8, T]); I["xs"] = din("xs", [128, 8, T])
    I["ckT"] = din("ckT", [NL, 2, 128, 256]); I["cv"] = din("cv", [NL, 128, 2, 128])
    I["pvec"] = din("pvec", [128, NPV]); I["vec"] = din("vec", [128, NV])
    NLD = cfg.get("nld", NL)
    I["w_in_p"] = din("w_in_p", [NLD, 17, 128, 8, 128]); I["w_out_p"] = din("w_out_p", [NLD, 8, 128, 8, 128])
    I["w1_p"] = din("w1_p", [NLD, NF, 128, 8, 128]); I["w3_p"] = din("w3_p", [NLD, NF, 128, 8, 128])
    I["w2_p"] = din("w2_p", [NLD, 8, 128, NF, 128]); I["w_mod_p"] = din("w_mod_p", [NLD, cfg.get("nf", 48), 128, 8, 128])
    I["gw"] = din("gw", [NL, 128, 8, 128])
    I["hf_w1"] = din("hf_w1", [NL, 33, 64]); I["hf_w2"] = din("hf_w2", [NL, 64, 64]); I["hf_w3"] = din("hf_w3", [NL, 65, 1024])
    I["ident"] = din("ident", [128, 128]); I["rrot"] = din("rrot", [128, 128])
    I["ropec"] = din("ropec", [128, T]); I["ropes"] = din("ropes", [128, T])
    for L in (256, 1024):
        I["zT%d" % L] = din("zT%d" % L, [33, L]); I["dec%d" % L] = din("dec%d" % L, [128, L // 128, 2, 256])
        I["dftc%d" % L] = din("dftc%d" % L, [128, L // 128, L]); I["dfts%d" % L] = din("dfts%d" % L, [128, L // 128, L])
    I["altr"] = din("altr", [1, T]); I["altcin"] = din("altcin", [128, 1])
    O = {}
    O["yp"] = dout("yp", [128, 8, T]); O["ys"] = dout("ys", [128, 8, T])
    O["nk"] = dout("nk", [NL, 128, T]); O["nv"] = dout("nv", [NL, 128, T]); O["nst"] = dout("nst", [128, 64])
    DBG = {}

    def dbg(name, ap, res, shape):
        if name not in dbg_names: return
        o = dout("dbg_" + name, shape, ap.dtype)
        kb.dma("sp", o, ap, [res], [], "dbg_" + name)
        kb.outsems.append("dbg_" + name)

    sb = kb.sb
    x = sb("x", [128, 8, T]); x_r = [[Res("x%d_%d" % (c, h)) for h in range(2)] for c in range(8)]
    hT = sb("hT", [128, 8, T], BF16); hT_r = [[Res() for h in range(2)] for c in range(8)]
    mix_r = [[Res() for h in range(2)] for c in range(8)]
    ovl = sb("ovl", [128, 60 * 1024], mybir.dt.uint8)
    mixT = ovl[:, 0:16 * 1024].bitcast(BF16).rearrange("p (c t) -> p c t", c=8)
    vec = sb("vec", [128, NV]); pvec = sb("pvec", [128, NPV]); cres = Res("consts")
    ident = sb("ident", [128, 128]); rrot = sb("rrot", [128, 128])
    identb = sb("identb", [128, 128], BF16)
    ropec = sb("ropec", [128, T]); ropes = sb("ropes", [128, T])
    onesb = sb("onesb", [128, 128], BF16); onesbd = sb("onesbd", [128, 128], BF16)
    altc = sb("altcb", [128, 1], BF16); altr = sb("altrb", [1, T], BF16)
    dft = {256: (sb("dc256", [128, 2, 256], BF16), sb("ds256", [128, 2, 256], BF16)),
           1024: (sb("dc1024", [128, 8, 1024], BF16), sb("ds1024", [128, 8, 1024], BF16))}
    modv = sb("modv", [128, NL, 48, 2])
    modA = sb("modA", [128, NL, 2, 2, 8])
    sT = sb("sT", [128, 8, 2], BF16)
    stt = sb("stt", [128, 64])
    stt_r = Res("stt")
    gwb = sb("gwb", [128, 8, 128], BF16); gwb_r = Res("gwb")
    fw1 = sb("fw1", [33, 64]); fw2 = sb("fw2", [64, 64]); fw_r = Res("fw")
    rstd_t = [sb("rstd%d" % i, [128, 512]) for i in range(2)]; rstd_r = [Res("rstd%d" % i) for i in range(2)]; rstd_p = [0]
    dect = [sb("dect%d" % i, [128, 2, 256]) for i in range(2)]; dect_r = [Res("dect%d" % i) for i in range(2)]
    KN = sb("KN", [1, 512]); KN_r = Res("KN")
    cneg = sb("cneg", [128, 4]); cneg_r = Res("cneg")
    scr = [sb("scr%d" % i, [128, 512]) for i in range(6)]; scr_r = [Res("scr%d" % i) for i in range(6)]
    scb = [sb("scb%d" % i, [128, 512], BF16) for i in range(4)]; scb_r = [Res("scb%d" % i) for i in range(4)]
    scrp = [0]; scbp = [0]

    def S():
        i = scrp[0]; scrp[0] = (i + 1) % len(scr); return scr[i], scr_r[i]

    def SBF():
        i = scbp[0]; scbp[0] = (i + 1) % len(scb); return scb[i], scb_r[i]

    ws = WStream(kb)

    def carve(off, shape, dt):
        esz = 4 if dt == F32 else 2
        n = int(np.prod(shape[1:])) * esz
        v = ovl[:, off:off + n].bitcast(dt)
        if len(shape) == 3:
            v = v.rearrange("p (a b) -> p a b", a=shape[1])
        elif len(shape) == 4:
            v = v.rearrange("p (a b c) -> p a b c", a=shape[1], b=shape[2])
        return v, off + n

    V = lambda key, n=1: vec[:, VCOLS[key]:VCOLS[key] + n]

    def cload(q, dst, src):
        kb.dma(q, dst, src, [], [cres], "c0")
    cload("sp", vec[:], I["vec"]); cload("sp", pvec[:], I["pvec"])
    cload("sp", ident[:], I["ident"]); cload("sp", rrot[:], I["rrot"])
    cload("sp", ropec[:], I["ropec"]); cload("sp", ropes[:], I["ropes"])
    kb.dma("pool", identb[:], I["ident"], [], [cres], "c1")
    kb.dma("pool", altr[:], I["altr"], [], [cres], "c1")
    kb.dma("pool", dft[256][0][:], I["dftc256"], [], [cres], "c1")
    kb.dma("pool", dft[256][1][:], I["dfts256"], [], [cres], "c1")
    kb.dma("pool", altc[:], I["altcin"], [], [cres], "c1")
    cres2 = Res("c0all"); cres2.w = ("c0", kb.cnt["c0"])
    cres3 = Res("c1all"); cres3.w = ("c1", kb.cnt["c1"])
    CR = [cres2, cres3]
    kb.op("dve", lambda e: e.memset(onesb[:], 1.0), [], [cres])
    kb.op("dve", lambda e: e.memset(onesbd[:], 0.0), [], [cres])
    kb.op("dve", lambda e: e.memset(onesbd[0:64, 0:64], 1.0), [], [cres])
    kb.op("dve", lambda e: e.memset(onesbd[64:128, 64:128], 1.0), [], [cres])
    kb.op("dve", lambda e: e.memset(stt[:], 0.0), [], [stt_r])
    CR.append(cres)

    nmod = NLAY if cfg.get("stop") not in ("const",) else 0
    kb.op("act", lambda e: e.activation(out=sT[:].rearrange("p k j -> p (k j)"), in_=pvec[:, PV_COND:PV_COND + 16], func=AF.Silu), CR, [cres])
    mod_r = [Res("mod%d" % l) for l in range(NL)]

    def mod_piece(l, f):
        wv, wr = ws.load(I["w_mod_p"][l, f], 8)
        bk, br = kb.bank("A")
        for k in range(8):
            kb.mm(bk[:, 0:2], wv[:, k, :], sT[:, k, :], k == 0, k == 7, [wr, cres], br)
        kb.op("dve", lambda e: e.tensor_scalar(out=modv[:, l, f, :], in0=bk[:, 0:2], scalar1=V(("bmod", l), 48)[:, f:f + 1], scalar2=None, op0=ALU.add), [br] + CR, [mod_r[l]])

    def mod_finish(l, whs=(0, 1)):
        for j in range(2):
            for wh in whs:
                sc = modv[:, l, (8 if wh == 0 else 32):(16 if wh == 0 else 40), j]
                nw = V(("n1" if wh == 0 else "n2", l), 8)
                kb.op("dve", lambda e: e.scalar_tensor_tensor(out=modA[:, l, j, wh, :], in0=sc, scalar=1.0, in1=nw, op0=ALU.add, op1=ALU.mult), CR + [mod_r[l]], [mod_r[l]])

    pending_mod = []
    if nmod > 0:
        nf0 = cfg.get("nf", 48)
        for f in range(min(16, nf0)):
            mod_piece(0, f)
        mod_finish(0, (0,))
        pending_mod.extend(("p", 0, f) for f in range(16, nf0))
        pending_mod.append(("f", 0, (1,)))

    def run_pending_mod(n):
        for _ in range(n):
            if not pending_mod: return
            it = pending_mod.pop(0)
            if it[0] == "p": mod_piece(it[1], it[2])
            else: mod_finish(it[1], it[2])
    dbg("modv", modv[:].rearrange("p l f j -> p (l f j)"), mod_r[0], [128, NL * 96])

    def rms_bc(src_fn, src_res, nchunk, ones_t, inv_n, half_w=512):
        bk, br = kb.bank("B")
        for c in range(nchunk):
            sq, sqr = SBF()
            a, ar = src_fn(c)
            if nchunk > 1 and c % 2 == 0 and cfg.get("poolsq", True):
                kb.op("pool", lambda e: e.tensor_tensor(sq[:, 0:half_w], a, a, ALU.mult), [ar], [sqr])
            else:
                kb.op("act", lambda e: e.activation(out=sq[:, 0:half_w], in_=a, func=AF.Square), [ar], [sqr])
            kb.mm(bk[:, 0:half_w], ones_t[:], sq[:, 0:half_w], c == 0, c == nchunk - 1, [sqr, cres], br)
        i_ = rstd_p[0]; rstd_p[0] = 1 - i_
        s1, s1r = rstd_t[i_], rstd_r[i_]
        kb.op("act", lambda e: e.activation(out=s1[:, 0:half_w], in_=bk[:, 0:half_w], func=AF.Sqrt, bias=EPS, scale=inv_n), [br], [s1r])
        kb.op("dve", lambda e: e.reciprocal(s1[:, 0:half_w], s1[:, 0:half_w]), [s1r], [s1r])
        return s1, s1r

    def norm_mod(l, j, wh):
        shoff = 0 if wh == 0 else 24
        bks = [kb.bank("B"), kb.bank("B")]
        for c in range(8):
            for h in range(2):
                hs = slice(h * 512, (h + 1) * 512)
                sq, sqr = SBF()
                if (c + h) % 2 == 0:
                    kb.op("pool", lambda e: e.tensor_tensor(sq[:], x[:, c, hs], x[:, c, hs], ALU.mult), [x_r[c][h]], [sqr])
                else:
                    kb.op("act", lambda e: e.activation(out=sq[:], in_=x[:, c, hs], func=AF.Square), [x_r[c][h]], [sqr])
                kb.mm(bks[h][0][:, :], onesb[:], sq[:], c == 0, c == 7, [sqr, cres], bks[h][1])
        for h in range(2):
            kb.op("act", lambda e: e.activation(out=rstd_t[h][:], in_=bks[h][0][:, :], func=AF.Ln, bias=EPS, scale=1.0 / D), [bks[h][1]], [rstd_r[h]])
        for h in range(2):
            kb.op("act", lambda e: e.activation(out=rstd_t[h][:], in_=rstd_t[h][:], func=AF.Exp, scale=-0.5), [rstd_r[h]], [rstd_r[h]])
        for c in range(8):
            for h in range(2):
                hs = slice(h * 512, (h + 1) * 512)
                t1, t1r = S()
                kb.op("dve", lambda e: e.scalar_tensor_tensor(out=t1[:], in0=x[:, c, hs], scalar=modA[:, l, j, wh, c:c + 1], in1=rstd_t[h][:], op0=ALU.mult, op1=ALU.mult), [x_r[c][h], rstd_r[h], cres, mod_r[l]], [t1r])
                kb.op("act", lambda e: e.activation(out=hT[:, c, hs], in_=t1[:], func=AF.Identity, bias=modv[:, l, shoff + c, j:j + 1], scale=1.0), [t1r, cres, mod_r[l]], [hT_r[c][h]])

    def project(piece, nk, rhs_fn, pool="A"):
        wv, wr = ws.load(piece, nk)
        outs = []
        for h in range(2):
            bk, br = kb.bank(pool)
            for k in range(nk):
                a, ar = rhs_fn(k, h)
                kb.mm(bk[:, :], wv[:, k, :], a, k == 0, k == nk - 1, [wr, ar], br)
            outs.append((bk, br))
        run_pending_mod(2)
        return outs

    hT_rhs = lambda k, h: (hT[:, k, h * 512:(h + 1) * 512], hT_r[k][h])
    mix_rhs = lambda k, h: (mixT[:, k, h * 512:(h + 1) * 512], mix_r[k][h])

    def headnorm(l, bk, br, wkey):
        rstd, rr = rms_bc(lambda c: (bk[:, :], br), None, 1, onesbd, 1.0 / HD)
        o, o_r = S()
        kb.op("dve", lambda e: e.scalar_tensor_tensor(out=o[:], in0=bk[:, :], scalar=V((wkey, l)), in1=rstd[:], op0=ALU.mult, op1=ALU.mult), [br, rr, cres], [o_r])
        return o, o_r

    def rope(src, src_r, h, dst, dst_r):
        hs = slice(h * 512, (h + 1) * 512)
        bk, br = kb.bank("B")
        kb.mm(bk[:, :], rrot[:], src[:], True, True, [src_r, cres], br)
        t1, t1r = S()
        kb.op("dve", lambda e: e.tensor_tensor(t1[:], bk[:, :], ropes[:, hs], ALU.mult), [br, cres], [t1r])
        t2, t2r = S()
        kb.op("dve", lambda e: e.tensor_tensor(t2[:], src[:], ropec[:, hs], ALU.mult), [src_r, cres], [t2r])
        kb.op("dve", lambda e: e.tensor_tensor(dst, t1[:], t2[:], ALU.add), [t1r, t2r], [dst_r])

    def qk_prep(l, outs, wkey, dsts, dst_r, do_rope, extra=None):
        sq = []
        for h in range(2):
            t_, r_ = SBF(); sq.append((t_, r_))
            kb.op("act", lambda e: e.activation(out=t_[:], in_=outs[h][0][:, :], func=AF.Square), [outs[h][1]], [r_])
        rb = []
        for h in range(2):
            b_, br_ = kb.bank("B"); rb.append((b_, br_))
            kb.mm(b_[:, :], onesbd[:], sq[h][0][:], True, True, [sq[h][1], cres], br_)
        for h in range(2):
            kb.op("act", lambda e: e.activation(out=rstd_t[h][:], in_=rb[h][0][:, :], func=AF.Ln, bias=EPS, scale=1.0 / HD), [rb[h][1]], [rstd_r[h]])
        for h in range(2):
            kb.op("act", lambda e: e.activation(out=rstd_t[h][:], in_=rstd_t[h][:], func=AF.Exp, scale=-0.5), [rstd_r[h]], [rstd_r[h]])
        if not do_rope and extra is None:
            for h in range(2):
                kb.op("dve", lambda e: e.scalar_tensor_tensor(out=dsts[h], in0=outs[h][0][:, :], scalar=V((wkey, l)), in1=rstd_t[h][:], op0=ALU.mult, op1=ALU.mult), [outs[h][1], rstd_r[h], cres], [dst_r])
            return
        o = []
        for h in range(2):
            t_, r_ = S(); o.append((t_, r_))
            kb.op("dve", lambda e: e.scalar_tensor_tensor(out=t_[:], in0=outs[h][0][:, :], scalar=V((wkey, l)), in1=rstd_t[h][:], op0=ALU.mult, op1=ALU.mult), [outs[h][1], rstd_r[h], cres], [r_])
        if extra is not None:
            for h in range(2): extra(h, o[h][0], o[h][1])
        if not do_rope:
            for h in range(2):
                kb.op("act", lambda e: e.activation(out=dsts[h], in_=o[h][0][:], func=AF.Copy), [o[h][1]], [dst_r])
            return
        pb = []
        for h in range(2):
            b_, br_ = kb.bank("B"); pb.append((b_, br_))
            kb.mm(b_[:, :], rrot[:], o[h][0][:], True, True, [o[h][1], cres], br_)
        t1 = []; t2 = []
        for h in range(2):
            t_, r_ = S(); t1.append((t_, r_))
            kb.op("dve", lambda e: e.tensor_tensor(t_[:], pb[h][0][:, :], ropes[:, h * 512:(h + 1) * 512], ALU.mult), [pb[h][1], cres], [r_])
        for h in range(2):
            t_, r_ = S(); t2.append((t_, r_))
            kb.op("dve", lambda e: e.tensor_tensor(t_[:], o[h][0][:], ropec[:, h * 512:(h + 1) * 512], ALU.mult), [o[h][1], cres], [r_])
        for h in range(2):
            kb.op("dve", lambda e: e.tensor_tensor(dsts[h], t1[h][0][:], t2[h][0][:], ALU.add), [t1[h][1], t2[h][1]], [dst_r])

    def dwconv(out3, in3, res_out, res_in, center, taps, bias):
        L = in3.shape[-1]
        kb.op("act", lambda e: e.activation(out=out3, in_=in3, func=AF.Identity, bias=bias, scale=center), [res_in, cres], [res_out])
        for sh, wcol in taps:
            if sh < 0:
                o_ = out3[:, :, -sh:L]; i_ = in3[:, :, 0:L + sh]
            else:
                o_ = out3[:, :, 0:L - sh]; i_ = in3[:, :, sh:L]
            kb.op("dve", lambda e: e.scalar_tensor_tensor(out=o_, in0=i_, scalar=wcol, in1=o_, op0=ALU.mult, op1=ALU.add), [res_in, res_out, cres], [res_out])

    st = dict(nc=nc, kb=kb, es=es, I=I, O=O, x=x, x_r=x_r, hT=hT, hT_r=hT_r, mixT=mixT, mix_r=mix_r, ws=ws,
              V=V, CR=CR, cres=cres, S=S, SBF=SBF, carve=carve, dbg=dbg, norm_mod=norm_mod, project=project,
              hT_rhs=hT_rhs, mix_rhs=mix_rhs, mod_piece=mod_piece, mod_finish=mod_finish, mod_r=mod_r, rstd_t=rstd_t, rstd_r=rstd_r, run_pending_mod=run_pending_mod, headnorm=headnorm, rope=rope, qk_prep=qk_prep, dwconv=dwconv, modv=modv, pvec=pvec,
              ident=ident, identb=identb, onesb=onesb, altc=altc, altr=altr, dft=dft, stt=stt, stt_r=stt_r,
              gwb=gwb, gwb_r=gwb_r, fw1=fw1, fw2=fw2, fw_r=fw_r, cfg=cfg, NLAY=NLAY, ovl=ovl, dect=dect, dect_r=dect_r,
              KN=KN, KN_r=KN_r, cneg=cneg, cneg_r=cneg_r, ropec=ropec, ropes=ropes, sb=sb)
    return st


class _Stop(Exception):
    pass


def emit_passes(st):
    nc = st["nc"]; kb = st["kb"]; I = st["I"]; O = st["O"]; x = st["x"]; x_r = st["x_r"]
    hT = st["hT"]; hT_r = st["hT_r"]; mixT = st["mixT"]; mix_r = st["mix_r"]; ws = st["ws"]
    V = st["V"]; CR = st["CR"]; cres = st["cres"]; S = st["S"]; SBF = st["SBF"]; carve = st["carve"]; dbg = st["dbg"]
    project = st["project"]; hT_rhs = st["hT_rhs"]; mix_rhs = st["mix_rhs"]; headnorm = st["headnorm"]; rope = st["rope"]
    dwconv = st["dwconv"]; modv = st["modv"]; pvec = st["pvec"]; ident = st["ident"]; identb = st["identb"]
    onesb = st["onesb"]; altc = st["altc"]; altr = st["altr"]; dft = st["dft"]; stt = st["stt"]; stt_r = st["stt_r"]
    gwb = st["gwb"]; gwb_r = st["gwb_r"]; fw1 = st["fw1"]; fw2 = st["fw2"]; fw_r = st["fw_r"]
    dect = st["dect"]; dect_r = st["dect_r"]; KN = st["KN"]; KN_r = st["KN_r"]; cneg = st["cneg"]; cneg_r = st["cneg_r"]
    cfg = st["cfg"]; NLAY = st["NLAY"]
    WOFF = 16 * 1024; FOFF = 44 * 1024

    live = {"M": [], "W": [], "F": [], "G": []}

    def pe_warm(n):
        for i in range(n):
            bk, br = kb.bank("A")
            kb.mm(bk[:, :], onesb[:], hT[:, 0, 0:512], True, True, CR, br)

    STAGES = ["const", "mod", "filt", "norm", "attn", "lru", "hy", "wout"]

    def chk(name):
        sp = cfg.get("stop")
        if sp is not None and STAGES.index(name) >= STAGES.index(sp): raise _Stop()

    def merge_into(new_list, old_lists):
        ev = {}
        for ol in old_lists:
            for r in ol:
                if r.w is not None: ev[r.w[0]] = max(ev.get(r.w[0], 0), r.w[1])
                for sk, v in r.r.items(): ev[sk] = max(ev.get(sk, 0), v)
        for r in new_list:
            r.w = None; r.r = dict(ev)

    def w_phase(new_list):
        merge_into(new_list, [live["W"]]); live["W"] = list(new_list)

    allmix = [r for c in mix_r for r in c]
    live["M"] = allmix

    def run_pass(grp):
        P = grp == "P"; j = 0 if P else 1
        nseq, L = (4, 256) if P else (1, 1024); nP = L // 128
        KT = T if P else T + 256
        Cq, Sq = dft[L]
        xin = I["xp"] if P else I["xs"]
        for c in range(8):
            kb.dma("sp", x[:, c, :], xin[:, c, :], [], [x_r[c][0], x_r[c][1]], "xin%d" % c)
        if cfg["sample"] and "c2" not in kb.cnt:
            c2w = Res("c2w")
            kb.dma("pool", dft[1024][0][:], I["dftc1024"], [], [c2w], "c2")
            kb.dma("pool", dft[1024][1][:], I["dfts1024"], [], [c2w], "c2")
            st["c2all"] = Res("c2all"); st["c2all"].w = ("c2", kb.cnt["c2"])
        if not P:
            CR.append(st["c2all"])

        FIF = cfg.get("filt_in_ffn", True)

        def gen_filters(l, ffn_mode, oldG, L=L, nP=nP, Cq=Cq, Sq=Sq, grp=grp):
            At, _ = carve(FOFF, [128, nP, 512], BF16); Bt, _ = carve(FOFF + nP * 1024, [128, nP, 512], BF16)
            At_r = Res("At"); Bt_r = Res("Bt")
            merge_into([At_r, Bt_r], [live["F"]]); live["F"] = [At_r, Bt_r]
            st["tab_r"] = (At_r, Bt_r)
            if not ffn_mode:
                o_ = WOFF
                h2a = st["ovl"][0:65, o_:o_ + L * 2].bitcast(BF16); o_ += L * 4
                fw3 = st["ovl"][0:65, o_:o_ + 2048].bitcast(BF16); o_ += 4096
                Pq, o_ = carve(o_, [128, nP, 512], BF16); Qq, o_ = carve(o_, [128, nP, 512], BF16)
                h2a_r = Res("h2a"); fw3_r = Res("fw3"); Pq_r = Res("Pq"); Qq_r = Res("Qq")
                merge_into([h2a_r, fw3_r, Pq_r, Qq_r], [live["W"], oldG]); live["W"] = [h2a_r, fw3_r, Pq_r, Qq_r]
            else:
                h2a = st["rstd_t"][0][0:65, :].bitcast(BF16)[:, 0:L]
                fw3 = st["rstd_t"][1][0:65, :].bitcast(BF16)
                hTf = hT[:].rearrange("p c t -> p (c t)")
                Pq = hTf[:, 0:nP * 512].rearrange("p (a b) -> p a b", a=nP)
                Qq = hTf[:, 4096:4096 + nP * 512].rearrange("p (a b) -> p a b", a=nP)
                h2a_r = st["rstd_r"][0]; fw3_r = st["rstd_r"][1]; Pq_r = Res("PqF"); Qq_r = Res("QqF")
                allh = [r for c_ in hT_r for r in c_]
                merge_into([Pq_r, Qq_r], [allh])
            kb.dma("sp", fw1[:], I["hf_w1"][l], [], [fw_r], "fw")
            kb.dma("sp", fw2[:], I["hf_w2"][l], [], [fw_r], "fw")
            kb.dma("pool", fw3, I["hf_w3"][l], [], [fw3_r], "fw3")
            fwa = Res("fwall"); fwa.w = ("fw", kb.cnt["fw"])
            kb.op("dve", lambda e: e.memset(h2a[64:65, :], 1.0), [], [h2a_r])

            def sin_evac(ps, br, bcol, dst, dst_r, w):
                xx, xr = S()
                kb.op("dve", lambda e: e.tensor_scalar(out=xx[0:64, 0:w], in0=ps, scalar1=bcol, scalar2=None, op0=ALU.add), [br, cres], [xr])
                m1, m1r = S()
                kb.op("dve", lambda e: e.tensor_scalar(out=m1[0:64, 0:w], in0=xx[0:64, 0:w], scalar1=PI, scalar2=-2 * PI, op0=ALU.is_gt, op1=ALU.mult), [xr], [m1r])
                m2, m2r = S()
                kb.op("dve", lambda e: e.tensor_scalar(out=m2[0:64, 0:w], in0=xx[0:64, 0:w], scalar1=-PI, scalar2=2 * PI, op0=ALU.is_lt, op1=ALU.mult), [xr], [m2r])
                kb.op("dve", lambda e: e.tensor_tensor(xx[0:64, 0:w], xx[0:64, 0:w], m1[0:64, 0:w], ALU.add), [xr, m1r], [xr])
                kb.op("dve", lambda e: e.tensor_tensor(xx[0:64, 0:w], xx[0:64, 0:w], m2[0:64, 0:w], ALU.add), [xr, m2r], [xr])
                kb.op("dve", lambda e: e.tensor_scalar(out=xx[0:64, 0:w], in0=xx[0:64, 0:w], scalar1=3.1415925, scalar2=-3.1415925, op0=ALU.min, op1=ALU.max), [xr], [xr])
                kb.op("act", lambda e: e.activation(out=dst, in_=xx[0:64, 0:w], func=AF.Sin), [xr], [dst_r])

            for cb in range((L + 511) // 512):
                w = min(512, L - cb * 512)
                zt, ztr = S()
                kb.dma("sp", zt[0:33, 0:w], I["zT%d" % L][:, cb * 512:cb * 512 + w], [], [ztr], "zt")
                bk, br = kb.bank("A")
                kb.mm(bk[0:64, 0:w], fw1[0:33, 0:64], zt[0:33, 0:w], True, True, [fwa, ztr], br)
                h1, h1r = S()
                sin_evac(bk[0:64, 0:w], br, V(("fb1", l))[0:64, :], h1[0:64, 0:w], h1r, w)
                yield
                bk2, br2 = kb.bank("A")
                kb.mm(bk2[0:64, 0:w], fw2[0:64, 0:64], h1[0:64, 0:w], True, True, [fwa, h1r], br2)
                sin_evac(bk2[0:64, 0:w], br2, V(("fb2", l))[0:64, :], h2a[0:64, cb * 512:cb * 512 + w], h2a_r, w)
                yield
            for pc in range(nP):
                di = pc % 2
                kb.dma("sp", dect[di][:], I["dec%d" % L][:, pc], [], [dect_r[di]], "dec%d" % di)
                for ob in range(2):
                    bk, br = kb.bank("A")
                    kb.mm(bk[:, :], h2a[0:65, pc * 128:(pc + 1) * 128], fw3[0:65, ob * 512:(ob + 1) * 512], True, True, [h2a_r, fw3_r], br)
                    f_, fr = S()
                    kb.op("dve", lambda e: e.tensor_tensor(f_[:], bk[:, :], dect[di][:].rearrange("p a b -> p (a b)"), ALU.mult), [br, dect_r[di]], [fr])
                    kb.op("dve", lambda e: e.tensor_tensor(Pq[:, pc, ob * 256:(ob + 1) * 256], f_[:, 0:256], f_[:, 256:512], ALU.add), [fr], [Pq_r])
                    kb.op("dve", lambda e: e.tensor_tensor(Qq[:, pc, ob * 256:(ob + 1) * 256], f_[:, 256:512], f_[:, 0:256], ALU.subtract), [fr], [Qq_r])
                yield
            cwk = "cw%d" % L
            for m in range(nP):
                for (tab, tq, src, src_r, dst, dst_r) in ((Cq, 0, Pq, Pq_r, At, At_r), (Sq, 1, Qq, Qq_r, Bt, Bt_r)):
                    bk, br = kb.bank("A")
                    for pc in range(nP):
                        kb.mm(bk[:, :], tab[:, pc, m * 128:(m + 1) * 128], src[:, pc, :], pc == 0, pc == nP - 1, [src_r] + CR, br)
                    kb.op("act", lambda e: e.activation(out=dst[:, m, :], in_=bk[:, :], func=AF.Identity, scale=V(cwk, nP)[:, m:m + 1]), [br] + CR, [dst_r])
                yield
            bk, br = kb.bank("A")
            for pc in range(nP):
                kb.mm(bk[0:1, :], altc[:, 0:1], Pq[:, pc, :], pc == 0, pc == nP - 1, [Pq_r] + CR, br)
            kb.op("act", lambda e: e.activation(out=KN[0:1, :], in_=bk[0:1, :], func=AF.Identity, scale=0.5 / L), [br], [KN_r])
            if l == 0:
                dbg("At" + grp, At[:].rearrange("p a b -> p (a b)"), At_r, [128, nP * 512])
                dbg("KN" + grp, KN[:], KN_r, [1, 512])

            if ffn_mode:
                merge_into(allh, [[Pq_r, Qq_r]])

        for l in range(NLAY):
            oldG = live["G"]; live["G"] = []
            merge_into(allmix, [oldG])
            At, _ = carve(FOFF, [128, nP, 512], BF16); Bt, _ = carve(FOFF + nP * 1024, [128, nP, 512], BF16)
            if (l == 0 and not (st.get("pregen") and not P)) or not FIF:
                for _ in gen_filters(l, False, oldG): pass
            else:
                live["W"] = list(oldG)
            At_r, Bt_r = st["tab_r"]
            kb.dma("pool", gwb[:], I["gw"][l], [], [gwb_r], "gw")
            chk("filt")
            kb.op("act", lambda e: e.activation(out=cneg[:], in_=V(("lam", l), 4), func=AF.Exp, scale=-1.0), CR, [cneg_r])
            kb.op("act", lambda e: e.activation(out=cneg[:], in_=cneg[:], func=AF.Ln, bias=1.0, scale=1.0), [cneg_r], [cneg_r])
            kb.op("dve", lambda e: e.tensor_scalar(out=cneg[:], in0=cneg[:], scalar1=-8.0, scalar2=None, op0=ALU.mult), [cneg_r], [cneg_r])

            st["norm_mod"](l, j, 0)
            if l == 0: dbg("hT" + grp, hT[:, 0, :], hT_r[0][1], [128, T])

            chk("norm")
            o_ = WOFF
            qT, o_ = carve(o_, [128, 4, T], BF16); kdT, o_ = carve(o_, [128, 2, KT], BF16)
            Vtok, o_ = carve(o_, [128, KT // 128, 256], BF16); vT, o_ = carve(o_, [128, T], F32); kst, o_ = carve(o_, [128, T], F32)
            q_r = [Res("q%d" % c) for c in range(4)]; kd_r = [Res("kd0"), Res("kd1")]; vt_r = Res("Vtok"); vT_r = Res("vT"); kst_r = Res("kst")
            w_phase(q_r + kd_r + [vt_r, vT_r, kst_r])
            koff = 0 if P else 256
            Vt4 = Vtok.rearrange("p k (g e) -> p k g e", g=2)
            kb.op("dve", lambda e: e.memset(Vtok[:], 1.0), [], [vt_r])
            if not P:
                for g in range(2):
                    kb.dma("pool", kdT[:, g, 0:256], I["ckT"][l, g], [], [kd_r[g]], "ck%d" % g)
                kb.dma("pool", Vt4[:, 0:2, :, 0:64], I["cv"][l].rearrange("p c (g d) -> p c g d", g=2), [vt_r], [vt_r], "cv")
            for g in range(2):
                outs = project(I["w_in_p"][l, g], 8, hT_rhs)
                if P:
                    def kextra(h, o_, o_r_, g=g):
                        kb.op("act", lambda e: e.activation(out=kst[g * 64:(g + 1) * 64, h * 512:(h + 1) * 512], in_=o_[0:64, :], func=AF.Copy), [o_r_], [kst_r])
                    st["qk_prep"](l, outs, "kn", [kdT[:, g, 0:512], kdT[:, g, 512:1024]], kd_r[g], False, kextra)
                else:
                    st["qk_prep"](l, outs, "kn", [kdT[:, g, 256:768], kdT[:, g, 768:1280]], kd_r[g], True)
            if P:
                kb.dma("sp", O["nk"][l], kst[:], [kst_r], [], "onk");
                if "onk" not in kb.outsems: kb.outsems.append("onk")
            outs = project(I["w_in_p"][l, 2], 8, hT_rhs)
            for h, (bk, br) in enumerate(outs):
                kb.op("act", lambda e: e.activation(out=vT[:, h * 512:(h + 1) * 512], in_=bk[:, :], func=AF.Copy), [br], [vT_r])
            if P:
                kb.dma("sp", O["nv"][l], vT[:], [vT_r], [], "onv")
                if "onv" not in kb.outsems: kb.outsems.append("onv")
            for tb4 in range(2):
                bk, br = kb.bank("B")
                for i4 in range(4):
                    tb = tb4 * 4 + i4
                    kb.mm(bk[:, i4 * 128:(i4 + 1) * 128], vT[:, tb * 128:(tb + 1) * 128], ident[:], True, True, [vT_r] + CR, br, transpose=True)
                for g_ in range(2):
                    kb.op("dve", lambda e: e.tensor_copy(Vt4[:, koff // 128 + tb4 * 4:koff // 128 + tb4 * 4 + 4, g_, 0:64], bk[:, :].rearrange("p (a g d) -> p a g d", a=4, g=2)[:, :, g_, :]), [br, vt_r], [vt_r])
            if l == 0:
                dbg("kdT" + grp, kdT[:, 0, :], kd_r[0], [128, KT])

            for c in range(4):
                outs = project(I["w_in_p"][l, 3 + c], 8, hT_rhs)
                st["qk_prep"](l, outs, "qn", [qT[:, c, 0:512], qT[:, c, 512:1024]], q_r[c], not P)
            for c in range(4):
                g = c // 2
                NI = 2 if P else KT // 128
                items = [(qh, i) for qh in range(2) for i in range(NI)]
                Sb = {}
                pair = {}

                def emit_S(n):
                    qh, i = items[n]
                    bks = [kb.bank("A"), kb.bank("A")]
                    if P:
                        s_ = qh * 2 + i
                        for kc in range(2):
                            for ph in range(2):
                                ps_ = slice(ph * 64, (ph + 1) * 64)
                                kb.mm(bks[ph][0][:, kc * 256:(kc + 1) * 256], kdT[ps_, g, s_ * 256 + kc * 128:s_ * 256 + (kc + 1) * 128], qT[ps_, c, s_ * 256:(s_ + 1) * 256], True, True, [kd_r[g], q_r[c]], bks[ph][1])
                    else:
                        for ph in range(2):
                            ps_ = slice(ph * 64, (ph + 1) * 64)
                            kb.mm(bks[ph][0][:, :], kdT[ps_, g, i * 128:(i + 1) * 128], qT[ps_, c, qh * 512:(qh + 1) * 512], True, True, [kd_r[g], q_r[c]], bks[ph][1])
                    Sb[n] = bks

                def emit_PV(n):
                    qh, i = items[n]
                    if i == 0:
                        pair[qh] = [kb.bank("B"), kb.bank("B")]
                    bks = Sb.pop(n)
                    pts = []
                    for ph in range(2):
                        pt, ptr = SBF(); pts.append((pt, ptr))
                        kb.op("act", lambda e: e.activation(out=pt[:], in_=bks[ph][0][:, :], func=AF.Exp, scale=0.125), [bks[ph][1]], [ptr])
                    for ph in range(2):
                        nb, nbr = pair[qh][ph]
                        pt, ptr = pts[ph]
                        if P:
                            s_ = qh * 2 + i
                            for kc in range(2):
                                kb.mm(nb[:, i * 256:(i + 1) * 256], Vt4[:, 2 * s_ + kc, g, :], pt[:, kc * 256:(kc + 1) * 256], kc == 0, kc == 1, [vt_r, ptr], nbr)
                        else:
                            kb.mm(nb[:, :], Vt4[:, i, g, :], pt[:], i == 0, i == NI - 1, [vt_r, ptr], nbr)
                    if i == NI - 1:
                        for ph in range(2):
                            nb, nbr = pair[qh][ph]
                            rc, rcr = S()
                            if P or cfg.get("actrc", False):
                                kb.op("act", lambda e: e.activation(out=rc[0:64, :], in_=nb[64:128, :], func=AF.Ln), [nbr], [rcr])
                                kb.op("act", lambda e: e.activation(out=rc[0:64, :], in_=rc[0:64, :], func=AF.Exp, scale=-1.0), [rcr], [rcr])
                            else:
                                kb.op("dve", lambda e: e.reciprocal(rc[0:64, :], nb[64:128, :]), [nbr], [rcr])
                            kb.op("dve", lambda e: e.tensor_tensor(mixT[ph * 64:(ph + 1) * 64, c, qh * 512:(qh + 1) * 512], nb[0:64, :], rc[0:64, :], ALU.mult), [nbr, rcr], [mix_r[c][qh]])

                LA = 1
                for n in range(len(items) + LA):
                    if n < len(items): emit_S(n)
                    if n - LA >= 0: emit_PV(n - LA)

            if l == 0:
                for cdbg in range(4):
                    dbg("att%d" % cdbg + grp, mixT[:, cdbg, :], mix_r[cdbg][1], [128, T])
            chk("attn")
            for c in range(2):
                o_ = WOFF
                lx, o_ = carve(o_, [128, T], F32); xc, o_ = carve(o_, [128, T], F32); xcb, o_ = carve(o_, [128, T], BF16)
                gl, o_ = carve(o_, [128, T], BF16)
                rgs = []; igs = []
                for d in range(2):
                    t_, o_ = carve(o_, [128, T], F32); rgs.append(t_)
                    t_, o_ = carve(o_, [128, T], F32); igs.append(t_)
                lx_r = Res("lx"); xc_r = Res("xc"); xcb_r = Res("xcb"); gl_r = Res("gl")
                rg_rs = [Res("rg0"), Res("rg1")]; ig_rs = [Res("ig0"), Res("ig1")]
                w_phase([lx_r, xc_r, xcb_r, gl_r] + rg_rs + ig_rs)
                hd = [lx, xc]; hd_r = [lx_r, xc_r]
                hd0, hd1 = hd
                outs = project(I["w_in_p"][l, 7 + 2 * c], 8, hT_rhs)
                for h, (bk, br) in enumerate(outs):
                    kb.op("act", lambda e: e.activation(out=lx[:, h * 512:(h + 1) * 512], in_=bk[:, :], func=AF.Copy), [br], [lx_r])
                outs = project(I["w_in_p"][l, 8 + 2 * c], 8, hT_rhs)
                for h, (bk, br) in enumerate(outs):
                    hs = slice(h * 512, (h + 1) * 512)
                    a1, a1r = S()
                    kb.op("act", lambda e: e.activation(out=a1[:], in_=bk[:, :], func=AF.Square), [br], [a1r])
                    kb.op("dve", lambda e: e.tensor_scalar(out=a1[:], in0=a1[:], scalar1=0.044715, scalar2=1.0, op0=ALU.mult, op1=ALU.add), [a1r], [a1r])
                    kb.op("dve", lambda e: e.tensor_tensor(a1[:], a1[:], bk[:, :], ALU.mult), [a1r, br], [a1r])
                    kb.op("act", lambda e: e.activation(out=a1[:], in_=a1[:], func=AF.Sigmoid, scale=1.5957691216), [a1r], [a1r])
                    kb.op("dve", lambda e: e.tensor_tensor(gl[:, hs], a1[:], bk[:, :], ALU.mult), [a1r, br], [gl_r])
                v3 = lambda a: a.rearrange("p (s t) -> p s t", s=nseq)
                lcw = V(("lcw", l), 8)
                dwconv(v3(xc), v3(lx), xc_r, lx_r, lcw[:, 4 + c:5 + c],
                       [(-2, lcw[:, 0 + c:1 + c]), (-1, lcw[:, 2 + c:3 + c]), (1, lcw[:, 6 + c:7 + c])], V(("lcb", l), 2)[:, c:c + 1])
                kb.op("act", lambda e: e.activation(out=xcb, in_=xc, func=AF.Copy), [xc_r], [xcb_r])
                if l == 0 and c == 0: dbg("xc" + grp, xc, xc_r, [128, T])
                for d in range(2):
                    for gt, dst, dst_r in ((0, rgs[d], rg_rs[d]), (1, igs[d], ig_rs[d])):
                        idx = (d * 2 + gt) * 2 + c
                        for h in range(2):
                            hs = slice(h * 512, (h + 1) * 512)
                            bk, br = kb.bank("A")
                            kb.mm(bk[:, :], gwb[:, idx, :], xcb[:, hs], True, True, [gwb_r, xcb_r], br)
                            kb.op("act", lambda e: e.activation(out=dst[:, hs], in_=bk[:, :], func=AF.Sigmoid, bias=V(("lgb", l), 8)[:, idx:idx + 1], scale=1.0), [br] + CR, [dst_r])
                for d in range(2):
                    kb.op("act", lambda e: e.activation(out=rgs[d], in_=rgs[d], func=AF.Exp, scale=cneg[:, d * 2 + c:d * 2 + c + 1]), [rg_rs[d], cneg_r], [rg_rs[d]])
                for d in range(2):
                    kb.op("dve", lambda e: e.tensor_tensor(igs[d], igs[d], xc, ALU.mult), [ig_rs[d], xc_r], [ig_rs[d]])
                a2s = []
                for d in range(2):
                    for h in range(2):
                        hs = slice(h * 512, (h + 1) * 512)
                        a2, a2r = S(); a2s.append((a2, a2r))
                        kb.op("dve", lambda e: e.tensor_tensor(a2[:], rgs[d][:, hs], rgs[d][:, hs], ALU.mult), [rg_rs[d]], [a2r])
                for a2, a2r in a2s:
                    kb.op("act", lambda e: e.activation(out=a2[:], in_=a2[:], func=AF.Sqrt, bias=1.0, scale=-1.0), [a2r], [a2r])
                for d in range(2):
                    for h in range(2):
                        hs = slice(h * 512, (h + 1) * 512)
                        a2, a2r = a2s[d * 2 + h]
                        kb.op("dve", lambda e: e.tensor_tensor(igs[d][:, hs], igs[d][:, hs], a2[:], ALU.mult), [ig_rs[d], a2r], [ig_rs[d]])
                for d in range(2):
                    rg = rgs[d]; ig = igs[d]
                    for s in range(nseq):
                        lo, hi = s * L, (s + 1) * L
                        init = 0.0 if P else pvec[:, PV_H0 + (l * 2 + d) * 2 + c:PV_H0 + (l * 2 + d) * 2 + c + 1]
                        if d == 0:
                            kb.op("dve", lambda e: e.tensor_tensor_scan(hd[d][:, lo:hi], rg[:, lo:hi], ig[:, lo:hi], init, ALU.mult, ALU.add), [rg_rs[d], ig_rs[d]] + CR, [hd_r[d]])
                        else:
                            rv = (lambda a: a[:, hi - 1::-1]) if lo == 0 else (lambda a: a[:, hi - 1:lo - 1:-1])
                            kb.op("dve", lambda e: e.tensor_tensor_scan(rv(hd[d]), rv(rg), rv(ig), init, ALU.mult, ALU.add), [rg_rs[d], ig_rs[d]] + CR, [hd_r[d]])
                    if P:
                        col = ((l * 2 + d) * 2 + c) * 4
                        src = v3(hd[d])[:, :, L - 1] if d == 0 else v3(hd[d])[:, :, 0]
                        kb.op("act", lambda e: e.activation(out=stt[:, col:col + 4], in_=src, func=AF.Copy), [hd_r[d]], [stt_r])
                if l == 0 and c == 0: dbg("hf" + grp, hd0, hd_r[0], [128, T]); dbg("hb" + grp, hd1, hd_r[1], [128, T])
                kb.op("dve", lambda e: e.tensor_tensor(hd0, hd0, hd1, ALU.add), hd_r, [hd_r[0]])
                for h in range(2):
                    hs = slice(h * 512, (h + 1) * 512)
                    kb.op("dve", lambda e: e.tensor_tensor(mixT[:, 4 + c, hs], hd0[:, hs], gl[:, hs], ALU.mult), [hd_r[0], gl_r], [mix_r[4 + c][h]])

            chk("lru")
            o_ = WOFF
            hv, o_ = carve(o_, [128, 2, T], BF16); hx1, o_ = carve(o_, [128, 2, T], BF16); hx2, o_ = carve(o_, [128, 2, T], BF16)
            o_s2 = o_
            raws = []; cvts = []
            for i2 in range(2):
                r_, o_ = carve(o_, [128, T], F32); c_, o_ = carve(o_, [128, T], F32)
                raws.append((r_, Res("raw%d" % i2))); cvts.append((c_, Res("cvt%d" % i2)))
            hv_r = Res("hv"); hx1_r = Res("hx1"); hx2_r = Res("hx2")
            w_phase([hv_r, hx1_r, hx2_r] + [r for _, r in raws] + [r for _, r in cvts])
            v3 = lambda a: a.rearrange("p (s t) -> p s t", s=nseq)
            hcw = V(("hcw", l), 18)
            for i in range(6):
                raw, raw_r = raws[i % 2]; cvt, cvt_r = cvts[i % 2]
                dst, dst_r = ((hv, hv_r), (hx1, hx1_r), (hx2, hx2_r))[i // 2]
                outs = project(I["w_in_p"][l, 11 + i], 8, hT_rhs)
                for h, (bk, br) in enumerate(outs):
                    hs = slice(h * 512, (h + 1) * 512)
                    kb.op("act", lambda e: e.activation(out=raw[:, hs], in_=bk[:, :], func=AF.Copy), [br], [raw_r])
                    kb.op("dve", lambda e: e.tensor_scalar(out=cvt[:, hs], in0=raw[:, hs], scalar1=hcw[:, 6 + i:7 + i], scalar2=None, op0=ALU.mult), [raw_r, cres], [cvt_r])
                r3 = v3(raw); c3 = v3(cvt); d3 = v3(dst[:, i % 2, :])
                kb.op("dve", lambda e: e.scalar_tensor_tensor(out=c3[:, :, 1:L], in0=r3[:, :, 0:L - 1], scalar=hcw[:, i:i + 1], in1=c3[:, :, 1:L], op0=ALU.mult, op1=ALU.add), [raw_r, cvt_r, cres], [cvt_r])
                kb.op("dve", lambda e: e.scalar_tensor_tensor(out=c3[:, :, 0:L - 1], in0=r3[:, :, 1:L], scalar=hcw[:, 12 + i:13 + i], in1=c3[:, :, 0:L - 1], op0=ALU.mult, op1=ALU.add), [raw_r, cvt_r, cres], [cvt_r])
                kb.op("act", lambda e: e.activation(out=dst[:, i % 2, :], in_=cvt, func=AF.Copy), [cvt_r], [dst_r])
            raw_r = raws[0][1]; cvt_r = raws[1][1]; raw_r2 = cvts[0][1]; cvt_r2 = cvts[1][1]
            if l == 0: dbg("hv" + grp, hv[:, 0, :], hv_r, [128, T])
            tok, o_ = carve(o_s2, [128, nP, nseq * 256], BF16); Zr, o_ = carve(o_, [128, nP, nseq * 256], BF16); Zs, o_ = carve(o_, [128, nP, nseq * 256], BF16)
            tok_r = Res("tok"); Zr_r = Res("Zr"); Zs_r = Res("Zs")
            merge_into([tok_r, Zr_r, Zs_r], [[raw_r, cvt_r, raw_r2, cvt_r2]]); live["W"] = [hv_r, hx1_r, hx2_r, tok_r, Zr_r, Zs_r]
            ZN = st.setdefault("ZN", None)
            if ZN is None:
                ZN = st["sb"]("ZN", [1, 1024], BF16); st["ZN"] = ZN; st["ZN_r"] = Res("ZN")
            ZN_r = st["ZN_r"]
            NCOL = nseq * 256; GW = min(512, NCOL); NG = NCOL // GW
            TW = min(L, 512); NTH = L // TW

            PWE = cfg.get("pwe", "pool")

            def longconv(o, u, u_r, epilogue):
                for s in range(nseq):
                    for tc in range(nP):
                        bk, br = kb.bank("B")
                        bkb = bk[:, :].bitcast(BF16)
                        for cc in range(2):
                            kb.mm(bkb[:, cc * 128:(cc + 1) * 128], u[:, cc, s * L + tc * 128:s * L + (tc + 1) * 128], identb[:], True, True, [u_r] + CR, br, transpose=True)
                        kb.op("act", lambda e: e.activation(out=tok[:, tc, s * 256:(s + 1) * 256], in_=bkb[:, 0:256], func=AF.Copy), [br], [tok_r])
                for gi in range(NG):
                    gs = slice(gi * GW, (gi + 1) * GW)
                    un, unr = kb.bank("B")
                    for tc in range(nP):
                        kb.mm(un[0:1, 0:GW], altc[:, 0:1], tok[:, tc, gs], tc == 0, tc == nP - 1, [tok_r] + CR, unr)
                    for si in range(GW // 256):
                        zc = gi * GW + si * 256
                        kb.op("dve", lambda e: e.tensor_tensor(ZN[0:1, zc:zc + 256], un[0:1, si * 256:(si + 1) * 256], KN[0:1, o * 256:(o + 1) * 256], ALU.mult), [unr, KN_r], [ZN_r])
                for m in range(nP):
                    for gi in range(NG):
                        gs = slice(gi * GW, (gi + 1) * GW)
                        ur, urr = kb.bank("A"); us, usr = kb.bank("B")
                        for tc in range(nP):
                            kb.mm(ur[:, 0:GW], Cq[:, tc, m * 128:(m + 1) * 128], tok[:, tc, gs], tc == 0, tc == nP - 1, [tok_r] + CR, urr)
                        for tc in range(nP):
                            kb.mm(us[:, 0:GW], Sq[:, tc, m * 128:(m + 1) * 128], tok[:, tc, gs], tc == 0, tc == nP - 1, [tok_r] + CR, usr)
                        for si in range(GW // 256):
                            cs = slice(si * 256, (si + 1) * 256); zc = gi * GW + si * 256
                            A_ = At[:, m, o * 256:(o + 1) * 256]; B_ = Bt[:, m, o * 256:(o + 1) * 256]
                            t1, t1r = S(); t2, t2r = S(); t3, t3r = S(); t4, t4r = S()
                            kb.op("dve", lambda e: e.tensor_tensor(t1[:, 0:256], ur[:, cs], A_, ALU.mult), [urr, At_r], [t1r])
                            kb.op("dve", lambda e: e.tensor_tensor(t2[:, 0:256], us[:, cs], B_, ALU.mult), [usr, Bt_r], [t2r])
                            kb.op(PWE, lambda e: e.tensor_tensor(Zr[:, m, zc:zc + 256], t1[:, 0:256], t2[:, 0:256], ALU.add), [t1r, t2r], [Zr_r])
                            kb.op("dve", lambda e: e.tensor_tensor(t3[:, 0:256], us[:, cs], A_, ALU.mult), [usr, At_r], [t3r])
                            kb.op("dve", lambda e: e.tensor_tensor(t4[:, 0:256], ur[:, cs], B_, ALU.mult), [urr, Bt_r], [t4r])
                            kb.op(PWE, lambda e: e.tensor_tensor(Zs[:, m, zc:zc + 256], t3[:, 0:256], t4[:, 0:256], ALU.subtract), [t3r, t4r], [Zs_r])
                for s in range(nseq):
                    for cc in range(2):
                        zs_ = slice(s * 256 + cc * 128, s * 256 + (cc + 1) * 128)
                        for th in range(NTH):
                            bk, br = kb.bank("A")
                            for m in range(nP):
                                kb.mm(bk[:, 0:TW], Zr[:, m, zs_], Cq[:, m, th * TW:(th + 1) * TW], m == 0, False, [Zr_r] + CR, br)
                            for m in range(nP):
                                kb.mm(bk[:, 0:TW], Zs[:, m, zs_], Sq[:, m, th * TW:(th + 1) * TW], False, False, [Zs_r] + CR, br)
                            kb.mm(bk[:, 0:TW], ZN[0:1, zs_], altr[0:1, th * TW:(th + 1) * TW], False, True, [ZN_r] + CR, br)
                            epilogue(cc, slice(s * L + th * TW, s * L + (th + 1) * TW), bk[:, 0:TW], br)

            hsk = V(("hsk", l), 4)

            def ep1(cc, ts, ps, br):
                t1, t1r = S()
                kb.op("dve", lambda e: e.scalar_tensor_tensor(out=t1[:, 0:TW], in0=hv[:, cc, ts], scalar=hsk[:, cc:cc + 1], in1=ps, op0=ALU.mult, op1=ALU.add), [hv_r, br] + CR, [t1r])
                kb.op("dve", lambda e: e.tensor_tensor(hx1[:, cc, ts], t1[:, 0:TW], hx1[:, cc, ts], ALU.mult), [t1r, hx1_r], [hx1_r])

            def ep2(cc, ts, ps, br):
                t1, t1r = S()
                kb.op("dve", lambda e: e.scalar_tensor_tensor(out=t1[:, 0:TW], in0=hx1[:, cc, ts], scalar=hsk[:, 2 + cc:3 + cc], in1=ps, op0=ALU.mult, op1=ALU.add), [hx1_r, br] + CR, [t1r])
                h_ = ts.start // 512
                kb.op("dve", lambda e: e.tensor_tensor(mixT[:, 6 + cc, ts], t1[:, 0:TW], hx2[:, cc, ts], ALU.mult), [t1r, hx2_r], [mix_r[6 + cc][h_]])

            longconv(0, hv, hv_r, ep1)
            if l == 0: dbg("z" + grp, hx1[:, 0, :], hx1_r, [128, T])
            longconv(1, hx1, hx1_r, ep2)
            if l == 0:
                for cdbg in range(8):
                    dbg("mix%d" % cdbg + grp, mixT[:, cdbg, :], mix_r[cdbg][1], [128, T])

            chk("hy")
            st["run_pending_mod"](100)
            for jo in range(8):
                outs = project(I["w_out_p"][l, jo], 8, mix_rhs)
                for h, (bk, br) in enumerate(outs):
                    hs = slice(h * 512, (h + 1) * 512)
                    kb.op("dve", lambda e: e.scalar_tensor_tensor(out=x[:, jo, hs], in0=bk[:, :], scalar=modv[:, l, 16 + jo, j:j + 1], in1=x[:, jo, hs], op0=ALU.mult, op1=ALU.add), [br, x_r[jo][h], cres, st["mod_r"][l]], [x_r[jo][h]])
            if l == 0: dbg("x1" + grp, x[:, 0, :], x_r[0][1], [128, T])

            chk("wout")
            st["norm_mod"](l, j, 1)
            G = st["ovl"][:, 0:44 * 1024].bitcast(BF16).rearrange("p (f t) -> p f t", f=NF)
            G_r = [[Res("G") for h in range(2)] for f in range(NF)]
            allG = [r for f in G_r for r in f]
            merge_into(allG, [live["W"], live["M"], live["F"]]); live["G"] = allG; live["W"] = []
            modq = list(range(48)) if (P and l + 1 < NLAY) or (not cfg["prompt"] and not P and l + 1 < NLAY) else []
            for f in range(NF):
                for _ in range(2):
                    if modq: st["mod_piece"](l + 1, modq.pop(0))
                o1 = project(I["w1_p"][l, f], 8, hT_rhs, pool="A")
                o3 = project(I["w3_p"][l, f], 8, hT_rhs, pool="B")
                for h in range(2):
                    sg, sgr = S()
                    kb.op("act", lambda e: e.activation(out=sg[:], in_=o1[h][0][:, :], func=AF.Silu), [o1[h][1]], [sgr])
                    kb.op("dve", lambda e: e.tensor_tensor(G[:, f, h * 512:(h + 1) * 512], sg[:], o3[h][0][:, :], ALU.mult), [sgr, o3[h][1]], [G_r[f][h]])
            if FIF and l + 1 < NLAY:
                fgen = gen_filters(l + 1, True, None)
            elif FIF and P and cfg["sample"] and cfg.get("pregen", True):
                if st["c2all"] not in CR: CR.append(st["c2all"])
                fgen = gen_filters(0, True, None, L=1024, nP=8, Cq=dft[1024][0], Sq=dft[1024][1], grp="S")
                st["pregen"] = True
            else:
                fgen = iter(())
            for jo in range(8):
                if modq:
                    st["mod_piece"](l + 1, modq.pop(0))
                    if not modq: st["mod_finish"](l + 1)
                wa, war = ws.load(I["w2_p"][l, jo][:, 0:11, :], 11)
                wb, wbr = ws.load(I["w2_p"][l, jo][:, 11:22, :], 11)
                for h in range(2):
                    hs = slice(h * 512, (h + 1) * 512)
                    bk, br = kb.bank("A")
                    for k in range(NF):
                        wv, wr = (wa, war) if k < 11 else (wb, wbr)
                        kb.mm(bk[:, :], wv[:, k % 11, :], G[:, k, hs], k == 0, k == NF - 1, [wr, G_r[k][h]], br)
                    kb.op("dve", lambda e: e.scalar_tensor_tensor(out=x[:, jo, hs], in0=bk[:, :], scalar=modv[:, l, 40 + jo, j:j + 1], in1=x[:, jo, hs], op0=ALU.mult, op1=ALU.add), [br, x_r[jo][h], cres, st["mod_r"][l]], [x_r[jo][h]])
                    next(fgen, None)
            for _ in fgen: pass
        yo = O["yp"] if P else O["ys"]
        for c in range(8):
            kb.dma("sp", yo[:, c, :], x[:, c, :], [x_r[c][0], x_r[c][1]], [], "oy%d" % c)
            if "oy%d" % c not in kb.outsems: kb.outsems.append("oy%d" % c)

    try:
        chk("mod")
        if cfg["prompt"]:
            run_pass("P")
            kb.dma("sp", O["nst"], stt[:], [stt_r], [], "onst"); kb.outsems.append("onst")
        if cfg["sample"]:
            run_pass("S")
    except _Stop:
        pass
    for sk, v in kb.cnt.items():
        if v > 0:
            nc.sync.wait_ge(kb.semobj[sk], v)


_CACHE = {}


def _get_program(cfg):
    key = (cfg["layers"], cfg["prompt"], cfg["sample"], tuple(cfg.get("dbg", [])), cfg.get("stop"))
    if key not in _CACHE:
        st = build_program(cfg)
        emit_passes(st)
        _CACHE[key] = st
    return _CACHE[key]


def kernel(**inputs):
    cfg = CFG
    st = _get_program(cfg)
    nc = st["nc"]
    shared = _host_shared(inputs)
    in_maps = []
    for c in range(8):
        m = dict(shared); m.update(_host_core(inputs, c)); in_maps.append(m)
    res = run_bass_kernel_spmd(nc, in_maps, core_ids=list(range(8)))
    R = res.results
    kernel.last = R
    f32 = np.float32
    y_p = np.zeros((32, 256, D), f32); y_s = np.zeros((2, T, D), f32)
    nk = np.zeros((32, NL, 256, 2, 64), f32); nv = np.zeros((32, NL, 256, 2, 64), f32); nst = np.zeros((32, NL, 2, 256), f32)
    for c in range(8):
        r = R[c]
        y_p[4 * c:4 * c + 4] = r["yp"].transpose(2, 1, 0).reshape(4, 256, D)
        if c < 2:
            y_s[c] = r["ys"].transpose(2, 1, 0).reshape(T, D)
        k = r["nk"].reshape(NL, 2, 64, 4, 256)
        nk[4 * c:4 * c + 4] = k.transpose(3, 0, 4, 1, 2)
        v = r["nv"].reshape(NL, 2, 64, 4, 256)
        nv[4 * c:4 * c + 4] = v.transpose(3, 0, 4, 1, 2)
        s_ = r["nst"].reshape(128, NL, 2, 2, 4)
        nst[4 * c:4 * c + 4] = s_.transpose(4, 1, 2, 3, 0).reshape(4, NL, 2, 256)
    return (y_p, y_s, nk, nv, nst)
```
